# Optimizing a Trainium2 kernel written in Bass

```python
import math
import jax, jax.numpy as jnp
from jax import lax
import numpy as np

D_MODEL = 1024
BATCH = 4
SEQ = 4096
DEPTH = 1

CHUNK = 64
Q_BLOCK = 128
EPS = 1e-6
D_FF = 2816
N_MOD = 9
MLA_HEADS = 8
MLA_Q_RANK = 768
MLA_KV_RANK = 256
MLA_NOPE = 128
MLA_ROPE = 64
MLA_V = 128
ROPE_THETA = 10000.0
GDN_HEADS = 8
GDN_DK = 128
GDN_DV = 128
CONV_W = 4
N_BRANCH = 2
IN_SPLITS = (MLA_Q_RANK, MLA_KV_RANK, MLA_ROPE, GDN_HEADS * GDN_DK, GDN_HEADS * GDN_DK, GDN_HEADS * GDN_DV, GDN_HEADS, GDN_HEADS, GDN_HEADS * GDN_DV, N_BRANCH * D_MODEL)
D_IN = sum(IN_SPLITS)

kernel_name = 'hybrid_mla_gdn_macaron_adaln_block'


def rmsnorm(x, g):
    xf = x.astype(jnp.float32)
    xf = xf * lax.rsqrt(jnp.mean(xf * xf, axis=-1, keepdims=True) + EPS)
    return (xf * g.astype(jnp.float32)).astype(x.dtype)


def modulate(x, g, shift, scale):
    return rmsnorm(x, g) * (1.0 + scale[:, None, :]) + shift[:, None, :]


def swiglu(h, w_gate, w_up, w_down):
    return (jax.nn.silu(h @ w_gate) * (h @ w_up)) @ w_down


def l2norm(t):
    return t * lax.rsqrt(jnp.sum(t * t, axis=-1, keepdims=True) + EPS)


def rope(x, cos, sin):
    xf = x.astype(jnp.float32)
    x1, x2 = jnp.split(xf, 2, axis=-1)
    return jnp.concatenate([x1 * cos - x2 * sin, x2 * cos + x1 * sin], axis=-1).astype(x.dtype)


def mla_branch(q_lat, kv_lat, k_pe, g_q, w_uq, g_kv, w_ukv, cos, sin):
    B, S, _ = q_lat.shape
    H = MLA_HEADS
    q = (rmsnorm(q_lat, g_q) @ w_uq).reshape(B, S, H, MLA_NOPE + MLA_ROPE)
    q_nope = q[..., :MLA_NOPE]
    q_pe = rope(q[..., MLA_NOPE:], cos[:, None, :], sin[:, None, :])
    kv = (rmsnorm(kv_lat, g_kv) @ w_ukv).reshape(B, S, H, MLA_NOPE + MLA_V)
    k_nope = kv[..., :MLA_NOPE]
    v = kv[..., MLA_NOPE:]
    k_pe = rope(k_pe, cos, sin)
    scale = (MLA_NOPE + MLA_ROPE) ** -0.5
    nb = S // Q_BLOCK
    qn_blk = q_nope.reshape(B, nb, Q_BLOCK, H, MLA_NOPE).transpose(1, 0, 2, 3, 4)
    qp_blk = q_pe.reshape(B, nb, Q_BLOCK, H, MLA_ROPE).transpose(1, 0, 2, 3, 4)
    k_chunk = jnp.arange(S) // CHUNK

    def attend(args):
        qn, qp, blk = args
        s = jnp.einsum('bqhd,bkhd->bhqk', qn, k_nope) + jnp.einsum('bqhr,bkr->bhqk', qp, k_pe)
        s = s.astype(jnp.float32) * scale
        q_chunk = (blk * Q_BLOCK + jnp.arange(Q_BLOCK)) // CHUNK
        mask = k_chunk[None, :] <= q_chunk[:, None]
        s = jnp.where(mask, s, -jnp.inf)
        p = jax.nn.softmax(s, axis=-1).astype(v.dtype)
        return jnp.einsum('bhqk,bkhd->bqhd', p, v)

    o = lax.map(attend, (qn_blk, qp_blk, jnp.arange(nb)))
    return o.transpose(1, 0, 2, 3, 4).reshape(B, S, H * MLA_V)


def causal_depthwise_conv(x, w):
    C = x.shape[-1]
    xp = jnp.pad(x, ((0, 0), (CONV_W - 1, 0), (0, 0)))
    return lax.conv_general_dilated(xp, w[:, None, :].astype(x.dtype), window_strides=(1,), padding='VALID', dimension_numbers=('NWC', 'WIO', 'NWC'), feature_group_count=C)


def gated_deltanet_branch(qkv, a, b, z, w_conv, a_log, dt_bias, g_out):
    B, S, _ = qkv.shape
    N = S // CHUNK
    H, DK, DV = GDN_HEADS, GDN_DK, GDN_DV
    f32 = jnp.float32
    qkv = jax.nn.silu(causal_depthwise_conv(qkv, w_conv))
    q, k, v = jnp.split(qkv, [H * DK, 2 * H * DK], axis=-1)

    def to_chunks(t, d):
        return t.reshape(B, N, CHUNK, H, d).transpose(0, 3, 1, 2, 4).astype(f32)

    q = l2norm(to_chunks(q, DK)) * (DK ** -0.5)
    k = l2norm(to_chunks(k, DK))
    v = to_chunks(v, DV)
    beta = jax.nn.sigmoid(b.astype(f32)).reshape(B, N, CHUNK, H).transpose(0, 3, 1, 2)
    g = -jnp.exp(a_log.astype(f32)) * jax.nn.softplus(a.astype(f32) + dt_bias.astype(f32))
    g = g.reshape(B, N, CHUNK, H).transpose(0, 3, 1, 2)
    decay = jnp.cumsum(g, axis=-1)
    idx = jnp.arange(CHUNK)
    causal = idx[:, None] >= idx[None, :]
    strict = idx[:, None] > idx[None, :]
    L = jnp.exp(jnp.where(causal, decay[..., :, None] - decay[..., None, :], -jnp.inf))
    k_beta = k * beta[..., None]
    A = jnp.where(strict, jnp.einsum('bhnid,bhnjd->bhnij', k_beta, k) * L, 0.0)
    eye = jnp.eye(CHUNK, dtype=f32)
    T = lax.linalg.triangular_solve(A + eye, jnp.broadcast_to(eye, A.shape), left_side=True, lower=True, unit_diagonal=True)
    w = jnp.einsum('bhnij,bhnjd->bhnid', T, k_beta * jnp.exp(decay)[..., None])
    u = jnp.einsum('bhnij,bhnje->bhnie', T, v * beta[..., None])
    attn_intra = jnp.einsum('bhnid,bhnjd->bhnij', q, k) * L
    q_dec = q * jnp.exp(decay)[..., None]
    k_dec = k * jnp.exp(decay[..., -1:] - decay)[..., None]
    chunk_decay = jnp.exp(decay[..., -1])
    xs = tuple(jnp.moveaxis(t, 2, 0) for t in (q_dec, k_dec, w, u, attn_intra, chunk_decay))

    def step(state, inp):
        qd, kd, w_n, u_n, a_n, cd = inp
        v_new = u_n - jnp.einsum('bhcd,bhde->bhce', w_n, state)
        o = jnp.einsum('bhcd,bhde->bhce', qd, state) + jnp.einsum('bhij,bhje->bhie', a_n, v_new)
        state = state * cd[..., None, None] + jnp.einsum('bhcd,bhce->bhde', kd, v_new)
        return state, o

    state0 = jnp.zeros((B, H, DK, DV), f32)
    _, o = lax.scan(step, state0, xs)
    o = o.transpose(1, 0, 3, 2, 4).reshape(B, S, H, DV)
    o = rmsnorm(o, g_out) * jax.nn.silu(z.reshape(B, S, H, DV).astype(f32))
    return o.reshape(B, S, H * DV).astype(z.dtype)


def setup_inputs(seed: int = 0) -> dict:
    key = jax.random.key(seed)
    ks = jax.random.split(key, 32)
    f32 = jnp.float32
    L = DEPTH

    def dense(k, shape, fan_in, mult=1.0):
        return jax.random.normal(k, shape, f32) * (mult * fan_in ** -0.5)

    def gain(k, shape):
        return 1.0 + 0.05 * jax.random.normal(k, shape, f32)

    dt = jnp.exp(jax.random.uniform(ks[17], (L, GDN_HEADS), f32, math.log(1e-3), math.log(1e-1)))
    return {
        'x': jax.random.normal(ks[0], (BATCH, SEQ, D_MODEL), f32),
        'c': jax.random.normal(ks[1], (BATCH, D_MODEL), f32),
        'w_ada': dense(ks[2], (L, D_MODEL, N_MOD * D_MODEL), D_MODEL, 0.5),
        'b_ada': 0.01 * jax.random.normal(ks[3], (L, N_MOD * D_MODEL), f32),
        'g_ffn1': gain(ks[4], (L, D_MODEL)),
        'w1_gate': dense(ks[5], (L, D_MODEL, D_FF), D_MODEL),
        'w1_up': dense(ks[6], (L, D_MODEL, D_FF), D_MODEL),
        'w1_down': dense(ks[7], (L, D_FF, D_MODEL), D_FF),
        'g_mix': gain(ks[8], (L, D_MODEL)),
        'w_in': dense(ks[9], (L, D_MODEL, D_IN), D_MODEL),
        'g_q_lat': gain(ks[10], (L, MLA_Q_RANK)),
        'w_uq': dense(ks[11], (L, MLA_Q_RANK, MLA_HEADS * (MLA_NOPE + MLA_ROPE)), MLA_Q_RANK),
        'g_kv_lat': gain(ks[12], (L, MLA_KV_RANK)),
        'w_ukv': dense(ks[13], (L, MLA_KV_RANK, MLA_HEADS * (MLA_NOPE + MLA_V)), MLA_KV_RANK),
        'w_conv': dense(ks[14], (L, CONV_W, 3 * GDN_HEADS * GDN_DK), CONV_W),
        'a_log': jnp.log(jax.random.uniform(ks[15], (L, GDN_HEADS), f32, 1.0, 16.0)),
        'dt_bias': dt + jnp.log(-jnp.expm1(-dt)),
        'g_gdn_out': gain(ks[16], (L, GDN_DV)),
        'w_o_mla': dense(ks[18], (L, MLA_HEADS * MLA_V, D_MODEL), MLA_HEADS * MLA_V),
        'w_o_gdn': dense(ks[19], (L, GDN_HEADS * GDN_DV, D_MODEL), GDN_HEADS * GDN_DV),
        'w_out': dense(ks[20], (L, D_MODEL, D_MODEL), D_MODEL),
        'g_ffn2': gain(ks[21], (L, D_MODEL)),
        'w2_gate': dense(ks[22], (L, D_MODEL, D_FF), D_MODEL),
        'w2_up': dense(ks[23], (L, D_MODEL, D_FF), D_MODEL),
        'w2_down': dense(ks[24], (L, D_FF, D_MODEL), D_FF),
        'g_final': gain(ks[25], (D_MODEL,)),
    }


def reference(x, c, w_ada, b_ada, g_ffn1, w1_gate, w1_up, w1_down, g_mix, w_in, g_q_lat, w_uq, g_kv_lat, w_ukv, w_conv, a_log, dt_bias, g_gdn_out, w_o_mla, w_o_gdn, w_out, g_ffn2, w2_gate, w2_up, w2_down, g_final):
    S = x.shape[1]
    pos = jnp.arange(S, dtype=jnp.float32)
    inv_freq = ROPE_THETA ** (-jnp.arange(0, MLA_ROPE, 2, dtype=jnp.float32) / MLA_ROPE)
    ang = pos[:, None] * inv_freq[None, :]
    cos, sin = jnp.cos(ang), jnp.sin(ang)
    split_at = np.cumsum(IN_SPLITS)[:-1].tolist()
    c_act = jax.nn.silu(c)
    for l in range(DEPTH):
        ada = c_act @ w_ada[l] + b_ada[l]
        sh1, sc1, gt1, sh2, sc2, gt2, sh3, sc3, gt3 = jnp.split(ada, N_MOD, axis=-1)
        h = modulate(x, g_ffn1[l], sh1, sc1)
        x = x + 0.5 * gt1[:, None, :] * swiglu(h, w1_gate[l], w1_up[l], w1_down[l])
        h = modulate(x, g_mix[l], sh2, sc2)
        proj = h @ w_in[l]
        q_lat, kv_lat, k_pe, gq, gk, gv, ga, gb, gz, gates = jnp.split(proj, split_at, axis=-1)
        y_mla = mla_branch(q_lat, kv_lat, k_pe, g_q_lat[l], w_uq[l], g_kv_lat[l], w_ukv[l], cos, sin) @ w_o_mla[l]
        y_gdn = gated_deltanet_branch(jnp.concatenate([gq, gk, gv], axis=-1), ga, gb, gz, w_conv[l], a_log[l], dt_bias[l], g_gdn_out[l]) @ w_o_gdn[l]
        gate_mla, gate_gdn = jnp.split(jax.nn.sigmoid(gates), N_BRANCH, axis=-1)
        mixed = (gate_mla * y_mla + gate_gdn * y_gdn) @ w_out[l]
        x = x + gt2[:, None, :] * mixed
        h = modulate(x, g_ffn2[l], sh3, sc3)
        x = x + 0.5 * gt3[:, None, :] * swiglu(h, w2_gate[l], w2_up[l], w2_down[l])
    return rmsnorm(x, g_final)
```

```python
import numpy as np
from contextlib import ExitStack
import concourse.bass as bass
import concourse.mybir as mybir
from concourse.bass_utils import run_bass_kernel_spmd

F32 = mybir.dt.float32
BF16 = mybir.dt.bfloat16
AF = mybir.ActivationFunctionType
ALU = mybir.AluOpType
AX = mybir.AxisListType

S = 4096
D = 1024
DFF = 2816
NFC = 22
EPS = 1e-6
NCORES = 4
TF = 512
NEG = -30000.0
import os as _os
D2MUL = _os.environ.get('D2MUL', 'pool')
QSCALE = 192.0 ** -0.5


class Res:
    __slots__ = ("w", "r")

    def __init__(self):
        self.w = None
        self.r = {}


class Eng:
    def __init__(self, name, idx):
        self.name = name
        self.idx = idx
        self.ops = []
        self.count = 0
        self.waited = {}


class Prog:
    NDMA = 48

    def __init__(self):
        self.eng = {n: Eng(n, i) for i, n in enumerate(["pe", "act", "dve", "pool", "sp"])}
        self.dma_uses = [0] * self.NDMA
        self.dma_next = 0
        self.dma_next_sw = 0
        self.nsem = 5 + self.NDMA

    def _deps(self, reads, writes):
        deps = {}

        def add(s, v):
            if deps.get(s, 0) < v:
                deps[s] = v
        for r in reads:
            if r.w is not None:
                add(*r.w)
        for w in writes:
            if w.w is not None:
                add(*w.w)
            for s, v in w.r.items():
                add(s, v)
        return deps

    def _waits(self, e, deps):
        waits = []
        for s, v in deps.items():
            if e.name == "pe" and s == e.idx:
                continue
            if e.waited.get(s, 0) < v:
                e.waited[s] = v
                waits.append((s, v))
        return waits

    def _mark(self, tok, reads, writes):
        for r in reads:
            if r.r.get(tok[0], 0) < tok[1]:
                r.r[tok[0]] = tok[1]
        for w in writes:
            w.w = tok
            w.r = {}

    def op(self, en, fns, reads=(), writes=()):
        e = self.eng[en]
        if not isinstance(fns, (list, tuple)):
            fns = [fns]
        waits = self._waits(e, self._deps(reads, writes))
        e.count += 1
        tok = (e.idx, e.count)
        e.ops.append((waits, list(fns), (e.idx, 1)))
        self._mark(tok, reads, writes)

    def dma(self, en, out, in_, reads=(), writes=()):
        e = self.eng[en]
        if en == "pool":
            slot = 32 + self.dma_next_sw
            self.dma_next_sw = (self.dma_next_sw + 1) % 16
        else:
            slot = self.dma_next
            self.dma_next = (self.dma_next + 1) % 32
        s = 5 + slot
        deps = self._deps(reads, writes)
        prev = self.dma_uses[slot] * 16
        if prev and deps.get(s, 0) < prev:
            deps[s] = prev
        waits = self._waits(e, deps)
        self.dma_uses[slot] += 1
        tok = (s, self.dma_uses[slot] * 16)
        e.ops.append((waits, [lambda g, out=out, in_=in_: g.dma_start(out=out, in_=in_)], (s, 16)))
        self._mark(tok, reads, writes)

    def barrier(self):
        deps = {e.idx: e.count for e in self.eng.values() if e.count}
        for slot, u in enumerate(self.dma_uses):
            if u:
                deps[5 + slot] = u * 16
        for e in self.eng.values():
            w = self._waits(e, dict(deps))
            if w:
                e.ops.append((w, [], None))

    def final_wait(self, en, resources):
        e = self.eng[en]
        deps = self._deps([], resources)
        e.ops.append((self._waits(e, deps), [], None))

    def emit(self, nc, sems):
        with nc.Block() as block:
            def run(e, g):
                for waits, fns, inc in e.ops:
                    for s, v in waits:
                        g.wait_ge(sems[s], v)
                    ins = None
                    for f in fns:
                        ins = f(g)
                    if inc is not None and ins is not None:
                        ins.then_inc(sems[inc[0]], inc[1])

            @block.tensor
            def _(g):
                run(self.eng["pe"], g)

            @block.scalar
            def _(g):
                run(self.eng["act"], g)

            @block.vector
            def _(g):
                run(self.eng["dve"], g)

            @block.gpsimd
            def _(g):
                run(self.eng["pool"], g)

            @block.sync
            def _(g):
                run(self.eng["sp"], g)


class Ring:
    def __init__(self, tiles):
        self.t = [(t, Res()) for t in tiles]
        self.i = 0

    def next(self):
        r = self.t[self.i % len(self.t)]
        self.i += 1
        return r


def mm(out, lhsT, rhs, start, stop):
    return lambda g: g.matmul(out, lhsT, rhs, start=start, stop=stop)


def build(upto="F", dbg=False):
    nc = bass.Bass("TRN2", target_bir_lowering=False)
    P = Prog()

    def din(name, shape, dt=F32):
        return nc.dram_tensor(name, list(shape), dt, kind="ExternalInput").ap()

    def dscr(name, shape, dt):
        return nc.dram_tensor(name, list(shape), dt, kind=("ExternalOutput" if (dbg and name in dbg) else "Internal")).ap()

    x_in = din("x", [S, D])
    c_fm = din("c_fm", [128, 8])
    w_ada = din("w_ada", [D, 9 * D])
    bada_fm = din("bada_fm", [128, 72])
    gains_fm = din("gains_fm", [128, 4, 8])
    w_gate = [din("w1_gate", [D, DFF]), din("w2_gate", [D, DFF])]
    w_up = [din("w1_up", [D, DFF]), din("w2_up", [D, DFF])]
    w_down = [din("w1_down", [DFF, D]), din("w2_down", [DFF, D])]
    w_in = din("w_in", [D, 7248])
    gq_fm = din("gq_fm", [128, 6])
    gkv_fm = din("gkv_fm", [128, 2])
    w_uq = din("w_uq", [768, 1536])
    w_ukv = din("w_ukv", [256, 2048])
    wconv_fm = din("wconv_fm", [128, 24, 4])
    alog_bc = din("alog_bc", [64, 8])
    dtb_bc = din("dtb_bc", [64, 8])
    ggdn_fm = din("ggdn_fm", [128, 1])
    w_o_mla = din("w_o_mla", [D, D])
    w_o_gdn = din("w_o_gdn", [D, D])
    w_out = din("w_out", [D, D])
    cosT = din("cosT", [64, 4096])
    sinS = din("sinS", [64, 4096])
    c_ident = din("c_ident", [128, 128])
    c_tri = din("c_tri", [64, 64])
    c_eye = din("c_eye", [64, 64])
    c_negm = din("c_negm", [64, 64])
    y_out = nc.dram_tensor("y", [S, D], F32, kind="ExternalOutput").ap()

    X1T = dscr("X1T", [D, S], F32); rX1T = [Res() for _ in range(8)]
    H2T = dscr("H2T", [D, S], BF16); rH2T = [Res() for _ in range(8)]
    QLAT = dscr("QLAT", [768, S], BF16); rQLAT = Res()
    KVLAT = dscr("KVLAT", [256, S], BF16); rKVLAT = Res()
    GQKV = dscr("GQKV", [3072, S], BF16); rGQKV = [Res() for _ in range(24)]
    ZT = dscr("ZT", [1024, S], BF16); rZT = Res()
    GATES = dscr("GATES", [2048, S], BF16); rGATES = Res()
    OMLA = dscr("OMLA", [1024, S], BF16); rOMLA = Res()
    OGDN = dscr("OGDN", [1024, S], BF16); rOGDN = Res()
    X2T = dscr("X2T", [D, S], F32); rX2T = [Res() for _ in range(8)]

    with ExitStack() as top:
        cnt = [0]

        def sbt(st, shape, dt):
            cnt[0] += 1
            return st.enter_context(nc.sbuf_tensor(f"t{cnt[0]}", list(shape), dt))

        def pst(st, shape, dt=F32):
            cnt[0] += 1
            return st.enter_context(nc.psum_tensor(f"p{cnt[0]}", list(shape), dt))

        sems = [top.enter_context(nc.semaphore(f"s{i}")) for i in range(P.nsem)]

        ident_f = sbt(top, [128, 128], F32); r_const = Res()
        ident_b = sbt(top, [128, 128], BF16)
        ones_b = sbt(top, [128, 128], BF16)
        P.dma("sp", ident_f[:], c_ident, writes=[r_const])
        P.op("dve", lambda g: g.tensor_copy(ident_b[:], ident_f[:]), reads=[r_const], writes=[r_const])
        P.op("dve", lambda g: g.memset(ones_b[:], 1.0), writes=[r_const])
        gains = sbt(top, [128, 4, 8], F32)
        P.dma("sp", gains[:], gains_fm, writes=[r_const])
        ada = sbt(top, [128, 72], F32); r_ada = Res()
        PA = sbt(top, [128, 3, 8], F32)
        PG = sbt(top, [128, 3, 8], F32)

        with ExitStack() as st:
            c_sb = sbt(st, [128, 8], F32); rc = Res()
            csil = sbt(st, [128, 8], BF16)
            bada = sbt(st, [128, 72], F32); rb = Res()
            P.dma("sp", c_sb[:], c_fm, writes=[rc])
            P.dma("sp", bada[:], bada_fm, writes=[rb])
            P.op("act", lambda g: g.activation(csil[:], c_sb[:], AF.Silu), reads=[rc], writes=[rc])
            adaps = pst(st, [128, 72]); rps = Res()
            wring = Ring([sbt(st, [128, 8, 1024], BF16) for _ in range(2)])
            wv = w_ada.rearrange("(kc p) n -> p kc n", p=128)
            for j in range(9):
                wb, rw = wring.next()
                P.dma("pool", wb[:], wv[:, :, j * 1024:(j + 1) * 1024], writes=[rw])
                fns = []
                for cc in range(8):
                    for kc in range(8):
                        fns.append(mm(adaps[:, j * 8 + cc:j * 8 + cc + 1], wb[:, kc, cc * 128:(cc + 1) * 128],
                                      csil[:, kc:kc + 1], kc == 0, kc == 7))
                P.op("pe", fns, reads=[rw, rc], writes=[rps])
            P.op("dve", lambda g: g.tensor_tensor(out=ada[:], in0=adaps[:], in1=bada[:], op=ALU.add),
                 reads=[rps, rb], writes=[r_ada])
            for j in range(3):
                P.op("dve", lambda g, j=j: g.scalar_tensor_tensor(
                    out=PA[:, j, :], in0=ada[:, (3 * j + 1) * 8:(3 * j + 2) * 8], scalar=1.0,
                    in1=gains[:, j, :], op0=ALU.add, op1=ALU.mult), reads=[r_ada, r_const], writes=[r_ada])
                P.op("dve", lambda g, j=j: g.tensor_scalar(
                    out=PG[:, j, :], in0=ada[:, (3 * j + 2) * 8:(3 * j + 3) * 8], scalar1=(1.0 if j == 1 else 0.5),
                    scalar2=None, op0=ALU.mult), reads=[r_ada], writes=[r_ada])

        P.barrier()
        if upto == "0":
            ado = nc.dram_tensor("ada_o", [128, 72], F32, kind="ExternalOutput").ap()
            P.dma("sp", ado, ada[:], reads=[r_ada], writes=[r_ada])
            P.final_wait("sp", [r_ada])
            P.emit(nc, sems)
            return nc

        def Bcol(j, kc):
            return ada[:, 3 * j * 8 + kc:3 * j * 8 + kc + 1]

        def fm_stats(src3, rsrc, nkc, T, dim, sq, rsq, pss, rpss, rstd, rrstd):
            P.op("act", lambda g: g.activation(sq[:, 0:nkc, 0:T], src3, AF.Square), reads=[rsrc], writes=[rsq])
            P.op("pe", [mm(pss[:, 0:T], ones_b[:], sq[:, kc, 0:T], kc == 0, kc == nkc - 1) for kc in range(nkc)],
                 reads=[rsq, r_const], writes=[rpss])
            P.op("act", lambda g: g.activation(rstd[:, 0:T], pss[:, 0:T], AF.Sqrt, scale=1.0 / dim, bias=EPS),
                 reads=[rpss], writes=[rrstd])
            P.op("dve", lambda g: g.reciprocal(rstd[:, 0:T], rstd[:, 0:T]), reads=[rrstd], writes=[rrstd])

        def modulate(j, xT, rxT, hT, rhT, T, sq, rsq, pss, rpss, rstd, rrstd, tmp_ring):
            fm_stats(xT[:, :, 0:T], rxT, 8, T, D, sq, rsq, pss, rpss, rstd, rrstd)
            for kc in range(8):
                tmp, rt = tmp_ring.next()
                P.op("dve", lambda g, kc=kc, tmp=tmp: g.scalar_tensor_tensor(
                    out=tmp[:, 0:T], in0=xT[:, kc, 0:T], scalar=PA[:, j, kc:kc + 1], in1=rstd[:, 0:T],
                    op0=ALU.mult, op1=ALU.mult), reads=[rxT, rrstd, r_ada], writes=[rt])
                P.op("act", lambda g, kc=kc, tmp=tmp: g.activation(
                    hT[:, kc, 0:T], tmp[:, 0:T], AF.Identity, bias=Bcol(j, kc), scale=1.0),
                    reads=[rt, r_ada], writes=[rhT])

        fsplit = [(0, 6), (6, 12), (12, 17), (17, 22)]

        def load_gu(which, st, defer=False):
            wg = sbt(st, [128, 8, DFF], BF16)
            wu = sbt(st, [128, 8, DFF], BF16)
            rwg = [Res() for _ in fsplit]; rwu = [Res() for _ in fsplit]
            gv_ = w_gate[which].rearrange("(kc p) f -> p kc f", p=128)
            uv_ = w_up[which].rearrange("(kc p) f -> p kc f", p=128)
            def issue():
                for qi, (a, b) in enumerate(fsplit):
                    P.dma("pool", wg[:, :, a * 128:b * 128], gv_[:, :, a * 128:b * 128], writes=[rwg[qi]])
                    P.dma("pool", wu[:, :, a * 128:b * 128], uv_[:, :, a * 128:b * 128], writes=[rwu[qi]])
            if not defer:
                issue()
                return wg, wu, rwg, rwu
            return (wg, wu, rwg, rwu), issue

        def ffn_phase(which, pre=None):
            with ExitStack() as st:
                if pre is None:
                    pre = load_gu(which, st)
                wg, wu, rwg, rwu = pre
                wd = sbt(st, [128, NFC, D], BF16)
                rwd = [Res() for _ in range(NFC)]
                dv_ = w_down[which].rearrange("(fc p) n -> p fc n", p=128)
                import os
                for fc in range(0, NFC if not os.environ.get('SKIPW') else 0, 2):
                    P.dma("pool", wd[:, fc:fc + 2, :], dv_[:, fc:fc + 2, :], writes=[rwd[fc], rwd[fc + 1]])

                def q_of(f):
                    for qi, (a, b) in enumerate(fsplit):
                        if a <= f < b:
                            return qi
                xT = sbt(st, [128, 8, TF], F32); rxT = Res()
                hT = sbt(st, [128, 8, TF], BF16); rhT = Res()
                actT = sbt(st, [128, NFC, TF], BF16); ract = [Res() for _ in range(NFC)]
                sq = sbt(st, [128, 8, TF], BF16); rsq = Res()
                rstd = sbt(st, [128, TF], F32); rrstd = Res()
                tmp_ring = Ring([sbt(st, [128, TF], F32) for _ in range(2)])
                xtm_ring = Ring([sbt(st, [128, D], F32) for _ in range(2)])
                ps_ring = Ring([pst(st, [128, 512]) for _ in range(6)])
                pss = pst(st, [128, 512]); rpss = Res()
                jn = 0 if which == 0 else 2
                import os
                NTG = int(os.environ.get('NTG', S // TF)); LVL = int(os.environ.get('LVL', 9))
                for tg in range(NTG):
                    t0 = tg * TF
                    if which == 0:
                        for tt in range(TF // 128):
                            xtm, rxtm = xtm_ring.next()
                            P.dma("sp", xtm[:], x_in[t0 + tt * 128:t0 + (tt + 1) * 128, :], writes=[rxtm])
                            for half in range(2):
                                pt, rpt = ps_ring.next()
                                P.op("pe", [lambda g, kc=kc, pt=pt, xtm=xtm, half=half: g.transpose(
                                    pt[:, (kc - 4 * half) * 128:(kc - 4 * half + 1) * 128], xtm[:, kc * 128:(kc + 1) * 128], ident_f[:])
                                    for kc in range(4 * half, 4 * half + 4)], reads=[rxtm, r_const], writes=[rpt])
                                dst = xT[:, 4 * half:4 * half + 4, tt * 128:(tt + 1) * 128]
                                src = pt[:].rearrange("p (k t) -> p k t", k=4)
                                if half:
                                    P.op("act", lambda g, dst=dst, src=src: g.copy(dst, src), reads=[rpt], writes=[rxT])
                                else:
                                    P.op("dve", lambda g, dst=dst, src=src: g.tensor_copy(dst, src), reads=[rpt], writes=[rxT])
                    else:
                        P.dma("sp", xT[:], X2T.rearrange("(kc p) t -> p kc t", p=128)[:, :, t0:t0 + TF],
                              reads=[rX2T[tg]], writes=[rxT])
                    if LVL < 1:
                        P.dma('sp', X1T.rearrange('(kc p) t -> p kc t', p=128)[:, :, t0:t0 + TF], xT[:], reads=[rxT], writes=[rX1T[tg]])
                        continue
                    modulate(jn, xT, rxT, hT, rhT, TF, sq, rsq, pss, rpss, rstd, rrstd, tmp_ring)
                    if LVL < 2:
                        P.dma('sp', H2T.rearrange('(kc p) t -> p kc t', p=128)[:, :, t0:t0 + TF], hT[:], reads=[rhT], writes=[rH2T[tg]])
                        continue
                    for f in range(NFC):
                        pg, rpg = ps_ring.next()
                        pu, rpu = ps_ring.next()
                        P.op("pe", [mm(pg[:, 0:TF], wg[:, kc, f * 128:(f + 1) * 128], hT[:, kc, :], kc == 0, kc == 7)
                                    for kc in range(8)], reads=[rhT, rwg[q_of(f)]], writes=[rpg])
                        P.op("pe", [mm(pu[:, 0:TF], wu[:, kc, f * 128:(f + 1) * 128], hT[:, kc, :], kc == 0, kc == 7)
                                    for kc in range(8)], reads=[rhT, rwu[q_of(f)]], writes=[rpu])
                        tmp, rt = tmp_ring.next()
                        P.op("act", lambda g, pg=pg, tmp=tmp: g.activation(tmp[:, 0:TF], pg[:, 0:TF], AF.Silu),
                             reads=[rpg], writes=[rt])
                        wr = [ract[f]]
                        P.op("dve", lambda g, pu=pu, tmp=tmp, f=f: g.tensor_tensor(
                            out=actT[:, f, :], in0=tmp[:, 0:TF], in1=pu[:, 0:TF], op=ALU.mult),
                            reads=[rt, rpu], writes=wr)
                    for n in range(8):
                        pd, rpd = ps_ring.next()
                        P.op("pe", [mm(pd[:, 0:TF], wd[:, f, n * 128:(n + 1) * 128], actT[:, f, :], f == 0, f == NFC - 1)
                                    for f in range(NFC)], reads=ract + rwd, writes=[rpd])
                        P.op("dve", lambda g, pd=pd, n=n: g.scalar_tensor_tensor(
                            out=xT[:, n, :], in0=pd[:, 0:TF], scalar=PG[:, jn, n:n + 1], in1=xT[:, n, :],
                            op0=ALU.mult, op1=ALU.add), reads=[rpd, r_ada, rxT], writes=[rxT])
                    if which == 0:
                        P.dma("sp", X1T.rearrange("(kc p) t -> p kc t", p=128)[:, :, t0:t0 + TF], xT[:],
                              reads=[rxT], writes=[rX1T[tg]])
                        modulate(1, xT, rxT, hT, rhT, TF, sq, rsq, pss, rpss, rstd, rrstd, tmp_ring)
                        P.dma("sp", H2T.rearrange("(kc p) t -> p kc t", p=128)[:, :, t0:t0 + TF], hT[:],
                              reads=[rhT], writes=[rH2T[tg]])
                    else:
                        fm_stats(xT[:, :, :], rxT, 8, TF, D, sq, rsq, pss, rpss, rstd, rrstd)
                        for kc in range(8):
                            P.op("dve", lambda g, kc=kc: g.scalar_tensor_tensor(
                                out=xT[:, kc, :], in0=xT[:, kc, :], scalar=gains[:, 3, kc:kc + 1], in1=rstd[:, :],
                                op0=ALU.mult, op1=ALU.mult), reads=[rxT, rrstd, r_const], writes=[rxT])
                        for tt in range(TF // 128):
                            xtm, rxtm = xtm_ring.next()
                            for half in range(2):
                                pt, rpt = ps_ring.next()
                                P.op("pe", [lambda g, kc=kc, pt=pt, half=half, tt=tt: g.transpose(
                                    pt[:, (kc - 4 * half) * 128:(kc - 4 * half + 1) * 128],
                                    xT[:, kc, tt * 128:(tt + 1) * 128], ident_f[:])
                                    for kc in range(4 * half, 4 * half + 4)], reads=[rxT, r_const], writes=[rpt])
                                if half:
                                    P.op("act", lambda g, pt=pt, xtm=xtm: g.copy(xtm[:, 512:1024], pt[:]),
                                         reads=[rpt], writes=[rxtm])
                                else:
                                    P.op("dve", lambda g, pt=pt, xtm=xtm: g.tensor_copy(xtm[:, 0:512], pt[:]),
                                         reads=[rpt], writes=[rxtm])
                            P.dma("sp", y_out[t0 + tt * 128:t0 + (tt + 1) * 128, :], xtm[:], reads=[rxtm], writes=[r_y])

        def rsq_guard(rsq, ract):
            for f in range(8):
                for s_, v_ in ract[f].r.items():
                    if rsq.r.get(s_, 0) < v_:
                        rsq.r[s_] = v_
                if ract[f].w is not None:
                    s_, v_ = ract[f].w
                    if rsq.r.get(s_, 0) < v_:
                        rsq.r[s_] = v_
            return rsq

        r_y = Res()

        ffn_phase(0)
        P.barrier()
        if upto == "A":
            P.final_wait("sp", rX1T + rH2T)
            P.emit(nc, sems)
            return nc

        QDT = dscr("QDT", [8, 128, S], BF16); rQDT = Res()
        AITs = dscr("AITs", [8, 64, S], BF16); rAIT = Res()
        ATs = dscr("ATs", [512, 4096], F32); rATs = [Res() for _ in range(8)]
        TTs = dscr("TTs", [512, 4096], BF16); rTTs = [Res() for _ in range(4)]
        KBDs = dscr("KBDs", [8, 64, 64 * 128], BF16); rKBD = Res()
        KDECs = dscr("KDECs", [8, 64, 64 * 128], BF16); rKDEC = Res()
        VBs = dscr("VBs", [8, 64, 64 * 128], BF16); rVB = Res()
        c_eyeb = din("c_eyeb", [128, 4096])

        def gdn_phase():
            with ExitStack() as gst:
                tri_f = sbt(gst, [64, 64], F32); eye_f = sbt(gst, [64, 64], F32); negm = sbt(gst, [64, 64], F32)
                ones_f = sbt(gst, [64, 128], F32); rcg = Res()
                P.dma("sp", tri_f[:], c_tri, writes=[rcg]); P.dma("sp", eye_f[:], c_eye, writes=[rcg]); P.dma("sp", negm[:], c_negm, writes=[rcg])
                P.op("dve", lambda g: g.memset(ones_f[:], 1.0), writes=[rcg])
                dec_tm = sbt(gst, [64, 64, 8], F32); s_kbd = sbt(gst, [64, 64, 8], F32); s_kdec = sbt(gst, [64, 64, 8], F32)
                s_vb = sbt(gst, [64, 64, 8], F32); cd_bc = sbt(gst, [128, 64, 8], F32); rdc = Res()
                ggdn = sbt(gst, [128, 1], F32); wcv = sbt(gst, [128, 24, 4], F32)
                P.dma("sp", ggdn[:], ggdn_fm, writes=[rcg]); P.dma("sp", wcv[:], wconv_fm, writes=[rcg])
                gflat = g_tm[:].rearrange("p n h -> p (n h)")
                fl = lambda t: t[:].rearrange("p n h -> p (n h)")
                with ExitStack() as st:
                    p1 = pst(st, [64, 512]); p2 = pst(st, [64, 512]); p3 = pst(st, [128, 512]); rp = Res()
                    P.op("pe", [mm(p1[:], tri_f[:], gflat, True, True), mm(p2[:], ones_f[:, 0:64], gflat, True, True),
                                mm(p3[:], ones_f[:, 0:128], gflat, True, True)], reads=[r_gl, rcg], writes=[rp])
                    P.op("dve", lambda g: g.tensor_copy(fl(dec_tm), p1[:]), reads=[rp], writes=[rdc])
                    P.op("dve", lambda g: g.tensor_tensor(out=fl(s_kbd), in0=p1[:], in1=fl(lnb_tm), op=ALU.add), reads=[rp, r_gl], writes=[rdc])
                    P.op("act", lambda g: g.activation(fl(s_kbd), fl(s_kbd), AF.Exp), reads=[rdc], writes=[rdc])
                    P.op("dve", lambda g: g.tensor_tensor(out=fl(s_kdec), in0=p2[:], in1=fl(dec_tm), op=ALU.subtract), reads=[rp, rdc], writes=[rdc])
                    P.op("act", lambda g: g.activation(fl(s_kdec), fl(s_kdec), AF.Exp), reads=[rdc], writes=[rdc])
                    P.op("act", lambda g: g.activation(fl(s_vb), fl(lnb_tm), AF.Exp), reads=[r_gl], writes=[rdc])
                    P.op("act", lambda g: g.activation(fl(cd_bc), p3[:], AF.Exp), reads=[rp], writes=[rdc])
                P.barrier()
                with ExitStack() as st:
                    xin_ring = Ring([sbt(st, [128, S + 4], BF16) for _ in range(2)])
                    y_ring = Ring([sbt(st, [128, S], F32) for _ in range(2)])
                    qT = sbt(st, [128, S], BF16); kT = sbt(st, [128, S], BF16); vT = sbt(st, [128, S], BF16)
                    rqT = Res(); rkT = Res(); rvT = Res()
                    QD = sbt(st, [128, S], BF16); rQD = Res()
                    AIT = sbt(st, [64, 64, 64], BF16); rAITt = Res()
                    KBD = sbt(st, [64, 64, 128], BF16); KDEC = sbt(st, [64, 64, 128], BF16); VB = sbt(st, [64, 64, 128], BF16)
                    rKBDt = Res(); rKDECt = Res(); rVBt = Res()
                    sqr = Ring([sbt(st, [128, 512], BF16) for _ in range(2)])
                    rsr = Ring([sbt(st, [128, 512], F32) for _ in range(2)])
                    gt1r = Ring([sbt(st, [64, 8, 64], F32) for _ in range(2)])
                    gt2r = Ring([sbt(st, [64, 8, 64], F32) for _ in range(2)])
                    tmr = Ring([sbt(st, [64, 8, 64], F32) for _ in range(2)])
                    l1r = Ring([sbt(st, [64, 8, 64], F32) for _ in range(2)])
                    l2r = Ring([sbt(st, [64, 8, 64], F32) for _ in range(2)])
                    edr = Ring([sbt(st, [128, 8, 64], F32) for _ in range(2)])
                    atr = Ring([sbt(st, [64, 8, 64], F32) for _ in range(2)])
                    psr = Ring([pst(st, [128, 512]) for _ in range(4)])
                    ptr_ = Ring([pst(st, [64, 1024], F32) for _ in range(2)])
                    for (xt_, rxt_) in xin_ring.t:
                        P.op("pool", lambda g, xt_=xt_: g.memset(xt_[:, 0:4], 0.0), writes=[rxt_])
                    dg_ring = Ring([sbt(st, [128, 4, 128], BF16) for _ in range(2)])
                    for h in range(8):
                        for which_, dstT, rdst in ((0, qT, rqT), (1, kT, rkT), (2, vT, rvT)):
                            ch = which_ * 8 + h
                            (xA, rxA), (xB, rxB) = xin_ring.t
                            y, ry = y_ring.next()
                            P.dma("sp", xA[:, 4:S + 4], GQKV[ch * 128:(ch + 1) * 128, :], reads=[rGQKV[ch]], writes=[rxA])
                            P.op("act", lambda g, xA=xA, xB=xB: g.copy(xB[:, 3:S + 3], xA[:, 4:S + 4]), reads=[rxA], writes=[rxB])
                            dg, rdg = dg_ring.next()
                            P.op("pool", lambda g, dg=dg, ch=ch: g.tensor_tensor(out=dg[:], in0=ident_b[:].unsqueeze(1).to_broadcast([128, 4, 128]),
                                 in1=wcv[:, ch, :].unsqueeze(2).to_broadcast([128, 4, 128]), op=ALU.mult), reads=[r_const, rcg], writes=[rdg])
                            for tg in range(8):
                                T0 = tg * 512
                                ps, rps = psr.next()
                                P.op("pe", [mm(ps[:], dg[:, 0, :], xB[:, T0:T0 + 512], True, False),
                                            mm(ps[:], dg[:, 1, :], xA[:, T0 + 2:T0 + 514], False, False),
                                            mm(ps[:], dg[:, 2, :], xB[:, T0 + 2:T0 + 514], False, False),
                                            mm(ps[:], dg[:, 3, :], xA[:, T0 + 4:T0 + 516], False, True)],
                                     reads=[rdg, rxA, rxB], writes=[rps])
                                if which_ == 2:
                                    P.op("act", lambda g, ps=ps, T0=T0: g.activation(vT[:, T0:T0 + 512], ps[:], AF.Silu), reads=[rps], writes=[rvT])
                                else:
                                    P.op("act", lambda g, ps=ps, T0=T0, y=y: g.activation(y[:, T0:T0 + 512], ps[:], AF.Silu), reads=[rps], writes=[ry])
                            if which_ != 2:
                                for tg in range(8):
                                    tsl = slice(tg * 512, (tg + 1) * 512)
                                    sq_, rsq_ = sqr.next(); rs_, rrs_ = rsr.next(); ps, rps = psr.next()
                                    P.op("act", lambda g, sq_=sq_, tsl=tsl, y=y: g.activation(sq_[:], y[:, tsl], AF.Square), reads=[ry], writes=[rsq_])
                                    P.op("pe", mm(ps[:], ones_b[:], sq_[:], True, True), reads=[rsq_, r_const], writes=[rps])
                                    P.op("act", lambda g, rs_=rs_, ps=ps: g.activation(rs_[:], ps[:], AF.Sqrt, scale=1.0, bias=EPS), reads=[rps], writes=[rrs_])
                                    P.op("dve", lambda g, rs_=rs_: g.reciprocal(rs_[:], rs_[:]), reads=[rrs_], writes=[rrs_])
                                    P.op("dve", lambda g, rs_=rs_, tsl=tsl, dstT=dstT, y=y, sc=(128.0 ** -0.5 if which_ == 0 else 1.0): g.scalar_tensor_tensor(
                                        out=dstT[:, tsl], in0=y[:, tsl], scalar=sc, in1=rs_[:], op0=ALU.mult, op1=ALU.mult), reads=[ry, rrs_], writes=[rdst])
                        for nb in range(8):
                            nsl = slice(nb * 8, (nb + 1) * 8)
                            tsl = slice(nb * 512, (nb + 1) * 512)
                            gbc = g_tm[:, nsl, h:h + 1].to_broadcast([64, 8, 64])
                            lbc = lnb_tm[:, nsl, h:h + 1].to_broadcast([64, 8, 64])
                            dbc = dec_tm[:, nsl, h:h + 1].to_broadcast([64, 8, 64])
                            gt1, rg1 = gt1r.next(); gt2, rg2 = gt2r.next()
                            P.op("pool", lambda g, gt1=gt1, gbc=gbc: g.tensor_tensor(out=gt1[:], in0=tri_f[:].unsqueeze(1).to_broadcast([64, 8, 64]), in1=gbc, op=ALU.mult),
                                 reads=[r_gl, rcg], writes=[rg1])
                            P.op("pool", lambda g, gt2=gt2, lbc=lbc: g.tensor_tensor(out=gt2[:], in0=eye_f[:].unsqueeze(1).to_broadcast([64, 8, 64]), in1=lbc, op=ALU.mult),
                                 reads=[r_gl, rcg], writes=[rg2])
                            P.op("pool", lambda g, gt1=gt1, gt2=gt2: g.tensor_tensor(out=gt2[:], in0=gt2[:], in1=gt1[:], op=ALU.add), reads=[rg1, rg2], writes=[rg2])
                            pd1, rpd1 = psr.next(); pd2, rpd2 = psr.next()
                            P.op("pe", mm(pd1[:], ones_f[:, 0:128], gt1[:].rearrange("p a b -> p (a b)"), True, True), reads=[rg1, rcg], writes=[rpd1])
                            P.op("pe", mm(pd2[0:64, :], ones_f[:, 0:64], gt2[:].rearrange("p a b -> p (a b)"), True, True), reads=[rg2, rcg], writes=[rpd2])
                            ed, red = edr.next()
                            P.op("act", lambda g, ed=ed, pd1=pd1: g.activation(ed[:].rearrange("p a b -> p (a b)"), pd1[:], AF.Exp), reads=[rpd1], writes=[red])
                            P.op("pool", lambda g, ed=ed, tsl=tsl: g.tensor_tensor(out=QD[:, tsl], in0=qT[:, tsl], in1=ed[:].rearrange("p a b -> p (a b)"), op=ALU.mult),
                                 reads=[rqT, red], writes=[rQD])
                            l1, rl1 = l1r.next(); l2, rl2 = l2r.next()
                            for (pd, rpd, ll, rll) in ((pd1, rpd1, l1, rl1), (pd2, rpd2, l2, rl2)):
                                tm_, rtm_ = tmr.next()
                                P.op("dve", lambda g, tm_=tm_, pd=pd, dbc=dbc: g.tensor_tensor(out=tm_[:], in0=pd[0:64, :].rearrange("p (a b) -> p a b", a=8), in1=dbc, op=ALU.subtract),
                                     reads=[rpd, rdc], writes=[rtm_])
                                P.op("pool", lambda g, tm_=tm_: g.tensor_tensor(out=tm_[:], in0=tm_[:], in1=negm[:].unsqueeze(1).to_broadcast([64, 8, 64]), op=ALU.add),
                                     reads=[rtm_, rcg], writes=[rtm_])
                                P.op("act", lambda g, tm_=tm_, ll=ll: g.activation(ll[:], tm_[:], AF.Exp), reads=[rtm_], writes=[rll])
                            pkq, rpkq = psr.next(); pkk, rpkk = psr.next()
                            P.op("pe", [mm(pkq[0:64, n * 64:(n + 1) * 64], kT[:, nb * 512 + n * 64:nb * 512 + (n + 1) * 64], qT[:, nb * 512 + n * 64:nb * 512 + (n + 1) * 64], True, True)
                                        for n in range(8)], reads=[rkT, rqT], writes=[rpkq])
                            P.op("pe", [mm(pkk[0:64, n * 64:(n + 1) * 64], kT[:, nb * 512 + n * 64:nb * 512 + (n + 1) * 64], kT[:, nb * 512 + n * 64:nb * 512 + (n + 1) * 64], True, True)
                                        for n in range(8)], reads=[rkT], writes=[rpkk])
                            P.op("dve", lambda g, pkq=pkq, l1=l1, nsl=nsl: g.tensor_tensor(out=AIT[:, nsl, :], in0=pkq[0:64, :].rearrange("p (a b) -> p a b", a=8), in1=l1[:], op=ALU.mult),
                                 reads=[rpkq, rl1], writes=[rAITt])
                            at_, rat_ = atr.next()
                            P.op("dve", lambda g, pkk=pkk, l2=l2, at_=at_: g.tensor_tensor(out=at_[:], in0=pkk[0:64, :].rearrange("p (a b) -> p a b", a=8), in1=l2[:], op=ALU.mult),
                                 reads=[rpkk, rl2], writes=[rat_])
                            P.dma("sp", ATs[h * 64 + nb * 8:h * 64 + nb * 8 + 8, :].rearrange("n (j i) -> j n i", j=64), at_[:], reads=[rat_], writes=[rATs[h]])
                            pk_, rpk_ = ptr_.next(); pv_, rpv_ = ptr_.next()
                            P.op("pe", [mm(pk_[:, n * 128:(n + 1) * 128], kT[:, nb * 512 + n * 64:nb * 512 + (n + 1) * 64], ident_b[:], True, True)
                                        for n in range(8)], reads=[rkT, r_const], writes=[rpk_])
                            P.op("pe", [mm(pv_[:, n * 128:(n + 1) * 128], vT[:, nb * 512 + n * 64:nb * 512 + (n + 1) * 64], ident_b[:], True, True)
                                        for n in range(8)], reads=[rvT, r_const], writes=[rpv_])
                            for (pp, rpp, dst_, rdst_, sc_) in ((pk_, rpk_, KBD, rKBDt, s_kbd), (pk_, rpk_, KDEC, rKDECt, s_kdec), (pv_, rpv_, VB, rVBt, s_vb)):
                                P.op("dve", lambda g, pp=pp, dst_=dst_, sc_=sc_, nsl=nsl, h=h: g.tensor_tensor(
                                    out=dst_[:, nsl, :], in0=pp[:].rearrange("p (a b) -> p a b", a=8), in1=sc_[:, nsl, h:h + 1].to_broadcast([64, 8, 128]), op=ALU.mult),
                                    reads=[rpp, rdc], writes=[rdst_])
                        P.dma("sp", QDT[h], QD[:], reads=[rQD], writes=[rQDT])
                        P.dma("sp", AITs[h], AIT[:].rearrange("p a b -> p (a b)"), reads=[rAITt], writes=[rAIT])
                        P.dma("sp", KBDs[h], KBD[:].rearrange("p a b -> p (a b)"), reads=[rKBDt], writes=[rKBD])
                        P.dma("sp", KDECs[h], KDEC[:].rearrange("p a b -> p (a b)"), reads=[rKDECt], writes=[rKDEC])
                        P.dma("sp", VBs[h], VB[:].rearrange("p a b -> p (a b)"), reads=[rVBt], writes=[rVB])
                P.barrier()
                if upto == "D1":
                    return True
                with ExitStack() as st:
                    sets = [(sbt(st, [128, 64, 64], F32), sbt(st, [128, 64, 64], F32), sbt(st, [128, 64, 64], F32), sbt(st, [128, 4096], BF16),
                             Res(), Res(), Res()) for _ in range(2)]
                    for pp_ in range(2):
                        for gi in range(2):
                            pg = pp_ * 2 + gi
                            A_, X_, tmp_, Xb, rA, rX, rT = sets[gi]
                            P.dma("sp", A_[:].rearrange("p a b -> p (a b)"), ATs[pg * 128:(pg + 1) * 128, :], reads=[rATs[2 * pg], rATs[2 * pg + 1]], writes=[rA])
                            P.dma("sp", X_[:].rearrange("p a b -> p (a b)"), c_eyeb, writes=[rX])
                        for j in range(62, -1, -1):
                            m = 63 - j
                            for gi in range(2):
                                A_, X_, tmp_, Xb, rA, rX, rT = sets[gi]
                                P.op("dve" if gi == 0 else D2MUL, lambda g, j=j, m=m, A_=A_, X_=X_, tmp_=tmp_: g.tensor_tensor(
                                    out=tmp_[:, 0:m, 0:m], in0=X_[:, j + 1:64, j + 1:64].rearrange("p i c -> p c i"),
                                    in1=A_[:, j, j + 1:64].unsqueeze(1).to_broadcast([128, m, m]), op=ALU.mult), reads=[rA, rX], writes=[rT])
                            for gi in range(2):
                                A_, X_, tmp_, Xb, rA, rX, rT = sets[gi]
                                P.op("dve", lambda g, j=j, m=m, X_=X_, tmp_=tmp_: g.tensor_reduce(
                                    out=X_[:, j, j + 1:64], in_=tmp_[:, 0:m, 0:m], axis=AX.X, op=ALU.add, negate=True), reads=[rT], writes=[rX])
                        for gi in range(2):
                            pg = pp_ * 2 + gi
                            A_, X_, tmp_, Xb, rA, rX, rT = sets[gi]
                            P.op("act", lambda g, X_=X_, Xb=Xb: g.copy(Xb[:], X_[:].rearrange("p a b -> p (a b)")), reads=[rX], writes=[rX])
                            P.dma("sp", TTs[pg * 128:(pg + 1) * 128, :], Xb[:], reads=[rX], writes=[rTTs[pg]])
                P.barrier()
                if upto == "D2":
                    return True
                with ExitStack() as st:
                    TTb = sbt(st, [64, 8, 8, 64], BF16); KBDb = sbt(st, [64, 8, 8, 128], BF16)
                    VBb = sbt(st, [64, 8, 8, 128], BF16)
                    kd_ring = Ring([sbt(st, [64, 8, 8, 128], BF16) for _ in range(2)])
                    qd_ring = Ring([sbt(st, [128, 8, 512], BF16) for _ in range(2)])
                    ai_ring = Ring([sbt(st, [64, 8, 8, 64], BF16) for _ in range(2)])
                    ZTb = sbt(st, [128, 8, 512], BF16); WTb = sbt(st, [128, 8, 8, 64], BF16); Ub = sbt(st, [64, 8, 8, 128], BF16)
                    Oraw = sbt(st, [128, 8, 512], F32); OG = sbt(st, [128, 8, 512], BF16)
                    Sf = sbt(st, [128, 8, 128], F32); Sb = sbt(st, [128, 8, 128], BF16); vnew = sbt(st, [64, 8, 128], BF16)
                    sq_ = sbt(st, [128, 512], BF16); rs_ = sbt(st, [128, 512], F32); t_ = sbt(st, [128, 512], F32)
                    rTTb = Res(); rKBDb = Res(); rKDECb = Res(); rVBb = Res(); rQDb = Res(); rAITb = Res(); rZTb = Res()
                    rWTb = Res(); rUb = Res(); rOraw = Res(); rOG = Res(); rS = Res(); rSb = Res(); rvn = Res(); rsq_ = Res(); rrs_ = Res(); rt_ = Res()
                    wps = pst(st, [128, 512]); ups = pst(st, [64, 1024]); wsps = pst(st, [64, 1024]); ops_ = pst(st, [128, 512]); dsps = pst(st, [128, 1024])
                    rwps = Res(); rups = Res(); rwsps = Res(); rops = Res(); rdsps = Res()
                    rSh = [Res(), Res()]; rSbh = [Res(), Res()]; rwsh = [Res(), Res()]; rvnh = [Res(), Res()]; _ro = Res(); ropsh = [_ro, _ro]; rdsh = [Res(), Res()]
                    P.op("dve", lambda g: g.memset(Sf[:], 0.0), writes=rSh)
                    P.op("dve", lambda g: g.memset(Sb[:], 0.0), writes=rSbh)
                    for nb in range(8):
                        tsl = slice(nb * 512, (nb + 1) * 512)
                        for h in range(8):
                            P.dma("sp", TTb[:, h, :, :], TTs[h * 64 + nb * 8:h * 64 + nb * 8 + 8, :].rearrange("n (j i) -> j n i", j=64),
                                  reads=[rTTs[h // 2]], writes=[rTTb])
                        P.dma("sp", KBDb[:], KBDs.rearrange("h t (n d) -> t h n d", d=128)[:, :, nb * 8:(nb + 1) * 8, :], reads=[rKBD], writes=[rKBDb])
                        P.dma("sp", VBb[:], VBs.rearrange("h t (n d) -> t h n d", d=128)[:, :, nb * 8:(nb + 1) * 8, :], reads=[rVB], writes=[rVBb])
                        KDECb, rKDECb = kd_ring.next(); QDb, rQDb = qd_ring.next(); AITb, rAITb = ai_ring.next()
                        P.dma("sp", KDECb[:], KDECs.rearrange("h t (n d) -> t h n d", d=128)[:, :, nb * 8:(nb + 1) * 8, :], reads=[rKDEC], writes=[rKDECb])
                        P.dma("sp", QDb[:], QDT.rearrange("h d t -> d h t")[:, :, tsl], reads=[rQDT], writes=[rQDb])
                        P.dma("sp", AITb[:], AITs.rearrange("h j (n i) -> j h n i", i=64)[:, :, nb * 8:(nb + 1) * 8, :], reads=[rAIT], writes=[rAITb])
                        P.dma("sp", ZTb[:], ZT.rearrange("(h p) t -> p h t", p=128)[:, :, tsl], reads=[rZT], writes=[rZTb])
                        for h in range(8):
                            P.op("pe", [mm(wps[:, n * 64:(n + 1) * 64], KBDb[:, h, n, :], TTb[:, h, n, :], True, True) for n in range(8)],
                                 reads=[rKBDb, rTTb], writes=[rwps])
                            P.op("act", lambda g, h=h: g.copy(WTb[:, h, :, :], wps[:].rearrange("p (a b) -> p a b", a=8)), reads=[rwps], writes=[rWTb])
                            P.op("pe", [mm(ups[:, n * 128:(n + 1) * 128], TTb[:, h, n, :], VBb[:, h, n, :], True, True) for n in range(8)],
                                 reads=[rVBb, rTTb], writes=[rups])
                            P.op("dve", lambda g, h=h: g.tensor_copy(Ub[:, h, :, :], ups[:].rearrange("p (a b) -> p a b", a=8)), reads=[rups], writes=[rUb])
                        for n in range(8):
                            ng = nb * 8 + n
                            for hg in range(2):
                                hs = range(4 * hg, 4 * hg + 4)
                                P.op("pe", [mm(wsps[:, h * 128:(h + 1) * 128], WTb[:, h, n, :], Sb[:, h, :], True, True) for h in hs],
                                     reads=[rWTb, rSbh[hg]], writes=[rwsh[hg]])
                            for hg in range(2):
                                hsl = slice(4 * hg, 4 * hg + 4)
                                P.op("dve", lambda g, n=n, hsl=hsl, hg=hg: g.tensor_tensor(out=vnew[:, hsl, :], in0=Ub[:, hsl, n, :],
                                     in1=wsps[:, hg * 512:(hg + 1) * 512].rearrange("p (a b) -> p a b", a=4), op=ALU.subtract),
                                     reads=[rUb, rwsh[hg]], writes=[rvnh[hg]])
                            for hg in range(2):
                                hs = range(4 * hg, 4 * hg + 4)
                                fns = []
                                for h in hs:
                                    fns.append(mm(ops_[:, h * 64:(h + 1) * 64], Sb[:, h, :], QDb[:, h, n * 64:(n + 1) * 64], True, False))
                                    fns.append(mm(ops_[:, h * 64:(h + 1) * 64], vnew[:, h, :], AITb[:, h, n, :], False, True))
                                P.op("pe", fns, reads=[rSbh[hg], rQDb, rvnh[hg], rAITb], writes=[ropsh[hg]])
                                P.op("pe", [mm(dsps[:, h * 128:(h + 1) * 128], KDECb[:, h, n, :], vnew[:, h, :], True, True) for h in hs],
                                     reads=[rKDECb, rvnh[hg]], writes=[rdsh[hg]])
                            for hg in range(2):
                                hsl = slice(4 * hg, 4 * hg + 4)
                                P.op("act", lambda g, n=n, hsl=hsl, hg=hg: g.copy(Oraw[:, hsl, n * 64:(n + 1) * 64],
                                     ops_[:, hg * 256:(hg + 1) * 256].rearrange("p (a b) -> p a b", a=4)), reads=[ropsh[hg]], writes=[rOraw])
                                P.op("dve", lambda g, ng=ng, hsl=hsl: g.tensor_tensor(out=Sf[:, hsl, :], in0=Sf[:, hsl, :],
                                     in1=cd_bc[:, ng, hsl].unsqueeze(2).to_broadcast([128, 4, 128]), op=ALU.mult), reads=[rSh[hg], rdc], writes=[rSh[hg]])
                                P.op("dve", lambda g, hsl=hsl, hg=hg: g.tensor_tensor(out=Sf[:, hsl, :], in0=Sf[:, hsl, :],
                                     in1=dsps[:, hg * 512:(hg + 1) * 512].rearrange("p (a b) -> p a b", a=4), op=ALU.add), reads=[rSh[hg], rdsh[hg]], writes=[rSh[hg]])
                                P.op("act", lambda g, hsl=hsl: g.copy(Sb[:, hsl, :], Sf[:, hsl, :]), reads=[rSh[hg]], writes=[rSbh[hg]])
                        for h in range(8):
                            P.op("act", lambda g, h=h: g.activation(sq_[:], Oraw[:, h, :], AF.Square), reads=[rOraw], writes=[rsq_])
                            P.op("pe", mm(wps[:], ones_b[:], sq_[:], True, True), reads=[rsq_, r_const], writes=[rwps])
                            P.op("act", lambda g: g.activation(rs_[:], wps[:], AF.Sqrt, scale=1.0 / 128, bias=EPS), reads=[rwps], writes=[rrs_])
                            P.op("dve", lambda g: g.reciprocal(rs_[:], rs_[:]), reads=[rrs_], writes=[rrs_])
                            P.op("dve", lambda g, h=h: g.scalar_tensor_tensor(out=t_[:], in0=Oraw[:, h, :], scalar=ggdn[:, 0:1], in1=rs_[:], op0=ALU.mult, op1=ALU.mult),
                                 reads=[rOraw, rrs_, rcg], writes=[rt_])
                            P.op("dve", lambda g, h=h: g.tensor_tensor(out=OG[:, h, :], in0=t_[:], in1=ZTb[:, h, :], op=ALU.mult), reads=[rt_, rZTb], writes=[rOG])
                        P.dma("sp", OGDN.rearrange("(h p) t -> p h t", p=128)[:, :, tsl], OG[:], reads=[rOG], writes=[rOGDN])
                P.barrier()

        with ExitStack() as mid:
            KR = sbt(mid, [64, S], BF16); rKR = Res()
            g_tm = sbt(mid, [64, 64, 8], F32); lnb_tm = sbt(mid, [64, 64, 8], F32); r_gl = Res()
            bc = ExitStack()
            cos_sb = sbt(bc, [64, S], F32); sin_sb = sbt(bc, [64, S], F32); r_cs = Res()
            P.dma("sp", cos_sb[:], cosT, writes=[r_cs])
            P.dma("sp", sin_sb[:], sinS, writes=[r_cs])

            with ExitStack() as st:
                h2 = sbt(st, [128, 8, S], BF16); rh2 = [Res() for _ in range(8)]
                H2v = H2T.rearrange("(kc p) t -> p kc t", p=128)
                for tg in range(8):
                    P.dma("sp", h2[:, :, tg * 512:(tg + 1) * 512], H2v[:, :, tg * 512:(tg + 1) * 512],
                          reads=[rH2T[tg]], writes=[rh2[tg]])
                wring = Ring([sbt(st, [128, 8, 512], BF16) for _ in range(2)])
                oring = Ring([sbt(st, [128, S], BF16) for _ in range(2)])
                psr = Ring([pst(st, [128, 512]) for _ in range(4)])
                winv = w_in.rearrange("(kc p) n -> p kc n", p=128)
                jobs = [(0, 512, QLAT, 0, None, [rQLAT] * 4), (512, 256, QLAT, 512, None, [rQLAT] * 2),
                        (768, 256, KVLAT, 0, None, [rKVLAT] * 2)]
                for i in range(6):
                    jobs.append((1088 + i * 512, 512, GQKV, i * 512, None, rGQKV[i * 4:(i + 1) * 4]))
                for i in range(2):
                    jobs.append((4176 + i * 512, 512, ZT, i * 512, AF.Silu, [rZT] * 4))
                for i in range(4):
                    jobs.append((5200 + i * 512, 512, GATES, i * 512, AF.Sigmoid, [rGATES] * 4))
                ev = 0
                for (c0, W, dst, r0, func, rds) in jobs:
                    wb, rw = wring.next()
                    P.dma("pool", wb[:, :, 0:W], winv[:, :, c0:c0 + W], writes=[rw])
                    for cc in range(W // 128):
                        ot, rot = oring.next()
                        for tg in range(8):
                            ps, rps = psr.next()
                            P.op("pe", [mm(ps[:], wb[:, kc, cc * 128:(cc + 1) * 128], h2[:, kc, tg * 512:(tg + 1) * 512],
                                           kc == 0, kc == 7) for kc in range(8)], reads=[rw, rh2[tg]], writes=[rps])
                            dsl = ot[:, tg * 512:(tg + 1) * 512]
                            if func is not None:
                                P.op("act", lambda g, dsl=dsl, ps=ps, func=func: g.activation(dsl, ps[:], func),
                                     reads=[rps], writes=[rot])
                            elif ev % 2 == 0:
                                P.op("act", lambda g, dsl=dsl, ps=ps: g.copy(dsl, ps[:]), reads=[rps], writes=[rot])
                            else:
                                P.op("dve", lambda g, dsl=dsl, ps=ps: g.tensor_copy(dsl, ps[:]), reads=[rps], writes=[rot])
                            ev += 1
                        P.dma("sp", dst[r0 + cc * 128:r0 + (cc + 1) * 128, :], ot[:], reads=[rot], writes=[rds[cc]])
                wkp = sbt(st, [128, 8, 64], BF16); wkps = sbt(st, [128, 8, 64], BF16); rwk = Res()
                P.dma("pool", wkp[:], winv[:, :, 1024:1088], writes=[rwk])
                P.dma("pool", wkps[:, :, 0:32], winv[:, :, 1056:1088], writes=[rwk])
                P.dma("pool", wkps[:, :, 32:64], winv[:, :, 1024:1056], writes=[rwk])
                t1r = Ring([sbt(st, [64, 512], F32) for _ in range(2)])
                t2r = Ring([sbt(st, [64, 512], F32) for _ in range(2)])
                for tg in range(8):
                    pa, rpa = psr.next(); pb, rpb = psr.next()
                    tsl = slice(tg * 512, (tg + 1) * 512)
                    P.op("pe", [mm(pa[0:64, :], wkp[:, kc, :], h2[:, kc, tsl], kc == 0, kc == 7) for kc in range(8)],
                         reads=[rwk, rh2[tg]], writes=[rpa])
                    P.op("pe", [mm(pb[0:64, :], wkps[:, kc, :], h2[:, kc, tsl], kc == 0, kc == 7) for kc in range(8)],
                         reads=[rwk, rh2[tg]], writes=[rpb])
                    t1, rt1 = t1r.next(); t2, rt2 = t2r.next()
                    P.op("dve", lambda g, t1=t1, pa=pa, tsl=tsl: g.tensor_tensor(out=t1[:], in0=pa[0:64, :], in1=cos_sb[:, tsl], op=ALU.mult),
                         reads=[rpa, r_cs], writes=[rt1])
                    P.op("dve", lambda g, t2=t2, pb=pb, tsl=tsl: g.tensor_tensor(out=t2[:], in0=pb[0:64, :], in1=sin_sb[:, tsl], op=ALU.mult),
                         reads=[rpb, r_cs], writes=[rt2])
                    P.op("dve", lambda g, t1=t1, t2=t2, tsl=tsl: g.tensor_tensor(out=KR[:, tsl], in0=t1[:], in1=t2[:], op=ALU.add),
                         reads=[rt1, rt2], writes=[rKR])
                wab = sbt(st, [128, 8, 16], BF16); rwab = Res()
                P.dma("pool", wab[:], winv[:, :, 4160:4176], writes=[rwab])
                alog_sb = sbt(st, [64, 8], F32); dtb_sb = sbt(st, [64, 8], F32); r_ad = Res()
                P.dma("sp", alog_sb[:], alog_bc, writes=[r_ad])
                P.dma("sp", dtb_sb[:], dtb_bc, writes=[r_ad])
                abps = pst(st, [64, 1024]); rab = Res()
                for n in range(64):
                    P.op("pe", [mm(abps[:, n * 16:(n + 1) * 16], h2[:, kc, n * 64:(n + 1) * 64], wab[:, kc, :], kc == 0, kc == 7)
                                for kc in range(8)], reads=[rwab, rh2[n // 8]], writes=[rab])
                abv = abps[:].rearrange("p (n c) -> p n c", c=16)
                tA = sbt(st, [64, 64, 8], F32); tB = sbt(st, [64, 64, 8], F32); rtA = Res(); rtB = Res()
                P.op("dve", lambda g: g.tensor_tensor(out=tA[:], in0=abv[:, :, 0:8], in1=dtb_sb[:].unsqueeze(1).to_broadcast([64, 64, 8]), op=ALU.add),
                     reads=[rab, r_ad], writes=[rtA])
                P.op("act", lambda g: g.activation(tA[:], tA[:], AF.Exp), reads=[rtA], writes=[rtA])
                P.op("act", lambda g: g.activation(tA[:], tA[:], AF.Ln, bias=1.0), reads=[rtA], writes=[rtA])
                P.op("act", lambda g: g.activation(alog_sb[:], alog_sb[:], AF.Exp), reads=[r_ad], writes=[r_ad])
                P.op("dve", lambda g: g.tensor_scalar(out=alog_sb[:], in0=alog_sb[:], scalar1=-1.0, scalar2=None, op0=ALU.mult),
                     reads=[r_ad], writes=[r_ad])
                P.op("dve", lambda g: g.tensor_tensor(out=g_tm[:], in0=tA[:], in1=alog_sb[:].unsqueeze(1).to_broadcast([64, 64, 8]), op=ALU.mult),
                     reads=[rtA, r_ad], writes=[r_gl])
                P.op("act", lambda g: g.activation(tB[:], abv[:, :, 8:16], AF.Exp, scale=-1.0), reads=[rab], writes=[rtB])
                P.op("act", lambda g: g.activation(tB[:], tB[:], AF.Ln, bias=1.0), reads=[rtB], writes=[rtB])
                P.op("dve", lambda g: g.tensor_scalar(out=lnb_tm[:], in0=tB[:], scalar1=-1.0, scalar2=None, op0=ALU.mult),
                     reads=[rtB], writes=[r_gl])
            P.barrier()
            if upto == "B":
                if dbg:
                    kro = nc.dram_tensor("KR_o", [64, S], BF16, kind="ExternalOutput").ap()
                    gto = nc.dram_tensor("g_o", [64, 512], F32, kind="ExternalOutput").ap()
                    lbo = nc.dram_tensor("lnb_o", [64, 512], F32, kind="ExternalOutput").ap()
                    P.dma("sp", kro, KR[:], reads=[rKR], writes=[rKR])
                    P.dma("sp", gto, g_tm[:].rearrange("p n h -> p (n h)"), reads=[r_gl], writes=[r_gl])
                    P.dma("sp", lbo, lnb_tm[:].rearrange("p n h -> p (n h)"), reads=[r_gl], writes=[r_gl])
                P.final_wait("sp", [rQLAT, rKVLAT, rZT, rGATES, rKR, r_gl] + rGQKV)
                bc.close()
                P.emit(nc, sems)
                return nc

            with ExitStack() as st:
                qn = sbt(st, [128, 6, S], BF16); rqn = Res()
                kvn = sbt(st, [128, 2, S], BF16); rkvn = Res()
                for kc in range(6):
                    P.dma("sp", qn[:, kc, :], QLAT[kc * 128:(kc + 1) * 128, :], reads=[rQLAT], writes=[rqn])
                for kc in range(2):
                    P.dma("sp", kvn[:, kc, :], KVLAT[kc * 128:(kc + 1) * 128, :], reads=[rKVLAT], writes=[rkvn])
                gq_sb = sbt(st, [128, 6], F32); gkv_sb = sbt(st, [128, 2], F32); rgg = Res()
                P.dma("sp", gq_sb[:], gq_fm, writes=[rgg])
                P.dma("sp", gkv_sb[:], gkv_fm, writes=[rgg])
                wuq = sbt(st, [128, 6, 1536], BF16); wuqs = sbt(st, [128, 6, 8, 64], BF16); wukv = sbt(st, [128, 2, 2048], BF16)
                rwq = Res()
                wuqv = w_uq.rearrange("(kc p) n -> p kc n", p=128)
                wuq4 = w_uq.rearrange("(kc p) (h c) -> p kc h c", p=128, c=192)
                for kc in range(6):
                    P.dma("pool", wuq[:, kc, :], wuqv[:, kc, :], writes=[rwq])
                    P.dma("pool", wuqs[:, kc, :, 0:32], wuq4[:, kc, :, 160:192], writes=[rwq])
                    P.dma("pool", wuqs[:, kc, :, 32:64], wuq4[:, kc, :, 128:160], writes=[rwq])
                P.dma("pool", wukv[:], w_ukv.rearrange("(kc p) n -> p kc n", p=128), writes=[rwq])
                sq = sbt(st, [128, 8, 512], BF16); rsq = Res()
                rstd = sbt(st, [128, 512], F32); rrstd = Res()
                psr = Ring([pst(st, [128, 512]) for _ in range(4)])
                pacc = Ring([pst(st, [128, 512]) for _ in range(4)])
                for tg in range(8):
                    tsl = slice(tg * 512, (tg + 1) * 512)
                    for (src, rsrc, nkc, dim, gsb) in ((qn, rqn, 6, 768, gq_sb), (kvn, rkvn, 2, 256, gkv_sb)):
                        pss, rpss = psr.next()
                        fm_stats(src[:, :, tsl], rsrc, nkc, 512, dim, sq, rsq, pss, rpss, rstd, rrstd)
                        for kc in range(nkc):
                            P.op("dve", lambda g, src=src, kc=kc, gsb=gsb, tsl=tsl: g.scalar_tensor_tensor(
                                out=src[:, kc, tsl], in0=src[:, kc, tsl], scalar=gsb[:, kc:kc + 1], in1=rstd[:],
                                op0=ALU.mult, op1=ALU.mult), reads=[rsrc, rrstd, rgg], writes=[rsrc])
                QN = sbt(st, [128, S], BF16); rQN = Res()
                QR = sbt(st, [64, S], BF16); rQR = Res()
                KN = sbt(st, [128, S], BF16); rKN = Res()
                Vh = sbt(st, [128, 32, 128], BF16); rV = Res()
                OH = sbt(st, [128, S], BF16); rOH = Res()
                t1r = Ring([sbt(st, [64, 512], F32) for _ in range(2)])
                t2r = Ring([sbt(st, [64, 512], F32) for _ in range(2)])
                ptr = Ring([sbt(st, [128, 512], BF16) for _ in range(4)])
                rden_sb = sbt(st, [128, 512], F32); rrd = Res()
                for h in range(8):
                    for tg in range(8):
                        tsl = slice(tg * 512, (tg + 1) * 512)
                        ps, rps = psr.next()
                        P.op("pe", [mm(ps[:], wuq[:, kc, h * 192:h * 192 + 128], qn[:, kc, tsl], kc == 0, kc == 5) for kc in range(6)],
                             reads=[rwq, rqn], writes=[rps])
                        P.op("act", lambda g, ps=ps, tsl=tsl: g.activation(QN[:, tsl], ps[:], AF.Copy, scale=QSCALE), reads=[rps], writes=[rQN])
                        pa, rpa = psr.next(); pb, rpb = psr.next()
                        P.op("pe", [mm(pa[0:64, :], wuq[:, kc, h * 192 + 128:h * 192 + 192], qn[:, kc, tsl], kc == 0, kc == 5) for kc in range(6)],
                             reads=[rwq, rqn], writes=[rpa])
                        P.op("pe", [mm(pb[0:64, :], wuqs[:, kc, h, :], qn[:, kc, tsl], kc == 0, kc == 5) for kc in range(6)],
                             reads=[rwq, rqn], writes=[rpb])
                        t1, rt1 = t1r.next(); t2, rt2 = t2r.next()
                        P.op("dve", lambda g, t1=t1, pa=pa, tsl=tsl: g.scalar_tensor_tensor(out=t1[:], in0=pa[0:64, :], scalar=QSCALE, in1=cos_sb[:, tsl], op0=ALU.mult, op1=ALU.mult),
                             reads=[rpa, r_cs], writes=[rt1])
                        P.op("dve", lambda g, t2=t2, pb=pb, tsl=tsl: g.scalar_tensor_tensor(out=t2[:], in0=pb[0:64, :], scalar=QSCALE, in1=sin_sb[:, tsl], op0=ALU.mult, op1=ALU.mult),
                             reads=[rpb, r_cs], writes=[rt2])
                        P.op("dve", lambda g, t1=t1, t2=t2, tsl=tsl: g.tensor_tensor(out=QR[:, tsl], in0=t1[:], in1=t2[:], op=ALU.add),
                             reads=[rt1, rt2], writes=[rQR])
                        pk, rpk = psr.next()
                        P.op("pe", [mm(pk[:], wukv[:, kc, h * 256:h * 256 + 128], kvn[:, kc, tsl], kc == 0, kc == 1) for kc in range(2)],
                             reads=[rwq, rkvn], writes=[rpk])
                        P.op("act", lambda g, pk=pk, tsl=tsl: g.copy(KN[:, tsl], pk[:]), reads=[rpk], writes=[rKN])
                        pv, rpv = psr.next()
                        fns = []
                        for tt in range(4):
                            for kc in range(2):
                                fns.append(mm(pv[:, tt * 128:(tt + 1) * 128], kvn[:, kc, (tg * 4 + tt) * 128:(tg * 4 + tt + 1) * 128],
                                              wukv[:, kc, h * 256 + 128:h * 256 + 256], kc == 0, kc == 1))
                        P.op("pe", fns, reads=[rwq, rkvn], writes=[rpv])
                        P.op("dve", lambda g, pv=pv, tg=tg: g.tensor_copy(Vh[:, tg * 4:(tg + 1) * 4, :], pv[:].rearrange("p (a b) -> p a b", a=4)),
                             reads=[rpv], writes=[rV])
                    for gq_ in range(8):
                        q0 = gq_ * 512
                        nj = 4 * gq_ + 4
                        pden, rpden = pacc.next(); po, rpo = pacc.next()
                        pendq = []

                        def emit_acc(j, c0, pt, rpt):
                            P.op("pe", [mm(pden[:, c0:512], ones_b[:], pt[:, c0:512], j == 0, j == nj - 1),
                                        mm(po[:, c0:512], Vh[:, j, :], pt[:, c0:512], j == 0, j == nj - 1)],
                                 reads=[rpt, rV, r_const], writes=[rpden, rpo])
                        for j in range(nj):
                            c0 = max(0, (j - 4 * gq_) * 128)
                            ps, rps = psr.next()
                            P.op("pe", [mm(ps[:, c0:512], KN[:, j * 128:(j + 1) * 128], QN[:, q0 + c0:q0 + 512], True, False),
                                        mm(ps[:, c0:512], KR[:, j * 128:(j + 1) * 128], QR[:, q0 + c0:q0 + 512], False, True)],
                                 reads=[rKN, rQN, rKR, rQR], writes=[rps])
                            pt, rpt = ptr.next()
                            P.op("act", lambda g, pt=pt, ps=ps, c0=c0: g.activation(pt[:, c0:512], ps[:, c0:512], AF.Exp), reads=[rps], writes=[rpt])
                            if j >= 4 * gq_:
                                P.op("dve", lambda g, pt=pt, c0=c0: g.memset(pt[64:128, c0:c0 + 64], 0.0), reads=[rpt], writes=[rpt])
                            pendq.append((j, c0, pt, rpt))
                            if len(pendq) > 2:
                                emit_acc(*pendq.pop(0))
                        while pendq:
                            emit_acc(*pendq.pop(0))
                        P.op("dve", lambda g, pden=pden: g.reciprocal(rden_sb[:], pden[:]), reads=[rpden], writes=[rrd])
                        P.op("dve", lambda g, po=po, q0=q0: g.tensor_tensor(out=OH[:, q0:q0 + 512], in0=po[:], in1=rden_sb[:], op=ALU.mult),
                             reads=[rpo, rrd], writes=[rOH])
                    P.dma("sp", OMLA[h * 128:(h + 1) * 128, :], OH[:], reads=[rOH], writes=[rOMLA])
                if upto == "C" and dbg:
                    for nm, tl, rr, pp in (("QN_o", QN, rQN, 128), ("QR_o", QR, rQR, 64), ("KN_o", KN, rKN, 128), ("qn_o", qn[:, 0, :], rqn, 128)):
                        o_ = nc.dram_tensor(nm, [pp, S], BF16, kind="ExternalOutput").ap()
                        P.dma("sp", o_, tl[:] if nm != "qn_o" else tl, reads=[rr], writes=[rr])
                    o_ = nc.dram_tensor("V_o", [128, 32 * 128], BF16, kind="ExternalOutput").ap()
                    P.dma("sp", o_, Vh[:].rearrange("p a b -> p (a b)"), reads=[rV], writes=[rV])
            P.barrier()
            if upto == "C":
                P.final_wait("sp", [rOMLA])
                bc.close()
                P.emit(nc, sems)
                return nc

            bc.close()
            early = gdn_phase()
            P.barrier()
            if upto == "D" or early:
                P.final_wait("sp", [rOGDN])
                P.emit(nc, sems)
                return nc

            mid.close()
            fw = ExitStack()
            pre2, issue2 = load_gu(1, fw, defer=True)
            with ExitStack() as st:
                wom = sbt(st, [128, 8, D], BF16); wog = sbt(st, [128, 8, D], BF16); wo = sbt(st, [128, 8, D], BF16); rwe = Res()
                for wt_, src_ in ((wom, w_o_mla), (wog, w_o_gdn), (wo, w_out)):
                    sv = src_.rearrange("(kc p) n -> p kc n", p=128)
                    for kc in range(0, 8, 2):
                        P.dma("pool", wt_[:, kc:kc + 2, :], sv[:, kc:kc + 2, :], writes=[rwe])
                issue2()
                om = sbt(st, [128, 8, 512], BF16); og = sbt(st, [128, 8, 512], BF16); gt_ = sbt(st, [128, 16, 512], BF16)
                x1t = sbt(st, [128, 8, 512], F32); zt = sbt(st, [128, 8, 512], BF16)
                rom = Res(); rog = Res(); rgt = Res(); rx1 = Res(); rzt = Res()
                ta = Ring([sbt(st, [128, 512], F32) for _ in range(2)])
                tb = Ring([sbt(st, [128, 512], F32) for _ in range(2)])
                psr = Ring([pst(st, [128, 512]) for _ in range(6)])
                for tg in range(8):
                    tsl = slice(tg * 512, (tg + 1) * 512)
                    P.dma("sp", om[:], OMLA.rearrange("(kc p) t -> p kc t", p=128)[:, :, tsl], reads=[rOMLA], writes=[rom])
                    P.dma("sp", og[:], OGDN.rearrange("(kc p) t -> p kc t", p=128)[:, :, tsl], reads=[rOGDN], writes=[rog])
                    P.dma("sp", gt_[:], GATES.rearrange("(kc p) t -> p kc t", p=128)[:, :, tsl], reads=[rGATES], writes=[rgt])
                    P.dma("sp", x1t[:], X1T.rearrange("(kc p) t -> p kc t", p=128)[:, :, tsl], reads=[rX1T[tg]], writes=[rx1])
                    for n in range(8):
                        pm, rpm = psr.next(); pg_, rpg_ = psr.next()
                        P.op("pe", [mm(pm[:], wom[:, kc, n * 128:(n + 1) * 128], om[:, kc, :], kc == 0, kc == 7) for kc in range(8)],
                             reads=[rwe, rom], writes=[rpm])
                        P.op("pe", [mm(pg_[:], wog[:, kc, n * 128:(n + 1) * 128], og[:, kc, :], kc == 0, kc == 7) for kc in range(8)],
                             reads=[rwe, rog], writes=[rpg_])
                        a_, ra_ = ta.next(); b_, rb_ = tb.next()
                        P.op("dve", lambda g, a_=a_, pm=pm, n=n: g.tensor_tensor(out=a_[:], in0=pm[:], in1=gt_[:, n, :], op=ALU.mult),
                             reads=[rpm, rgt], writes=[ra_])
                        P.op("dve", lambda g, b_=b_, pg_=pg_, n=n: g.tensor_tensor(out=b_[:], in0=pg_[:], in1=gt_[:, 8 + n, :], op=ALU.mult),
                             reads=[rpg_, rgt], writes=[rb_])
                        P.op("dve", lambda g, a_=a_, b_=b_, n=n: g.tensor_tensor(out=zt[:, n, :], in0=a_[:], in1=b_[:], op=ALU.add),
                             reads=[ra_, rb_], writes=[rzt])
                    for n in range(8):
                        px, rpx = psr.next()
                        P.op("pe", [mm(px[:], wo[:, kc, n * 128:(n + 1) * 128], zt[:, kc, :], kc == 0, kc == 7) for kc in range(8)],
                             reads=[rwe, rzt], writes=[rpx])
                        P.op("dve", lambda g, px=px, n=n: g.scalar_tensor_tensor(out=x1t[:, n, :], in0=px[:], scalar=PG[:, 1, n:n + 1], in1=x1t[:, n, :],
                                                                        op0=ALU.mult, op1=ALU.add), reads=[rpx, r_ada, rx1], writes=[rx1])
                    P.dma("sp", X2T.rearrange("(kc p) t -> p kc t", p=128)[:, :, tsl], x1t[:], reads=[rx1], writes=[rX2T[tg]])
            P.barrier()
        P.barrier()
        if upto == "E":
            P.final_wait("sp", rX2T)
            fw.close()
            P.emit(nc, sems)
            return nc
        ffn_phase(1, pre2)
        P.final_wait("sp", [r_y])
        fw.close()
        P.emit(nc, sems)
    return nc


_CACHE = {}


def _consts():
    inv_freq = 10000.0 ** (-np.arange(0, 64, 2, dtype=np.float64) / 64.0)
    pos = np.arange(S, dtype=np.float64)
    ang = pos[:, None] * inv_freq[None, :]
    cos = np.cos(ang).astype(np.float32).T
    sin = np.sin(ang).astype(np.float32).T
    cosT = np.concatenate([cos, cos], 0)
    sinS = np.concatenate([-sin, sin], 0)
    ident = np.eye(128, dtype=np.float32)
    t = np.arange(64)
    tri = (t[:, None] <= t[None, :]).astype(np.float32)
    eye = np.eye(64, dtype=np.float32)
    negm = np.where(t[None, :] >= t[:, None], 0.0, NEG).astype(np.float32)
    return dict(cosT=np.ascontiguousarray(cosT), sinS=np.ascontiguousarray(sinS), c_ident=ident, c_tri=tri,
                c_eye=eye, c_negm=negm, c_eyeb=np.ascontiguousarray(np.broadcast_to(eye.reshape(1, 4096), (128, 4096))))


def _fm(v, nk):
    return np.ascontiguousarray(np.asarray(v, np.float32).reshape(nk, 128).T)


def make_in_maps(inp):
    cst = _consts()
    shared = dict(cst)
    shared["w_ada"] = np.ascontiguousarray(inp["w_ada"][0])
    shared["bada_fm"] = _fm(inp["b_ada"][0], 72)
    shared["gains_fm"] = np.ascontiguousarray(np.stack(
        [_fm(inp["g_ffn1"][0], 8), _fm(inp["g_mix"][0], 8), _fm(inp["g_ffn2"][0], 8), _fm(inp["g_final"], 8)], 1))
    for k in ["w1_gate", "w1_up", "w1_down", "w2_gate", "w2_up", "w2_down", "w_in", "w_uq", "w_ukv",
              "w_o_mla", "w_o_gdn", "w_out"]:
        shared[k] = np.ascontiguousarray(inp[k][0])
    shared["gq_fm"] = _fm(inp["g_q_lat"][0], 6)
    shared["gkv_fm"] = _fm(inp["g_kv_lat"][0], 2)
    wc = np.asarray(inp["w_conv"][0], np.float32)
    shared["wconv_fm"] = np.ascontiguousarray(wc.reshape(4, 24, 128).transpose(2, 1, 0))
    shared["alog_bc"] = np.ascontiguousarray(np.broadcast_to(np.asarray(inp["a_log"][0], np.float32)[None, :], (64, 8)))
    shared["dtb_bc"] = np.ascontiguousarray(np.broadcast_to(np.asarray(inp["dt_bias"][0], np.float32)[None, :], (64, 8)))
    shared["ggdn_fm"] = _fm(inp["g_gdn_out"][0], 1)
    maps = []
    for b in range(NCORES):
        m = dict(shared)
        m["x"] = np.ascontiguousarray(inp["x"][b])
        m["c_fm"] = _fm(inp["c"][b], 8)
        maps.append(m)
    return maps


def kernel(**inputs):
    if "nc" not in _CACHE:
        _CACHE["nc"] = build()
    nc = _CACHE["nc"]
    in_maps = make_in_maps(inputs)
    res = run_bass_kernel_spmd(nc, in_maps, core_ids=list(range(NCORES)))
    return np.stack([np.asarray(res.results[b]["y"], np.float32) for b in range(4)], 0)
```

```python
import numpy as np
from contextlib import ExitStack
import concourse.bass as bass
import concourse.mybir as mybir
from concourse.bass_utils import run_bass_kernel_spmd

F32 = mybir.dt.float32
BF16 = mybir.dt.bfloat16
AF = mybir.ActivationFunctionType
ALU = mybir.AluOpType
AX = mybir.AxisListType

S = 4096
D = 1024
DFF = 2816
NFC = 22
EPS = 1e-6
NCORES = 4
TF = 512
NEG = -30000.0
import os as _os
D2MUL = _os.environ.get('D2MUL', 'pool')
QSCALE = 192.0 ** -0.5


class Res:
    __slots__ = ("w", "r")

    def __init__(self):
        self.w = None
        self.r = {}


class Eng:
    def __init__(self, name, idx):
        self.name = name
        self.idx = idx
        self.ops = []
        self.count = 0
        self.waited = {}


class Prog:
    NDMA = 48

    def __init__(self):
        self.eng = {n: Eng(n, i) for i, n in enumerate(["pe", "act", "dve", "pool", "sp"])}
        self.dma_uses = [0] * self.NDMA
        self.dma_next = 0
        self.dma_next_sw = 0
        self.nsem = 5 + self.NDMA

    def _deps(self, reads, writes):
        deps = {}

        def add(s, v):
            if deps.get(s, 0) < v:
                deps[s] = v
        for r in reads:
            if r.w is not None:
                add(*r.w)
        for w in writes:
            if w.w is not None:
                add(*w.w)
            for s, v in w.r.items():
                add(s, v)
        return deps

    def _waits(self, e, deps):
        waits = []
        for s, v in deps.items():
            if e.name == "pe" and s == e.idx:
                continue
            if e.waited.get(s, 0) < v:
                e.waited[s] = v
                waits.append((s, v))
        return waits

    def _mark(self, tok, reads, writes):
        for r in reads:
            if r.r.get(tok[0], 0) < tok[1]:
                r.r[tok[0]] = tok[1]
        for w in writes:
            w.w = tok
            w.r = {}

    def op(self, en, fns, reads=(), writes=()):
        e = self.eng[en]
        if not isinstance(fns, (list, tuple)):
            fns = [fns]
        waits = self._waits(e, self._deps(reads, writes))
        e.count += 1
        tok = (e.idx, e.count)
        e.ops.append((waits, list(fns), (e.idx, 1)))
        self._mark(tok, reads, writes)

    def dma(self, en, out, in_, reads=(), writes=()):
        e = self.eng[en]
        if en == "pool":
            slot = 32 + self.dma_next_sw
            self.dma_next_sw = (self.dma_next_sw + 1) % 16
        else:
            slot = self.dma_next
            self.dma_next = (self.dma_next + 1) % 32
        s = 5 + slot
        deps = self._deps(reads, writes)
        prev = self.dma_uses[slot] * 16
        if prev and deps.get(s, 0) < prev:
            deps[s] = prev
        waits = self._waits(e, deps)
        self.dma_uses[slot] += 1
        tok = (s, self.dma_uses[slot] * 16)
        e.ops.append((waits, [lambda g, out=out, in_=in_: g.dma_start(out=out, in_=in_)], (s, 16)))
        self._mark(tok, reads, writes)

    def barrier(self):
        deps = {e.idx: e.count for e in self.eng.values() if e.count}
        for slot, u in enumerate(self.dma_uses):
            if u:
                deps[5 + slot] = u * 16
        for e in self.eng.values():
            w = self._waits(e, dict(deps))
            if w:
                e.ops.append((w, [], None))

    def final_wait(self, en, resources):
        e = self.eng[en]
        deps = self._deps([], resources)
        e.ops.append((self._waits(e, deps), [], None))

    def emit(self, nc, sems):
        with nc.Block() as block:
            def run(e, g):
                for waits, fns, inc in e.ops:
                    for s, v in waits:
                        g.wait_ge(sems[s], v)
                    ins = None
                    for f in fns:
                        ins = f(g)
                    if inc is not None and ins is not None:
                        ins.then_inc(sems[inc[0]], inc[1])

            @block.tensor
            def _(g):
                run(self.eng["pe"], g)

            @block.scalar
            def _(g):
                run(self.eng["act"], g)

            @block.vector
            def _(g):
                run(self.eng["dve"], g)

            @block.gpsimd
            def _(g):
                run(self.eng["pool"], g)

            @block.sync
            def _(g):
                run(self.eng["sp"], g)


class Ring:
    def __init__(self, tiles):
        self.t = [(t, Res()) for t in tiles]
        self.i = 0

    def next(self):
        r = self.t[self.i % len(self.t)]
        self.i += 1
        return r


def mm(out, lhsT, rhs, start, stop):
    return lambda g: g.matmul(out, lhsT, rhs, start=start, stop=stop)


def build(upto="F", dbg=False):
    nc = bass.Bass("TRN2", target_bir_lowering=False)
    P = Prog()

    def din(name, shape, dt=F32):
        return nc.dram_tensor(name, list(shape), dt, kind="ExternalInput").ap()

    def dscr(name, shape, dt):
        return nc.dram_tensor(name, list(shape), dt, kind=("ExternalOutput" if (dbg and name in dbg) else "Internal")).ap()

    x_in = din("x", [S, D])
    c_fm = din("c_fm", [128, 8])
    w_ada = din("w_ada", [D, 9 * D])
    bada_fm = din("bada_fm", [128, 72])
    gains_fm = din("gains_fm", [128, 4, 8])
    w_gate = [din("w1_gate", [D, DFF]), din("w2_gate", [D, DFF])]
    w_up = [din("w1_up", [D, DFF]), din("w2_up", [D, DFF])]
    w_down = [din("w1_down", [DFF, D]), din("w2_down", [DFF, D])]
    w_in = din("w_in", [D, 7248])
    gq_fm = din("gq_fm", [128, 6])
    gkv_fm = din("gkv_fm", [128, 2])
    w_uq = din("w_uq", [768, 1536])
    w_ukv = din("w_ukv", [256, 2048])
    wconv_fm = din("wconv_fm", [128, 24, 4])
    alog_bc = din("alog_bc", [64, 8])
    dtb_bc = din("dtb_bc", [64, 8])
    ggdn_fm = din("ggdn_fm", [128, 1])
    w_o_mla = din("w_o_mla", [D, D])
    w_o_gdn = din("w_o_gdn", [D, D])
    w_out = din("w_out", [D, D])
    cosT = din("cosT", [64, 4096])
    sinS = din("sinS", [64, 4096])
    c_ident = din("c_ident", [128, 128])
    c_tri = din("c_tri", [64, 64])
    c_eye = din("c_eye", [64, 64])
    c_negm = din("c_negm", [64, 64])
    y_out = nc.dram_tensor("y", [S, D], F32, kind="ExternalOutput").ap()

    X1T = dscr("X1T", [D, S], F32); rX1T = [Res() for _ in range(8)]
    H2T = dscr("H2T", [D, S], BF16); rH2T = [Res() for _ in range(8)]
    QLAT = dscr("QLAT", [768, S], BF16); rQLAT = Res()
    KVLAT = dscr("KVLAT", [256, S], BF16); rKVLAT = Res()
    GQKV = dscr("GQKV", [3072, S], BF16); rGQKV = [Res() for _ in range(24)]
    ZT = dscr("ZT", [1024, S], BF16); rZT = Res()
    GATES = dscr("GATES", [2048, S], BF16); rGATES = Res()
    OMLA = dscr("OMLA", [1024, S], BF16); rOMLA = Res()
    OGDN = dscr("OGDN", [1024, S], BF16); rOGDN = Res()
    X2T = dscr("X2T", [D, S], F32); rX2T = [Res() for _ in range(8)]

    with ExitStack() as top:
        cnt = [0]

        def sbt(st, shape, dt):
            cnt[0] += 1
            return st.enter_context(nc.sbuf_tensor(f"t{cnt[0]}", list(shape), dt))

        def pst(st, shape, dt=F32):
            cnt[0] += 1
            return st.enter_context(nc.psum_tensor(f"p{cnt[0]}", list(shape), dt))

        sems = [top.enter_context(nc.semaphore(f"s{i}")) for i in range(P.nsem)]

        ident_f = sbt(top, [128, 128], F32); r_const = Res()
        ident_b = sbt(top, [128, 128], BF16)
        ones_b = sbt(top, [128, 128], BF16)
        P.dma("sp", ident_f[:], c_ident, writes=[r_const])
        P.op("dve", lambda g: g.tensor_copy(ident_b[:], ident_f[:]), reads=[r_const], writes=[r_const])
        P.op("dve", lambda g: g.memset(ones_b[:], 1.0), writes=[r_const])
        gains = sbt(top, [128, 4, 8], F32)
        P.dma("sp", gains[:], gains_fm, writes=[r_const])
        ada = sbt(top, [128, 72], F32); r_ada = Res()
        PA = sbt(top, [128, 3, 8], F32)
        PG = sbt(top, [128, 3, 8], F32)

        with ExitStack() as st:
            c_sb = sbt(st, [128, 8], F32); rc = Res()
            csil = sbt(st, [128, 8], BF16)
            bada = sbt(st, [128, 72], F32); rb = Res()
            P.dma("sp", c_sb[:], c_fm, writes=[rc])
            P.dma("sp", bada[:], bada_fm, writes=[rb])
            P.op("act", lambda g: g.activation(csil[:], c_sb[:], AF.Silu), reads=[rc], writes=[rc])
            adaps = pst(st, [128, 72]); rps = Res()
            wring = Ring([sbt(st, [128, 8, 1024], BF16) for _ in range(2)])
            wv = w_ada.rearrange("(kc p) n -> p kc n", p=128)
            for j in range(9):
                wb, rw = wring.next()
                P.dma("pool", wb[:], wv[:, :, j * 1024:(j + 1) * 1024], writes=[rw])
                fns = []
                for cc in range(8):
                    for kc in range(8):
                        fns.append(mm(adaps[:, j * 8 + cc:j * 8 + cc + 1], wb[:, kc, cc * 128:(cc + 1) * 128],
                                      csil[:, kc:kc + 1], kc == 0, kc == 7))
                P.op("pe", fns, reads=[rw, rc], writes=[rps])
            P.op("dve", lambda g: g.tensor_tensor(out=ada[:], in0=adaps[:], in1=bada[:], op=ALU.add),
                 reads=[rps, rb], writes=[r_ada])
            for j in range(3):
                P.op("dve", lambda g, j=j: g.scalar_tensor_tensor(
                    out=PA[:, j, :], in0=ada[:, (3 * j + 1) * 8:(3 * j + 2) * 8], scalar=1.0,
                    in1=gains[:, j, :], op0=ALU.add, op1=ALU.mult), reads=[r_ada, r_const], writes=[r_ada])
                P.op("dve", lambda g, j=j: g.tensor_scalar(
                    out=PG[:, j, :], in0=ada[:, (3 * j + 2) * 8:(3 * j + 3) * 8], scalar1=(1.0 if j == 1 else 0.5),
                    scalar2=None, op0=ALU.mult), reads=[r_ada], writes=[r_ada])

        P.barrier()
        if upto == "0":
            ado = nc.dram_tensor("ada_o", [128, 72], F32, kind="ExternalOutput").ap()
            P.dma("sp", ado, ada[:], reads=[r_ada], writes=[r_ada])
            P.final_wait("sp", [r_ada])
            P.emit(nc, sems)
            return nc

        def Bcol(j, kc):
            return ada[:, 3 * j * 8 + kc:3 * j * 8 + kc + 1]

        def fm_stats(src3, rsrc, nkc, T, dim, sq, rsq, pss, rpss, rstd, rrstd):
            P.op("act", lambda g: g.activation(sq[:, 0:nkc, 0:T], src3, AF.Square), reads=[rsrc], writes=[rsq])
            P.op("pe", [mm(pss[:, 0:T], ones_b[:], sq[:, kc, 0:T], kc == 0, kc == nkc - 1) for kc in range(nkc)],
                 reads=[rsq, r_const], writes=[rpss])
            P.op("act", lambda g: g.activation(rstd[:, 0:T], pss[:, 0:T], AF.Sqrt, scale=1.0 / dim, bias=EPS),
                 reads=[rpss], writes=[rrstd])
            P.op("dve", lambda g: g.reciprocal(rstd[:, 0:T], rstd[:, 0:T]), reads=[rrstd], writes=[rrstd])

        def modulate(j, xT, rxT, hT, rhT, T, sq, rsq, pss, rpss, rstd, rrstd, tmp_ring):
            fm_stats(xT[:, :, 0:T], rxT, 8, T, D, sq, rsq, pss, rpss, rstd, rrstd)
            for kc in range(8):
                tmp, rt = tmp_ring.next()
                P.op("dve", lambda g, kc=kc, tmp=tmp: g.scalar_tensor_tensor(
                    out=tmp[:, 0:T], in0=xT[:, kc, 0:T], scalar=PA[:, j, kc:kc + 1], in1=rstd[:, 0:T],
                    op0=ALU.mult, op1=ALU.mult), reads=[rxT, rrstd, r_ada], writes=[rt])
                P.op("act", lambda g, kc=kc, tmp=tmp: g.activation(
                    hT[:, kc, 0:T], tmp[:, 0:T], AF.Identity, bias=Bcol(j, kc), scale=1.0),
                    reads=[rt, r_ada], writes=[rhT])

        fsplit = [(0, 6), (6, 12), (12, 17), (17, 22)]

        def load_gu(which, st, defer=False):
            wg = sbt(st, [128, 8, DFF], BF16)
            wu = sbt(st, [128, 8, DFF], BF16)
            rwg = [Res() for _ in fsplit]; rwu = [Res() for _ in fsplit]
            gv_ = w_gate[which].rearrange("(kc p) f -> p kc f", p=128)
            uv_ = w_up[which].rearrange("(kc p) f -> p kc f", p=128)
            def issue():
                for qi, (a, b) in enumerate(fsplit):
                    P.dma("pool", wg[:, :, a * 128:b * 128], gv_[:, :, a * 128:b * 128], writes=[rwg[qi]])
                    P.dma("pool", wu[:, :, a * 128:b * 128], uv_[:, :, a * 128:b * 128], writes=[rwu[qi]])
            if not defer:
                issue()
                return wg, wu, rwg, rwu
            return (wg, wu, rwg, rwu), issue

        def ffn_phase(which, pre=None):
            with ExitStack() as st:
                if pre is None:
                    pre = load_gu(which, st)
                wg, wu, rwg, rwu = pre
                wd = sbt(st, [128, NFC, D], BF16)
                rwd = [Res() for _ in range(NFC)]
                dv_ = w_down[which].rearrange("(fc p) n -> p fc n", p=128)
                import os
                for fc in range(0, NFC if not os.environ.get('SKIPW') else 0, 2):
                    P.dma("pool", wd[:, fc:fc + 2, :], dv_[:, fc:fc + 2, :], writes=[rwd[fc], rwd[fc + 1]])

                def q_of(f):
                    for qi, (a, b) in enumerate(fsplit):
                        if a <= f < b:
                            return qi
                xT = sbt(st, [128, 8, TF], F32); rxT = Res()
                hT = sbt(st, [128, 8, TF], BF16); rhT = Res()
                actT = sbt(st, [128, NFC, TF], BF16); ract = [Res() for _ in range(NFC)]
                sq = sbt(st, [128, 8, TF], BF16); rsq = Res()
                rstd = sbt(st, [128, TF], F32); rrstd = Res()
                tmp_ring = Ring([sbt(st, [128, TF], F32) for _ in range(2)])
                xtm_ring = Ring([sbt(st, [128, D], F32) for _ in range(2)])
                ps_ring = Ring([pst(st, [128, 512]) for _ in range(6)])
                pss = pst(st, [128, 512]); rpss = Res()
                jn = 0 if which == 0 else 2
                import os
                NTG = int(os.environ.get('NTG', S // TF)); LVL = int(os.environ.get('LVL', 9))
                for tg in range(NTG):
                    t0 = tg * TF
                    if which == 0:
                        for tt in range(TF // 128):
                            xtm, rxtm = xtm_ring.next()
                            P.dma("sp", xtm[:], x_in[t0 + tt * 128:t0 + (tt + 1) * 128, :], writes=[rxtm])
                            for half in range(2):
                                pt, rpt = ps_ring.next()
                                P.op("pe", [lambda g, kc=kc, pt=pt, xtm=xtm, half=half: g.transpose(
                                    pt[:, (kc - 4 * half) * 128:(kc - 4 * half + 1) * 128], xtm[:, kc * 128:(kc + 1) * 128], ident_f[:])
                                    for kc in range(4 * half, 4 * half + 4)], reads=[rxtm, r_const], writes=[rpt])
                                dst = xT[:, 4 * half:4 * half + 4, tt * 128:(tt + 1) * 128]
                                src = pt[:].rearrange("p (k t) -> p k t", k=4)
                                if half:
                                    P.op("act", lambda g, dst=dst, src=src: g.copy(dst, src), reads=[rpt], writes=[rxT])
                                else:
                                    P.op("dve", lambda g, dst=dst, src=src: g.tensor_copy(dst, src), reads=[rpt], writes=[rxT])
                    else:
                        P.dma("sp", xT[:], X2T.rearrange("(kc p) t -> p kc t", p=128)[:, :, t0:t0 + TF],
                              reads=[rX2T[tg]], writes=[rxT])
                    if LVL < 1:
                        P.dma('sp', X1T.rearrange('(kc p) t -> p kc t', p=128)[:, :, t0:t0 + TF], xT[:], reads=[rxT], writes=[rX1T[tg]])
                        continue
                    modulate(jn, xT, rxT, hT, rhT, TF, sq, rsq, pss, rpss, rstd, rrstd, tmp_ring)
                    if LVL < 2:
                        P.dma('sp', H2T.rearrange('(kc p) t -> p kc t', p=128)[:, :, t0:t0 + TF], hT[:], reads=[rhT], writes=[rH2T[tg]])
                        continue
                    for f in range(NFC):
                        pg, rpg = ps_ring.next()
                        pu, rpu = ps_ring.next()
                        P.op("pe", [mm(pg[:, 0:TF], wg[:, kc, f * 128:(f + 1) * 128], hT[:, kc, :], kc == 0, kc == 7)
                                    for kc in range(8)], reads=[rhT, rwg[q_of(f)]], writes=[rpg])
                        P.op("pe", [mm(pu[:, 0:TF], wu[:, kc, f * 128:(f + 1) * 128], hT[:, kc, :], kc == 0, kc == 7)
                                    for kc in range(8)], reads=[rhT, rwu[q_of(f)]], writes=[rpu])
                        tmp, rt = tmp_ring.next()
                        P.op("act", lambda g, pg=pg, tmp=tmp: g.activation(tmp[:, 0:TF], pg[:, 0:TF], AF.Silu),
                             reads=[rpg], writes=[rt])
                        wr = [ract[f]]
                        P.op("dve", lambda g, pu=pu, tmp=tmp, f=f: g.tensor_tensor(
                            out=actT[:, f, :], in0=tmp[:, 0:TF], in1=pu[:, 0:TF], op=ALU.mult),
                            reads=[rt, rpu], writes=wr)
                    for n in range(8):
                        pd, rpd = ps_ring.next()
                        P.op("pe", [mm(pd[:, 0:TF], wd[:, f, n * 128:(n + 1) * 128], actT[:, f, :], f == 0, f == NFC - 1)
                                    for f in range(NFC)], reads=ract + rwd, writes=[rpd])
                        P.op("dve", lambda g, pd=pd, n=n: g.scalar_tensor_tensor(
                            out=xT[:, n, :], in0=pd[:, 0:TF], scalar=PG[:, jn, n:n + 1], in1=xT[:, n, :],
                            op0=ALU.mult, op1=ALU.add), reads=[rpd, r_ada, rxT], writes=[rxT])
                    if which == 0:
                        P.dma("sp", X1T.rearrange("(kc p) t -> p kc t", p=128)[:, :, t0:t0 + TF], xT[:],
                              reads=[rxT], writes=[rX1T[tg]])
                        modulate(1, xT, rxT, hT, rhT, TF, sq, rsq, pss, rpss, rstd, rrstd, tmp_ring)
                        P.dma("sp", H2T.rearrange("(kc p) t -> p kc t", p=128)[:, :, t0:t0 + TF], hT[:],
                              reads=[rhT], writes=[rH2T[tg]])
                    else:
                        fm_stats(xT[:, :, :], rxT, 8, TF, D, sq, rsq, pss, rpss, rstd, rrstd)
                        for kc in range(8):
                            P.op("dve", lambda g, kc=kc: g.scalar_tensor_tensor(
                                out=xT[:, kc, :], in0=xT[:, kc, :], scalar=gains[:, 3, kc:kc + 1], in1=rstd[:, :],
                                op0=ALU.mult, op1=ALU.mult), reads=[rxT, rrstd, r_const], writes=[rxT])
                        for tt in range(TF // 128):
                            xtm, rxtm = xtm_ring.next()
                            for half in range(2):
                                pt, rpt = ps_ring.next()
                                P.op("pe", [lambda g, kc=kc, pt=pt, half=half, tt=tt: g.transpose(
                                    pt[:, (kc - 4 * half) * 128:(kc - 4 * half + 1) * 128],
                                    xT[:, kc, tt * 128:(tt + 1) * 128], ident_f[:])
                                    for kc in range(4 * half, 4 * half + 4)], reads=[rxT, r_const], writes=[rpt])
                                if half:
                                    P.op("act", lambda g, pt=pt, xtm=xtm: g.copy(xtm[:, 512:1024], pt[:]),
                                         reads=[rpt], writes=[rxtm])
                                else:
                                    P.op("dve", lambda g, pt=pt, xtm=xtm: g.tensor_copy(xtm[:, 0:512], pt[:]),
                                         reads=[rpt], writes=[rxtm])
                            P.dma("sp", y_out[t0 + tt * 128:t0 + (tt + 1) * 128, :], xtm[:], reads=[rxtm], writes=[r_y])

        def rsq_guard(rsq, ract):
            for f in range(8):
                for s_, v_ in ract[f].r.items():
                    if rsq.r.get(s_, 0) < v_:
                        rsq.r[s_] = v_
                if ract[f].w is not None:
                    s_, v_ = ract[f].w
                    if rsq.r.get(s_, 0) < v_:
                        rsq.r[s_] = v_
            return rsq

        r_y = Res()

        ffn_phase(0)
        P.barrier()
        if upto == "A":
            P.final_wait("sp", rX1T + rH2T)
            P.emit(nc, sems)
            return nc

        QDT = dscr("QDT", [8, 128, S], BF16); rQDT = Res()
        AITs = dscr("AITs", [8, 64, S], BF16); rAIT = Res()
        ATs = dscr("ATs", [512, 4096], F32); rATs = [Res() for _ in range(8)]
        TTs = dscr("TTs", [512, 4096], BF16); rTTs = [Res() for _ in range(4)]
        KBDs = dscr("KBDs", [8, 64, 64 * 128], BF16); rKBD = Res()
        KDECs = dscr("KDECs", [8, 64, 64 * 128], BF16); rKDEC = Res()
        VBs = dscr("VBs", [8, 64, 64 * 128], BF16); rVB = Res()
        c_eyeb = din("c_eyeb", [128, 4096])

        def gdn_phase():
            with ExitStack() as gst:
                tri_f = sbt(gst, [64, 64], F32); eye_f = sbt(gst, [64, 64], F32); negm = sbt(gst, [64, 64], F32)
                ones_f = sbt(gst, [64, 128], F32); rcg = Res()
                P.dma("sp", tri_f[:], c_tri, writes=[rcg]); P.dma("sp", eye_f[:], c_eye, writes=[rcg]); P.dma("sp", negm[:], c_negm, writes=[rcg])
                P.op("dve", lambda g: g.memset(ones_f[:], 1.0), writes=[rcg])
                dec_tm = sbt(gst, [64, 64, 8], F32); s_kbd = sbt(gst, [64, 64, 8], F32); s_kdec = sbt(gst, [64, 64, 8], F32)
                s_vb = sbt(gst, [64, 64, 8], F32); cd_bc = sbt(gst, [128, 64, 8], F32); rdc = Res()
                ggdn = sbt(gst, [128, 1], F32); wcv = sbt(gst, [128, 24, 4], F32)
                P.dma("sp", ggdn[:], ggdn_fm, writes=[rcg]); P.dma("sp", wcv[:], wconv_fm, writes=[rcg])
                gflat = g_tm[:].rearrange("p n h -> p (n h)")
                fl = lambda t: t[:].rearrange("p n h -> p (n h)")
                with ExitStack() as st:
                    p1 = pst(st, [64, 512]); p2 = pst(st, [64, 512]); p3 = pst(st, [128, 512]); rp = Res()
                    P.op("pe", [mm(p1[:], tri_f[:], gflat, True, True), mm(p2[:], ones_f[:, 0:64], gflat, True, True),
                                mm(p3[:], ones_f[:, 0:128], gflat, True, True)], reads=[r_gl, rcg], writes=[rp])
                    P.op("dve", lambda g: g.tensor_copy(fl(dec_tm), p1[:]), reads=[rp], writes=[rdc])
                    P.op("dve", lambda g: g.tensor_tensor(out=fl(s_kbd), in0=p1[:], in1=fl(lnb_tm), op=ALU.add), reads=[rp, r_gl], writes=[rdc])
                    P.op("act", lambda g: g.activation(fl(s_kbd), fl(s_kbd), AF.Exp), reads=[rdc], writes=[rdc])
                    P.op("dve", lambda g: g.tensor_tensor(out=fl(s_kdec), in0=p2[:], in1=fl(dec_tm), op=ALU.subtract), reads=[rp, rdc], writes=[rdc])
                    P.op("act", lambda g: g.activation(fl(s_kdec), fl(s_kdec), AF.Exp), reads=[rdc], writes=[rdc])
                    P.op("act", lambda g: g.activation(fl(s_vb), fl(lnb_tm), AF.Exp), reads=[r_gl], writes=[rdc])
                    P.op("act", lambda g: g.activation(fl(cd_bc), p3[:], AF.Exp), reads=[rp], writes=[rdc])
                P.barrier()
                with ExitStack() as st:
                    xin_ring = Ring([sbt(st, [128, S + 4], BF16) for _ in range(2)])
                    y_ring = Ring([sbt(st, [128, S], F32) for _ in range(2)])
                    qT = sbt(st, [128, S], BF16); kT = sbt(st, [128, S], BF16); vT = sbt(st, [128, S], BF16)
                    rqT = Res(); rkT = Res(); rvT = Res()
                    QD = sbt(st, [128, S], BF16); rQD = Res()
                    AIT = sbt(st, [64, 64, 64], BF16); rAITt = Res()
                    KBD = sbt(st, [64, 64, 128], BF16); KDEC = sbt(st, [64, 64, 128], BF16); VB = sbt(st, [64, 64, 128], BF16)
                    rKBDt = Res(); rKDECt = Res(); rVBt = Res()
                    sqr = Ring([sbt(st, [128, 512], BF16) for _ in range(2)])
                    rsr = Ring([sbt(st, [128, 512], F32) for _ in range(2)])
                    gt1r = Ring([sbt(st, [64, 8, 64], F32) for _ in range(2)])
                    gt2r = Ring([sbt(st, [64, 8, 64], F32) for _ in range(2)])
                    tmr = Ring([sbt(st, [64, 8, 64], F32) for _ in range(2)])
                    l1r = Ring([sbt(st, [64, 8, 64], F32) for _ in range(2)])
                    l2r = Ring([sbt(st, [64, 8, 64], F32) for _ in range(2)])
                    edr = Ring([sbt(st, [128, 8, 64], F32) for _ in range(2)])
                    atr = Ring([sbt(st, [64, 8, 64], F32) for _ in range(2)])
                    psr = Ring([pst(st, [128, 512]) for _ in range(4)])
                    ptr_ = Ring([pst(st, [64, 1024], F32) for _ in range(2)])
                    for (xt_, rxt_) in xin_ring.t:
                        P.op("pool", lambda g, xt_=xt_: g.memset(xt_[:, 0:4], 0.0), writes=[rxt_])
                    dg_ring = Ring([sbt(st, [128, 4, 128], BF16) for _ in range(2)])
                    for h in range(8):
                        for which_, dstT, rdst in ((0, qT, rqT), (1, kT, rkT), (2, vT, rvT)):
                            ch = which_ * 8 + h
                            (xA, rxA), (xB, rxB) = xin_ring.t
                            y, ry = y_ring.next()
                            P.dma("sp", xA[:, 4:S + 4], GQKV[ch * 128:(ch + 1) * 128, :], reads=[rGQKV[ch]], writes=[rxA])
                            P.op("act", lambda g, xA=xA, xB=xB: g.copy(xB[:, 3:S + 3], xA[:, 4:S + 4]), reads=[rxA], writes=[rxB])
                            dg, rdg = dg_ring.next()
                            P.op("pool", lambda g, dg=dg, ch=ch: g.tensor_tensor(out=dg[:], in0=ident_b[:].unsqueeze(1).to_broadcast([128, 4, 128]),
                                 in1=wcv[:, ch, :].unsqueeze(2).to_broadcast([128, 4, 128]), op=ALU.mult), reads=[r_const, rcg], writes=[rdg])
                            for tg in range(8):
                                T0 = tg * 512
                                ps, rps = psr.next()
                                P.op("pe", [mm(ps[:], dg[:, 0, :], xB[:, T0:T0 + 512], True, False),
                                            mm(ps[:], dg[:, 1, :], xA[:, T0 + 2:T0 + 514], False, False),
                                            mm(ps[:], dg[:, 2, :], xB[:, T0 + 2:T0 + 514], False, False),
                                            mm(ps[:], dg[:, 3, :], xA[:, T0 + 4:T0 + 516], False, True)],
                                     reads=[rdg, rxA, rxB], writes=[rps])
                                if which_ == 2:
                                    P.op("act", lambda g, ps=ps, T0=T0: g.activation(vT[:, T0:T0 + 512], ps[:], AF.Silu), reads=[rps], writes=[rvT])
                                else:
                                    P.op("act", lambda g, ps=ps, T0=T0, y=y: g.activation(y[:, T0:T0 + 512], ps[:], AF.Silu), reads=[rps], writes=[ry])
                            if which_ != 2:
                                for tg in range(8):
                                    tsl = slice(tg * 512, (tg + 1) * 512)
                                    sq_, rsq_ = sqr.next(); rs_, rrs_ = rsr.next(); ps, rps = psr.next()
                                    P.op("act", lambda g, sq_=sq_, tsl=tsl, y=y: g.activation(sq_[:], y[:, tsl], AF.Square), reads=[ry], writes=[rsq_])
                                    P.op("pe", mm(ps[:], ones_b[:], sq_[:], True, True), reads=[rsq_, r_const], writes=[rps])
                                    P.op("act", lambda g, rs_=rs_, ps=ps: g.activation(rs_[:], ps[:], AF.Sqrt, scale=1.0, bias=EPS), reads=[rps], writes=[rrs_])
                                    P.op("dve", lambda g, rs_=rs_: g.reciprocal(rs_[:], rs_[:]), reads=[rrs_], writes=[rrs_])
                                    P.op("dve", lambda g, rs_=rs_, tsl=tsl, dstT=dstT, y=y, sc=(128.0 ** -0.5 if which_ == 0 else 1.0): g.scalar_tensor_tensor(
                                        out=dstT[:, tsl], in0=y[:, tsl], scalar=sc, in1=rs_[:], op0=ALU.mult, op1=ALU.mult), reads=[ry, rrs_], writes=[rdst])
                        for nb in range(8):
                            nsl = slice(nb * 8, (nb + 1) * 8)
                            tsl = slice(nb * 512, (nb + 1) * 512)
                            gbc = g_tm[:, nsl, h:h + 1].to_broadcast([64, 8, 64])
                            lbc = lnb_tm[:, nsl, h:h + 1].to_broadcast([64, 8, 64])
                            dbc = dec_tm[:, nsl, h:h + 1].to_broadcast([64, 8, 64])
                            gt1, rg1 = gt1r.next(); gt2, rg2 = gt2r.next()
                            P.op("pool", lambda g, gt1=gt1, gbc=gbc: g.tensor_tensor(out=gt1[:], in0=tri_f[:].unsqueeze(1).to_broadcast([64, 8, 64]), in1=gbc, op=ALU.mult),
                                 reads=[r_gl, rcg], writes=[rg1])
                            P.op("pool", lambda g, gt2=gt2, lbc=lbc: g.tensor_tensor(out=gt2[:], in0=eye_f[:].unsqueeze(1).to_broadcast([64, 8, 64]), in1=lbc, op=ALU.mult),
                                 reads=[r_gl, rcg], writes=[rg2])
                            P.op("pool", lambda g, gt1=gt1, gt2=gt2: g.tensor_tensor(out=gt2[:], in0=gt2[:], in1=gt1[:], op=ALU.add), reads=[rg1, rg2], writes=[rg2])
                            pd1, rpd1 = psr.next(); pd2, rpd2 = psr.next()
                            P.op("pe", mm(pd1[:], ones_f[:, 0:128], gt1[:].rearrange("p a b -> p (a b)"), True, True), reads=[rg1, rcg], writes=[rpd1])
                            P.op("pe", mm(pd2[0:64, :], ones_f[:, 0:64], gt2[:].rearrange("p a b -> p (a b)"), True, True), reads=[rg2, rcg], writes=[rpd2])
                            ed, red = edr.next()
                            P.op("act", lambda g, ed=ed, pd1=pd1: g.activation(ed[:].rearrange("p a b -> p (a b)"), pd1[:], AF.Exp), reads=[rpd1], writes=[red])
                            P.op("pool", lambda g, ed=ed, tsl=tsl: g.tensor_tensor(out=QD[:, tsl], in0=qT[:, tsl], in1=ed[:].rearrange("p a b -> p (a b)"), op=ALU.mult),
                                 reads=[rqT, red], writes=[rQD])
                            l1, rl1 = l1r.next(); l2, rl2 = l2r.next()
                            for (pd, rpd, ll, rll) in ((pd1, rpd1, l1, rl1), (pd2, rpd2, l2, rl2)):
                                tm_, rtm_ = tmr.next()
                                P.op("dve", lambda g, tm_=tm_, pd=pd, dbc=dbc: g.tensor_tensor(out=tm_[:], in0=pd[0:64, :].rearrange("p (a b) -> p a b", a=8), in1=dbc, op=ALU.subtract),
                                     reads=[rpd, rdc], writes=[rtm_])
                                P.op("pool", lambda g, tm_=tm_: g.tensor_tensor(out=tm_[:], in0=tm_[:], in1=negm[:].unsqueeze(1).to_broadcast([64, 8, 64]), op=ALU.add),
                                     reads=[rtm_, rcg], writes=[rtm_])
                                P.op("act", lambda g, tm_=tm_, ll=ll: g.activation(ll[:], tm_[:], AF.Exp), reads=[rtm_], writes=[rll])
                            pkq, rpkq = psr.next(); pkk, rpkk = psr.next()
                            P.op("pe", [mm(pkq[0:64, n * 64:(n + 1) * 64], kT[:, nb * 512 + n * 64:nb * 512 + (n + 1) * 64], qT[:, nb * 512 + n * 64:nb * 512 + (n + 1) * 64], True, True)
                                        for n in range(8)], reads=[rkT, rqT], writes=[rpkq])
                            P.op("pe", [mm(pkk[0:64, n * 64:(n + 1) * 64], kT[:, nb * 512 + n * 64:nb * 512 + (n + 1) * 64], kT[:, nb * 512 + n * 64:nb * 512 + (n + 1) * 64], True, True)
                                        for n in range(8)], reads=[rkT], writes=[rpkk])
                            P.op("dve", lambda g, pkq=pkq, l1=l1, nsl=nsl: g.tensor_tensor(out=AIT[:, nsl, :], in0=pkq[0:64, :].rearrange("p (a b) -> p a b", a=8), in1=l1[:], op=ALU.mult),
                                 reads=[rpkq, rl1], writes=[rAITt])
                            at_, rat_ = atr.next()
                            P.op("dve", lambda g, pkk=pkk, l2=l2, at_=at_: g.tensor_tensor(out=at_[:], in0=pkk[0:64, :].rearrange("p (a b) -> p a b", a=8), in1=l2[:], op=ALU.mult),
                                 reads=[rpkk, rl2], writes=[rat_])
                            P.dma("sp", ATs[h * 64 + nb * 8:h * 64 + nb * 8 + 8, :].rearrange("n (j i) -> j n i", j=64), at_[:], reads=[rat_], writes=[rATs[h]])
                            pk_, rpk_ = ptr_.next(); pv_, rpv_ = ptr_.next()
                            P.op("pe", [mm(pk_[:, n * 128:(n + 1) * 128], kT[:, nb * 512 + n * 64:nb * 512 + (n + 1) * 64], ident_b[:], True, True)
                                        for n in range(8)], reads=[rkT, r_const], writes=[rpk_])
                            P.op("pe", [mm(pv_[:, n * 128:(n + 1) * 128], vT[:, nb * 512 + n * 64:nb * 512 + (n + 1) * 64], ident_b[:], True, True)
                                        for n in range(8)], reads=[rvT, r_const], writes=[rpv_])
                            for (pp, rpp, dst_, rdst_, sc_) in ((pk_, rpk_, KBD, rKBDt, s_kbd), (pk_, rpk_, KDEC, rKDECt, s_kdec), (pv_, rpv_, VB, rVBt, s_vb)):
                                P.op("dve", lambda g, pp=pp, dst_=dst_, sc_=sc_, nsl=nsl, h=h: g.tensor_tensor(
                                    out=dst_[:, nsl, :], in0=pp[:].rearrange("p (a b) -> p a b", a=8), in1=sc_[:, nsl, h:h + 1].to_broadcast([64, 8, 128]), op=ALU.mult),
                                    reads=[rpp, rdc], writes=[rdst_])
                        P.dma("sp", QDT[h], QD[:], reads=[rQD], writes=[rQDT])
                        P.dma("sp", AITs[h], AIT[:].rearrange("p a b -> p (a b)"), reads=[rAITt], writes=[rAIT])
                        P.dma("sp", KBDs[h], KBD[:].rearrange("p a b -> p (a b)"), reads=[rKBDt], writes=[rKBD])
                        P.dma("sp", KDECs[h], KDEC[:].rearrange("p a b -> p (a b)"), reads=[rKDECt], writes=[rKDEC])
                        P.dma("sp", VBs[h], VB[:].rearrange("p a b -> p (a b)"), reads=[rVBt], writes=[rVB])
                P.barrier()
                if upto == "D1":
                    return True
                with ExitStack() as st:
                    sets = [(sbt(st, [128, 64, 64], F32), sbt(st, [128, 64, 64], F32), sbt(st, [128, 64, 64], F32), sbt(st, [128, 4096], BF16),
                             Res(), Res(), Res()) for _ in range(2)]
                    for pp_ in range(2):
                        for gi in range(2):
                            pg = pp_ * 2 + gi
                            A_, X_, tmp_, Xb, rA, rX, rT = sets[gi]
                            P.dma("sp", A_[:].rearrange("p a b -> p (a b)"), ATs[pg * 128:(pg + 1) * 128, :], reads=[rATs[2 * pg], rATs[2 * pg + 1]], writes=[rA])
                            P.dma("sp", X_[:].rearrange("p a b -> p (a b)"), c_eyeb, writes=[rX])
                        for j in range(62, -1, -1):
                            m = 63 - j
                            for gi in range(2):
                                A_, X_, tmp_, Xb, rA, rX, rT = sets[gi]
                                P.op("dve" if gi == 0 else D2MUL, lambda g, j=j, m=m, A_=A_, X_=X_, tmp_=tmp_: g.tensor_tensor(
                                    out=tmp_[:, 0:m, 0:m], in0=X_[:, j + 1:64, j + 1:64].rearrange("p i c -> p c i"),
                                    in1=A_[:, j, j + 1:64].unsqueeze(1).to_broadcast([128, m, m]), op=ALU.mult), reads=[rA, rX], writes=[rT])
                            for gi in range(2):
                                A_, X_, tmp_, Xb, rA, rX, rT = sets[gi]
                                P.op("dve", lambda g, j=j, m=m, X_=X_, tmp_=tmp_: g.tensor_reduce(
                                    out=X_[:, j, j + 1:64], in_=tmp_[:, 0:m, 0:m], axis=AX.X, op=ALU.add, negate=True), reads=[rT], writes=[rX])
                        for gi in range(2):
                            pg = pp_ * 2 + gi
                            A_, X_, tmp_, Xb, rA, rX, rT = sets[gi]
                            P.op("act", lambda g, X_=X_, Xb=Xb: g.copy(Xb[:], X_[:].rearrange("p a b -> p (a b)")), reads=[rX], writes=[rX])
                            P.dma("sp", TTs[pg * 128:(pg + 1) * 128, :], Xb[:], reads=[rX], writes=[rTTs[pg]])
                P.barrier()
                if upto == "D2":
                    return True
                with ExitStack() as st:
                    TTb = sbt(st, [64, 8, 8, 64], BF16); KBDb = sbt(st, [64, 8, 8, 128], BF16)
                    VBb = sbt(st, [64, 8, 8, 128], BF16)
                    kd_ring = Ring([sbt(st, [64, 8, 8, 128], BF16) for _ in range(2)])
                    qd_ring = Ring([sbt(st, [128, 8, 512], BF16) for _ in range(2)])
                    ai_ring = Ring([sbt(st, [64, 8, 8, 64], BF16) for _ in range(2)])
                    ZTb = sbt(st, [128, 8, 512], BF16); WTb = sbt(st, [128, 8, 8, 64], BF16); Ub = sbt(st, [64, 8, 8, 128], BF16)
                    Oraw = sbt(st, [128, 8, 512], F32); OG = sbt(st, [128, 8, 512], BF16)
                    Sf = sbt(st, [128, 8, 128], F32); Sb = sbt(st, [128, 8, 128], BF16); vnew = sbt(st, [64, 8, 128], BF16)
                    sq_ = sbt(st, [128, 512], BF16); rs_ = sbt(st, [128, 512], F32); t_ = sbt(st, [128, 512], F32)
                    rTTb = Res(); rKBDb = Res(); rKDECb = Res(); rVBb = Res(); rQDb = Res(); rAITb = Res(); rZTb = Res()
                    rWTb = Res(); rUb = Res(); rOraw = Res(); rOG = Res(); rS = Res(); rSb = Res(); rvn = Res(); rsq_ = Res(); rrs_ = Res(); rt_ = Res()
                    wps = pst(st, [128, 512]); ups = pst(st, [64, 1024]); wsps = pst(st, [64, 1024]); ops_ = pst(st, [128, 512]); dsps = pst(st, [128, 1024])
                    rwps = Res(); rups = Res(); rwsps = Res(); rops = Res(); rdsps = Res()
                    rSh = [Res(), Res()]; rSbh = [Res(), Res()]; rwsh = [Res(), Res()]; rvnh = [Res(), Res()]; _ro = Res(); ropsh = [_ro, _ro]; rdsh = [Res(), Res()]
                    P.op("dve", lambda g: g.memset(Sf[:], 0.0), writes=rSh)
                    P.op("dve", lambda g: g.memset(Sb[:], 0.0), writes=rSbh)
                    for nb in range(8):
                        tsl = slice(nb * 512, (nb + 1) * 512)
                        for h in range(8):
                            P.dma("sp", TTb[:, h, :, :], TTs[h * 64 + nb * 8:h * 64 + nb * 8 + 8, :].rearrange("n (j i) -> j n i", j=64),
                                  reads=[rTTs[h // 2]], writes=[rTTb])
                        P.dma("sp", KBDb[:], KBDs.rearrange("h t (n d) -> t h n d", d=128)[:, :, nb * 8:(nb + 1) * 8, :], reads=[rKBD], writes=[rKBDb])
                        P.dma("sp", VBb[:], VBs.rearrange("h t (n d) -> t h n d", d=128)[:, :, nb * 8:(nb + 1) * 8, :], reads=[rVB], writes=[rVBb])
                        KDECb, rKDECb = kd_ring.next(); QDb, rQDb = qd_ring.next(); AITb, rAITb = ai_ring.next()
                        P.dma("sp", KDECb[:], KDECs.rearrange("h t (n d) -> t h n d", d=128)[:, :, nb * 8:(nb + 1) * 8, :], reads=[rKDEC], writes=[rKDECb])
                        P.dma("sp", QDb[:], QDT.rearrange("h d t -> d h t")[:, :, tsl], reads=[rQDT], writes=[rQDb])
                        P.dma("sp", AITb[:], AITs.rearrange("h j (n i) -> j h n i", i=64)[:, :, nb * 8:(nb + 1) * 8, :], reads=[rAIT], writes=[rAITb])
                        P.dma("sp", ZTb[:], ZT.rearrange("(h p) t -> p h t", p=128)[:, :, tsl], reads=[rZT], writes=[rZTb])
                        for h in range(8):
                            P.op("pe", [mm(wps[:, n * 64:(n + 1) * 64], KBDb[:, h, n, :], TTb[:, h, n, :], True, True) for n in range(8)],
                                 reads=[rKBDb, rTTb], writes=[rwps])
                            P.op("act", lambda g, h=h: g.copy(WTb[:, h, :, :], wps[:].rearrange("p (a b) -> p a b", a=8)), reads=[rwps], writes=[rWTb])
                            P.op("pe", [mm(ups[:, n * 128:(n + 1) * 128], TTb[:, h, n, :], VBb[:, h, n, :], True, True) for n in range(8)],
                                 reads=[rVBb, rTTb], writes=[rups])
                            P.op("dve", lambda g, h=h: g.tensor_copy(Ub[:, h, :, :], ups[:].rearrange("p (a b) -> p a b", a=8)), reads=[rups], writes=[rUb])
                        for n in range(8):
                            ng = nb * 8 + n
                            for hg in range(2):
                                hs = range(4 * hg, 4 * hg + 4)
                                P.op("pe", [mm(wsps[:, h * 128:(h + 1) * 128], WTb[:, h, n, :], Sb[:, h, :], True, True) for h in hs],
                                     reads=[rWTb, rSbh[hg]], writes=[rwsh[hg]])
                            for hg in range(2):
                                hsl = slice(4 * hg, 4 * hg + 4)
                                P.op("dve", lambda g, n=n, hsl=hsl, hg=hg: g.tensor_tensor(out=vnew[:, hsl, :], in0=Ub[:, hsl, n, :],
                                     in1=wsps[:, hg * 512:(hg + 1) * 512].rearrange("p (a b) -> p a b", a=4), op=ALU.subtract),
                                     reads=[rUb, rwsh[hg]], writes=[rvnh[hg]])
                            for hg in range(2):
                                hs = range(4 * hg, 4 * hg + 4)
                                fns = []
                                for h in hs:
                                    fns.append(mm(ops_[:, h * 64:(h + 1) * 64], Sb[:, h, :], QDb[:, h, n * 64:(n + 1) * 64], True, False))
                                    fns.append(mm(ops_[:, h * 64:(h + 1) * 64], vnew[:, h, :], AITb[:, h, n, :], False, True))
                                P.op("pe", fns, reads=[rSbh[hg], rQDb, rvnh[hg], rAITb], writes=[ropsh[hg]])
                                P.op("pe", [mm(dsps[:, h * 128:(h + 1) * 128], KDECb[:, h, n, :], vnew[:, h, :], True, True) for h in hs],
                                     reads=[rKDECb, rvnh[hg]], writes=[rdsh[hg]])
                            for hg in range(2):
                                hsl = slice(4 * hg, 4 * hg + 4)
                                P.op("act", lambda g, n=n, hsl=hsl, hg=hg: g.copy(Oraw[:, hsl, n * 64:(n + 1) * 64],
                                     ops_[:, hg * 256:(hg + 1) * 256].rearrange("p (a b) -> p a b", a=4)), reads=[ropsh[hg]], writes=[rOraw])
                                P.op("dve", lambda g, ng=ng, hsl=hsl: g.tensor_tensor(out=Sf[:, hsl, :], in0=Sf[:, hsl, :],
                                     in1=cd_bc[:, ng, hsl].unsqueeze(2).to_broadcast([128, 4, 128]), op=ALU.mult), reads=[rSh[hg], rdc], writes=[rSh[hg]])
                                P.op("dve", lambda g, hsl=hsl, hg=hg: g.tensor_tensor(out=Sf[:, hsl, :], in0=Sf[:, hsl, :],
                                     in1=dsps[:, hg * 512:(hg + 1) * 512].rearrange("p (a b) -> p a b", a=4), op=ALU.add), reads=[rSh[hg], rdsh[hg]], writes=[rSh[hg]])
                                P.op("act", lambda g, hsl=hsl: g.copy(Sb[:, hsl, :], Sf[:, hsl, :]), reads=[rSh[hg]], writes=[rSbh[hg]])
                        for h in range(8):
                            P.op("act", lambda g, h=h: g.activation(sq_[:], Oraw[:, h, :], AF.Square), reads=[rOraw], writes=[rsq_])
                            P.op("pe", mm(wps[:], ones_b[:], sq_[:], True, True), reads=[rsq_, r_const], writes=[rwps])
                            P.op("act", lambda g: g.activation(rs_[:], wps[:], AF.Sqrt, scale=1.0 / 128, bias=EPS), reads=[rwps], writes=[rrs_])
                            P.op("dve", lambda g: g.reciprocal(rs_[:], rs_[:]), reads=[rrs_], writes=[rrs_])
                            P.op("dve", lambda g, h=h: g.scalar_tensor_tensor(out=t_[:], in0=Oraw[:, h, :], scalar=ggdn[:, 0:1], in1=rs_[:], op0=ALU.mult, op1=ALU.mult),
                                 reads=[rOraw, rrs_, rcg], writes=[rt_])
                            P.op("dve", lambda g, h=h: g.tensor_tensor(out=OG[:, h, :], in0=t_[:], in1=ZTb[:, h, :], op=ALU.mult), reads=[rt_, rZTb], writes=[rOG])
                        P.dma("sp", OGDN.rearrange("(h p) t -> p h t", p=128)[:, :, tsl], OG[:], reads=[rOG], writes=[rOGDN])
                P.barrier()

        with ExitStack() as mid:
            KR = sbt(mid, [64, S], BF16); rKR = Res()
            g_tm = sbt(mid, [64, 64, 8], F32); lnb_tm = sbt(mid, [64, 64, 8], F32); r_gl = Res()
            bc = ExitStack()
            cos_sb = sbt(bc, [64, S], F32); sin_sb = sbt(bc, [64, S], F32); r_cs = Res()
            P.dma("sp", cos_sb[:], cosT, writes=[r_cs])
            P.dma("sp", sin_sb[:], sinS, writes=[r_cs])

            with ExitStack() as st:
                h2 = sbt(st, [128, 8, S], BF16); rh2 = [Res() for _ in range(8)]
                H2v = H2T.rearrange("(kc p) t -> p kc t", p=128)
                for tg in range(8):
                    P.dma("sp", h2[:, :, tg * 512:(tg + 1) * 512], H2v[:, :, tg * 512:(tg + 1) * 512],
                          reads=[rH2T[tg]], writes=[rh2[tg]])
                wring = Ring([sbt(st, [128, 8, 512], BF16) for _ in range(2)])
                oring = Ring([sbt(st, [128, S], BF16) for _ in range(2)])
                psr = Ring([pst(st, [128, 512]) for _ in range(4)])
                winv = w_in.rearrange("(kc p) n -> p kc n", p=128)
                jobs = [(0, 512, QLAT, 0, None, [rQLAT] * 4), (512, 256, QLAT, 512, None, [rQLAT] * 2),
                        (768, 256, KVLAT, 0, None, [rKVLAT] * 2)]
                for i in range(6):
                    jobs.append((1088 + i * 512, 512, GQKV, i * 512, None, rGQKV[i * 4:(i + 1) * 4]))
                for i in range(2):
                    jobs.append((4176 + i * 512, 512, ZT, i * 512, AF.Silu, [rZT] * 4))
                for i in range(4):
                    jobs.append((5200 + i * 512, 512, GATES, i * 512, AF.Sigmoid, [rGATES] * 4))
                ev = 0
                for (c0, W, dst, r0, func, rds) in jobs:
                    wb, rw = wring.next()
                    P.dma("pool", wb[:, :, 0:W], winv[:, :, c0:c0 + W], writes=[rw])
                    for cc in range(W // 128):
                        ot, rot = oring.next()
                        for tg in range(8):
                            ps, rps = psr.next()
                            P.op("pe", [mm(ps[:], wb[:, kc, cc * 128:(cc + 1) * 128], h2[:, kc, tg * 512:(tg + 1) * 512],
                                           kc == 0, kc == 7) for kc in range(8)], reads=[rw, rh2[tg]], writes=[rps])
                            dsl = ot[:, tg * 512:(tg + 1) * 512]
                            if func is not None:
                                P.op("act", lambda g, dsl=dsl, ps=ps, func=func: g.activation(dsl, ps[:], func),
                                     reads=[rps], writes=[rot])
                            elif ev % 2 == 0:
                                P.op("act", lambda g, dsl=dsl, ps=ps: g.copy(dsl, ps[:]), reads=[rps], writes=[rot])
                            else:
                                P.op("dve", lambda g, dsl=dsl, ps=ps: g.tensor_copy(dsl, ps[:]), reads=[rps], writes=[rot])
                            ev += 1
                        P.dma("sp", dst[r0 + cc * 128:r0 + (cc + 1) * 128, :], ot[:], reads=[rot], writes=[rds[cc]])
                wkp = sbt(st, [128, 8, 64], BF16); wkps = sbt(st, [128, 8, 64], BF16); rwk = Res()
                P.dma("pool", wkp[:], winv[:, :, 1024:1088], writes=[rwk])
                P.dma("pool", wkps[:, :, 0:32], winv[:, :, 1056:1088], writes=[rwk])
                P.dma("pool", wkps[:, :, 32:64], winv[:, :, 1024:1056], writes=[rwk])
                t1r = Ring([sbt(st, [64, 512], F32) for _ in range(2)])
                t2r = Ring([sbt(st, [64, 512], F32) for _ in range(2)])
                for tg in range(8):
                    pa, rpa = psr.next(); pb, rpb = psr.next()
                    tsl = slice(tg * 512, (tg + 1) * 512)
                    P.op("pe", [mm(pa[0:64, :], wkp[:, kc, :], h2[:, kc, tsl], kc == 0, kc == 7) for kc in range(8)],
                         reads=[rwk, rh2[tg]], writes=[rpa])
                    P.op("pe", [mm(pb[0:64, :], wkps[:, kc, :], h2[:, kc, tsl], kc == 0, kc == 7) for kc in range(8)],
                         reads=[rwk, rh2[tg]], writes=[rpb])
                    t1, rt1 = t1r.next(); t2, rt2 = t2r.next()
                    P.op("dve", lambda g, t1=t1, pa=pa, tsl=tsl: g.tensor_tensor(out=t1[:], in0=pa[0:64, :], in1=cos_sb[:, tsl], op=ALU.mult),
                         reads=[rpa, r_cs], writes=[rt1])
                    P.op("dve", lambda g, t2=t2, pb=pb, tsl=tsl: g.tensor_tensor(out=t2[:], in0=pb[0:64, :], in1=sin_sb[:, tsl], op=ALU.mult),
                         reads=[rpb, r_cs], writes=[rt2])
                    P.op("dve", lambda g, t1=t1, t2=t2, tsl=tsl: g.tensor_tensor(out=KR[:, tsl], in0=t1[:], in1=t2[:], op=ALU.add),
                         reads=[rt1, rt2], writes=[rKR])
                wab = sbt(st, [128, 8, 16], BF16); rwab = Res()
                P.dma("pool", wab[:], winv[:, :, 4160:4176], writes=[rwab])
                alog_sb = sbt(st, [64, 8], F32); dtb_sb = sbt(st, [64, 8], F32); r_ad = Res()
                P.dma("sp", alog_sb[:], alog_bc, writes=[r_ad])
                P.dma("sp", dtb_sb[:], dtb_bc, writes=[r_ad])
                abps = pst(st, [64, 1024]); rab = Res()
                for n in range(64):
                    P.op("pe", [mm(abps[:, n * 16:(n + 1) * 16], h2[:, kc, n * 64:(n + 1) * 64], wab[:, kc, :], kc == 0, kc == 7)
                                for kc in range(8)], reads=[rwab, rh2[n // 8]], writes=[rab])
                abv = abps[:].rearrange("p (n c) -> p n c", c=16)
                tA = sbt(st, [64, 64, 8], F32); tB = sbt(st, [64, 64, 8], F32); rtA = Res(); rtB = Res()
                P.op("dve", lambda g: g.tensor_tensor(out=tA[:], in0=abv[:, :, 0:8], in1=dtb_sb[:].unsqueeze(1).to_broadcast([64, 64, 8]), op=ALU.add),
                     reads=[rab, r_ad], writes=[rtA])
                P.op("act", lambda g: g.activation(tA[:], tA[:], AF.Exp), reads=[rtA], writes=[rtA])
                P.op("act", lambda g: g.activation(tA[:], tA[:], AF.Ln, bias=1.0), reads=[rtA], writes=[rtA])
                P.op("act", lambda g: g.activation(alog_sb[:], alog_sb[:], AF.Exp), reads=[r_ad], writes=[r_ad])
                P.op("dve", lambda g: g.tensor_scalar(out=alog_sb[:], in0=alog_sb[:], scalar1=-1.0, scalar2=None, op0=ALU.mult),
                     reads=[r_ad], writes=[r_ad])
                P.op("dve", lambda g: g.tensor_tensor(out=g_tm[:], in0=tA[:], in1=alog_sb[:].unsqueeze(1).to_broadcast([64, 64, 8]), op=ALU.mult),
                     reads=[rtA, r_ad], writes=[r_gl])
                P.op("act", lambda g: g.activation(tB[:], abv[:, :, 8:16], AF.Exp, scale=-1.0), reads=[rab], writes=[rtB])
                P.op("act", lambda g: g.activation(tB[:], tB[:], AF.Ln, bias=1.0), reads=[rtB], writes=[rtB])
                P.op("dve", lambda g: g.tensor_scalar(out=lnb_tm[:], in0=tB[:], scalar1=-1.0, scalar2=None, op0=ALU.mult),
                     reads=[rtB], writes=[r_gl])
            P.barrier()
            if upto == "B":
                if dbg:
                    kro = nc.dram_tensor("KR_o", [64, S], BF16, kind="ExternalOutput").ap()
                    gto = nc.dram_tensor("g_o", [64, 512], F32, kind="ExternalOutput").ap()
                    lbo = nc.dram_tensor("lnb_o", [64, 512], F32, kind="ExternalOutput").ap()
                    P.dma("sp", kro, KR[:], reads=[rKR], writes=[rKR])
                    P.dma("sp", gto, g_tm[:].rearrange("p n h -> p (n h)"), reads=[r_gl], writes=[r_gl])
                    P.dma("sp", lbo, lnb_tm[:].rearrange("p n h -> p (n h)"), reads=[r_gl], writes=[r_gl])
                P.final_wait("sp", [rQLAT, rKVLAT, rZT, rGATES, rKR, r_gl] + rGQKV)
                bc.close()
                P.emit(nc, sems)
                return nc

            with ExitStack() as st:
                qn = sbt(st, [128, 6, S], BF16); rqn = Res()
                kvn = sbt(st, [128, 2, S], BF16); rkvn = Res()
                for kc in range(6):
                    P.dma("sp", qn[:, kc, :], QLAT[kc * 128:(kc + 1) * 128, :], reads=[rQLAT], writes=[rqn])
                for kc in range(2):
                    P.dma("sp", kvn[:, kc, :], KVLAT[kc * 128:(kc + 1) * 128, :], reads=[rKVLAT], writes=[rkvn])
                gq_sb = sbt(st, [128, 6], F32); gkv_sb = sbt(st, [128, 2], F32); rgg = Res()
                P.dma("sp", gq_sb[:], gq_fm, writes=[rgg])
                P.dma("sp", gkv_sb[:], gkv_fm, writes=[rgg])
                wuq = sbt(st, [128, 6, 1536], BF16); wuqs = sbt(st, [128, 6, 8, 64], BF16); wukv = sbt(st, [128, 2, 2048], BF16)
                rwq = Res()
                wuqv = w_uq.rearrange("(kc p) n -> p kc n", p=128)
                wuq4 = w_uq.rearrange("(kc p) (h c) -> p kc h c", p=128, c=192)
                for kc in range(6):
                    P.dma("pool", wuq[:, kc, :], wuqv[:, kc, :], writes=[rwq])
                    P.dma("pool", wuqs[:, kc, :, 0:32], wuq4[:, kc, :, 160:192], writes=[rwq])
                    P.dma("pool", wuqs[:, kc, :, 32:64], wuq4[:, kc, :, 128:160], writes=[rwq])
                P.dma("pool", wukv[:], w_ukv.rearrange("(kc p) n -> p kc n", p=128), writes=[rwq])
                sq = sbt(st, [128, 8, 512], BF16); rsq = Res()
                rstd = sbt(st, [128, 512], F32); rrstd = Res()
                psr = Ring([pst(st, [128, 512]) for _ in range(4)])
                pacc = Ring([pst(st, [128, 512]) for _ in range(4)])
                for tg in range(8):
                    tsl = slice(tg * 512, (tg + 1) * 512)
                    for (src, rsrc, nkc, dim, gsb) in ((qn, rqn, 6, 768, gq_sb), (kvn, rkvn, 2, 256, gkv_sb)):
                        pss, rpss = psr.next()
                        fm_stats(src[:, :, tsl], rsrc, nkc, 512, dim, sq, rsq, pss, rpss, rstd, rrstd)
                        for kc in range(nkc):
                            P.op("dve", lambda g, src=src, kc=kc, gsb=gsb, tsl=tsl: g.scalar_tensor_tensor(
                                out=src[:, kc, tsl], in0=src[:, kc, tsl], scalar=gsb[:, kc:kc + 1], in1=rstd[:],
                                op0=ALU.mult, op1=ALU.mult), reads=[rsrc, rrstd, rgg], writes=[rsrc])
                QN = sbt(st, [128, S], BF16); rQN = Res()
                QR = sbt(st, [64, S], BF16); rQR = Res()
                KN = sbt(st, [128, S], BF16); rKN = Res()
                Vh = sbt(st, [128, 32, 128], BF16); rV = Res()
                OH = sbt(st, [128, S], BF16); rOH = Res()
                t1r = Ring([sbt(st, [64, 512], F32) for _ in range(2)])
                t2r = Ring([sbt(st, [64, 512], F32) for _ in range(2)])
                ptr = Ring([sbt(st, [128, 512], BF16) for _ in range(4)])
                rden_sb = sbt(st, [128, 512], F32); rrd = Res()
                for h in range(8):
                    for tg in range(8):
                        tsl = slice(tg * 512, (tg + 1) * 512)
                        ps, rps = psr.next()
                        P.op("pe", [mm(ps[:], wuq[:, kc, h * 192:h * 192 + 128], qn[:, kc, tsl], kc == 0, kc == 5) for kc in range(6)],
                             reads=[rwq, rqn], writes=[rps])
                        P.op("act", lambda g, ps=ps, tsl=tsl: g.activation(QN[:, tsl], ps[:], AF.Copy, scale=QSCALE), reads=[rps], writes=[rQN])
                        pa, rpa = psr.next(); pb, rpb = psr.next()
                        P.op("pe", [mm(pa[0:64, :], wuq[:, kc, h * 192 + 128:h * 192 + 192], qn[:, kc, tsl], kc == 0, kc == 5) for kc in range(6)],
                             reads=[rwq, rqn], writes=[rpa])
                        P.op("pe", [mm(pb[0:64, :], wuqs[:, kc, h, :], qn[:, kc, tsl], kc == 0, kc == 5) for kc in range(6)],
                             reads=[rwq, rqn], writes=[rpb])
                        t1, rt1 = t1r.next(); t2, rt2 = t2r.next()
                        P.op("dve", lambda g, t1=t1, pa=pa, tsl=tsl: g.scalar_tensor_tensor(out=t1[:], in0=pa[0:64, :], scalar=QSCALE, in1=cos_sb[:, tsl], op0=ALU.mult, op1=ALU.mult),
                             reads=[rpa, r_cs], writes=[rt1])
                        P.op("dve", lambda g, t2=t2, pb=pb, tsl=tsl: g.scalar_tensor_tensor(out=t2[:], in0=pb[0:64, :], scalar=QSCALE, in1=sin_sb[:, tsl], op0=ALU.mult, op1=ALU.mult),
                             reads=[rpb, r_cs], writes=[rt2])
                        P.op("dve", lambda g, t1=t1, t2=t2, tsl=tsl: g.tensor_tensor(out=QR[:, tsl], in0=t1[:], in1=t2[:], op=ALU.add),
                             reads=[rt1, rt2], writes=[rQR])
                        pk, rpk = psr.next()
                        P.op("pe", [mm(pk[:], wukv[:, kc, h * 256:h * 256 + 128], kvn[:, kc, tsl], kc == 0, kc == 1) for kc in range(2)],
                             reads=[rwq, rkvn], writes=[rpk])
                        P.op("act", lambda g, pk=pk, tsl=tsl: g.copy(KN[:, tsl], pk[:]), reads=[rpk], writes=[rKN])
                        pv, rpv = psr.next()
                        fns = []
                        for tt in range(4):
                            for kc in range(2):
                                fns.append(mm(pv[:, tt * 128:(tt + 1) * 128], kvn[:, kc, (tg * 4 + tt) * 128:(tg * 4 + tt + 1) * 128],
                                              wukv[:, kc, h * 256 + 128:h * 256 + 256], kc == 0, kc == 1))
                        P.op("pe", fns, reads=[rwq, rkvn], writes=[rpv])
                        P.op("dve", lambda g, pv=pv, tg=tg: g.tensor_copy(Vh[:, tg * 4:(tg + 1) * 4, :], pv[:].rearrange("p (a b) -> p a b", a=4)),
                             reads=[rpv], writes=[rV])
                    for gq_ in range(8):
                        q0 = gq_ * 512
                        nj = 4 * gq_ + 4
                        pden, rpden = pacc.next(); po, rpo = pacc.next()
                        pendq = []

                        def emit_acc(j, c0, pt, rpt):
                            P.op("pe", [mm(pden[:, c0:512], ones_b[:], pt[:, c0:512], j == 0, j == nj - 1),
                                        mm(po[:, c0:512], Vh[:, j, :], pt[:, c0:512], j == 0, j == nj - 1)],
                                 reads=[rpt, rV, r_const], writes=[rpden, rpo])
                        for j in range(nj):
                            c0 = max(0, (j - 4 * gq_) * 128)
                            ps, rps = psr.next()
                            P.op("pe", [mm(ps[:, c0:512], KN[:, j * 128:(j + 1) * 128], QN[:, q0 + c0:q0 + 512], True, False),
                                        mm(ps[:, c0:512], KR[:, j * 128:(j + 1) * 128], QR[:, q0 + c0:q0 + 512], False, True)],
                                 reads=[rKN, rQN, rKR, rQR], writes=[rps])
                            pt, rpt = ptr.next()
                            P.op("act", lambda g, pt=pt, ps=ps, c0=c0: g.activation(pt[:, c0:512], ps[:, c0:512], AF.Exp), reads=[rps], writes=[rpt])
                            if j >= 4 * gq_:
                                P.op("dve", lambda g, pt=pt, c0=c0: g.memset(pt[64:128, c0:c0 + 64], 0.0), reads=[rpt], writes=[rpt])
                            pendq.append((j, c0, pt, rpt))
                            if len(pendq) > 2:
                                emit_acc(*pendq.pop(0))
                        while pendq:
                            emit_acc(*pendq.pop(0))
                        P.op("dve", lambda g, pden=pden: g.reciprocal(rden_sb[:], pden[:]), reads=[rpden], writes=[rrd])
                        P.op("dve", lambda g, po=po, q0=q0: g.tensor_tensor(out=OH[:, q0:q0 + 512], in0=po[:], in1=rden_sb[:], op=ALU.mult),
                             reads=[rpo, rrd], writes=[rOH])
                    P.dma("sp", OMLA[h * 128:(h + 1) * 128, :], OH[:], reads=[rOH], writes=[rOMLA])
                if upto == "C" and dbg:
                    for nm, tl, rr, pp in (("QN_o", QN, rQN, 128), ("QR_o", QR, rQR, 64), ("KN_o", KN, rKN, 128), ("qn_o", qn[:, 0, :], rqn, 128)):
                        o_ = nc.dram_tensor(nm, [pp, S], BF16, kind="ExternalOutput").ap()
                        P.dma("sp", o_, tl[:] if nm != "qn_o" else tl, reads=[rr], writes=[rr])
                    o_ = nc.dram_tensor("V_o", [128, 32 * 128], BF16, kind="ExternalOutput").ap()
                    P.dma("sp", o_, Vh[:].rearrange("p a b -> p (a b)"), reads=[rV], writes=[rV])
            P.barrier()
            if upto == "C":
                P.final_wait("sp", [rOMLA])
                bc.close()
                P.emit(nc, sems)
                return nc

            bc.close()
            early = gdn_phase()
            P.barrier()
            if upto == "D" or early:
                P.final_wait("sp", [rOGDN])
                P.emit(nc, sems)
                return nc

            mid.close()
            fw = ExitStack()
            pre2, issue2 = load_gu(1, fw, defer=True)
            with ExitStack() as st:
                wom = sbt(st, [128, 8, D], BF16); wog = sbt(st, [128, 8, D], BF16); wo = sbt(st, [128, 8, D], BF16); rwe = Res()
                for wt_, src_ in ((wom, w_o_mla), (wog, w_o_gdn), (wo, w_out)):
                    sv = src_.rearrange("(kc p) n -> p kc n", p=128)
                    for kc in range(0, 8, 2):
                        P.dma("pool", wt_[:, kc:kc + 2, :], sv[:, kc:kc + 2, :], writes=[rwe])
                issue2()
                om = sbt(st, [128, 8, 512], BF16); og = sbt(st, [128, 8, 512], BF16); gt_ = sbt(st, [128, 16, 512], BF16)
                x1t = sbt(st, [128, 8, 512], F32); zt = sbt(st, [128, 8, 512], BF16)
                rom = Res(); rog = Res(); rgt = Res(); rx1 = Res(); rzt = Res()
                ta = Ring([sbt(st, [128, 512], F32) for _ in range(2)])
                tb = Ring([sbt(st, [128, 512], F32) for _ in range(2)])
                psr = Ring([pst(st, [128, 512]) for _ in range(6)])
                def loads_a(tg_):
                    ts_ = slice(tg_ * 512, (tg_ + 1) * 512)
                    P.dma("sp", om[:], OMLA.rearrange("(kc p) t -> p kc t", p=128)[:, :, ts_], reads=[rOMLA], writes=[rom])
                    P.dma("sp", og[:], OGDN.rearrange("(kc p) t -> p kc t", p=128)[:, :, ts_], reads=[rOGDN], writes=[rog])
                    P.dma("sp", gt_[:], GATES.rearrange("(kc p) t -> p kc t", p=128)[:, :, ts_], reads=[rGATES], writes=[rgt])

                def load_x(tg_):
                    ts_ = slice(tg_ * 512, (tg_ + 1) * 512)
                    P.dma("sp", x1t[:], X1T.rearrange("(kc p) t -> p kc t", p=128)[:, :, ts_], reads=[rX1T[tg_]], writes=[rx1])
                loads_a(0)
                load_x(0)
                for tg in range(8):
                    tsl = slice(tg * 512, (tg + 1) * 512)
                    for n in range(8):
                        pm, rpm = psr.next(); pg_, rpg_ = psr.next()
                        P.op("pe", [mm(pm[:], wom[:, kc, n * 128:(n + 1) * 128], om[:, kc, :], kc == 0, kc == 7) for kc in range(8)],
                             reads=[rwe, rom], writes=[rpm])
                        P.op("pe", [mm(pg_[:], wog[:, kc, n * 128:(n + 1) * 128], og[:, kc, :], kc == 0, kc == 7) for kc in range(8)],
                             reads=[rwe, rog], writes=[rpg_])
                        a_, ra_ = ta.next(); b_, rb_ = tb.next()
                        P.op("dve", lambda g, a_=a_, pm=pm, n=n: g.tensor_tensor(out=a_[:], in0=pm[:], in1=gt_[:, n, :], op=ALU.mult),
                             reads=[rpm, rgt], writes=[ra_])
                        P.op("dve", lambda g, b_=b_, pg_=pg_, n=n: g.tensor_tensor(out=b_[:], in0=pg_[:], in1=gt_[:, 8 + n, :], op=ALU.mult),
                             reads=[rpg_, rgt], writes=[rb_])
                        P.op("dve", lambda g, a_=a_, b_=b_, n=n: g.tensor_tensor(out=zt[:, n, :], in0=a_[:], in1=b_[:], op=ALU.add),
                             reads=[ra_, rb_], writes=[rzt])
                    if tg + 1 < 8:
                        loads_a(tg + 1)
                    for n in range(8):
                        px, rpx = psr.next()
                        P.op("pe", [mm(px[:], wo[:, kc, n * 128:(n + 1) * 128], zt[:, kc, :], kc == 0, kc == 7) for kc in range(8)],
                             reads=[rwe, rzt], writes=[rpx])
                        P.op("dve", lambda g, px=px, n=n: g.scalar_tensor_tensor(out=x1t[:, n, :], in0=px[:], scalar=PG[:, 1, n:n + 1], in1=x1t[:, n, :],
                                                                        op0=ALU.mult, op1=ALU.add), reads=[rpx, r_ada, rx1], writes=[rx1])
                    P.dma("sp", X2T.rearrange("(kc p) t -> p kc t", p=128)[:, :, tsl], x1t[:], reads=[rx1], writes=[rX2T[tg]])
                    if tg + 1 < 8:
                        load_x(tg + 1)
            P.barrier()
        P.barrier()
        if upto == "E":
            P.final_wait("sp", rX2T)
            fw.close()
            P.emit(nc, sems)
            return nc
        ffn_phase(1, pre2)
        P.final_wait("sp", [r_y])
        fw.close()
        P.emit(nc, sems)
    return nc


_CACHE = {}


def _consts():
    inv_freq = 10000.0 ** (-np.arange(0, 64, 2, dtype=np.float64) / 64.0)
    pos = np.arange(S, dtype=np.float64)
    ang = pos[:, None] * inv_freq[None, :]
    cos = np.cos(ang).astype(np.float32).T
    sin = np.sin(ang).astype(np.float32).T
    cosT = np.concatenate([cos, cos], 0)
    sinS = np.concatenate([-sin, sin], 0)
    ident = np.eye(128, dtype=np.float32)
    t = np.arange(64)
    tri = (t[:, None] <= t[None, :]).astype(np.float32)
    eye = np.eye(64, dtype=np.float32)
    negm = np.where(t[None, :] >= t[:, None], 0.0, NEG).astype(np.float32)
    return dict(cosT=np.ascontiguousarray(cosT), sinS=np.ascontiguousarray(sinS), c_ident=ident, c_tri=tri,
                c_eye=eye, c_negm=negm, c_eyeb=np.ascontiguousarray(np.broadcast_to(eye.reshape(1, 4096), (128, 4096))))


def _fm(v, nk):
    return np.ascontiguousarray(np.asarray(v, np.float32).reshape(nk, 128).T)


def make_in_maps(inp):
    cst = _consts()
    shared = dict(cst)
    shared["w_ada"] = np.ascontiguousarray(inp["w_ada"][0])
    shared["bada_fm"] = _fm(inp["b_ada"][0], 72)
    shared["gains_fm"] = np.ascontiguousarray(np.stack(
        [_fm(inp["g_ffn1"][0], 8), _fm(inp["g_mix"][0], 8), _fm(inp["g_ffn2"][0], 8), _fm(inp["g_final"], 8)], 1))
    for k in ["w1_gate", "w1_up", "w1_down", "w2_gate", "w2_up", "w2_down", "w_in", "w_uq", "w_ukv",
              "w_o_mla", "w_o_gdn", "w_out"]:
        shared[k] = np.ascontiguousarray(inp[k][0])
    shared["gq_fm"] = _fm(inp["g_q_lat"][0], 6)
    shared["gkv_fm"] = _fm(inp["g_kv_lat"][0], 2)
    wc = np.asarray(inp["w_conv"][0], np.float32)
    shared["wconv_fm"] = np.ascontiguousarray(wc.reshape(4, 24, 128).transpose(2, 1, 0))
    shared["alog_bc"] = np.ascontiguousarray(np.broadcast_to(np.asarray(inp["a_log"][0], np.float32)[None, :], (64, 8)))
    shared["dtb_bc"] = np.ascontiguousarray(np.broadcast_to(np.asarray(inp["dt_bias"][0], np.float32)[None, :], (64, 8)))
    shared["ggdn_fm"] = _fm(inp["g_gdn_out"][0], 1)
    maps = []
    for b in range(NCORES):
        m = dict(shared)
        m["x"] = np.ascontiguousarray(inp["x"][b])
        m["c_fm"] = _fm(inp["c"][b], 8)
        maps.append(m)
    return maps


def kernel(**inputs):
    if "nc" not in _CACHE:
        _CACHE["nc"] = build()
    nc = _CACHE["nc"]
    in_maps = make_in_maps(inputs)
    res = run_bass_kernel_spmd(nc, in_maps, core_ids=list(range(NCORES)))
    return np.stack([np.asarray(res.results[b]["y"], np.float32) for b in range(4)], 0)
```

```python
import numpy as np
from contextlib import ExitStack
import concourse.bass as bass
import concourse.mybir as mybir
from concourse.bass_utils import run_bass_kernel_spmd

F32 = mybir.dt.float32
BF16 = mybir.dt.bfloat16
AF = mybir.ActivationFunctionType
ALU = mybir.AluOpType
AX = mybir.AxisListType

S = 4096
D = 1024
DFF = 2816
NFC = 22
EPS = 1e-6
NCORES = 4
TF = 512
NEG = -30000.0
import os as _os
D2MUL = _os.environ.get('D2MUL', 'pool')
QSCALE = 192.0 ** -0.5


class Res:
    __slots__ = ("w", "r")

    def __init__(self):
        self.w = None
        self.r = {}


class Eng:
    def __init__(self, name, idx):
        self.name = name
        self.idx = idx
        self.ops = []
        self.count = 0
        self.waited = {}


class Prog:
    NDMA = 48

    def __init__(self):
        self.eng = {n: Eng(n, i) for i, n in enumerate(["pe", "act", "dve", "pool", "sp"])}
        self.dma_uses = [0] * self.NDMA
        self.dma_next = 0
        self.dma_next_sw = 0
        self.nsem = 5 + self.NDMA

    def _deps(self, reads, writes):
        deps = {}

        def add(s, v):
            if deps.get(s, 0) < v:
                deps[s] = v
        for r in reads:
            if r.w is not None:
                add(*r.w)
        for w in writes:
            if w.w is not None:
                add(*w.w)
            for s, v in w.r.items():
                add(s, v)
        return deps

    def _waits(self, e, deps):
        waits = []
        for s, v in deps.items():
            if e.name == "pe" and s == e.idx:
                continue
            if e.waited.get(s, 0) < v:
                e.waited[s] = v
                waits.append((s, v))
        return waits

    def _mark(self, tok, reads, writes):
        for r in reads:
            if r.r.get(tok[0], 0) < tok[1]:
                r.r[tok[0]] = tok[1]
        for w in writes:
            w.w = tok
            w.r = {}

    def op(self, en, fns, reads=(), writes=()):
        e = self.eng[en]
        if not isinstance(fns, (list, tuple)):
            fns = [fns]
        waits = self._waits(e, self._deps(reads, writes))
        e.count += 1
        tok = (e.idx, e.count)
        e.ops.append((waits, list(fns), (e.idx, 1)))
        self._mark(tok, reads, writes)

    def dma(self, en, out, in_, reads=(), writes=()):
        e = self.eng[en]
        if en == "pool":
            slot = 32 + self.dma_next_sw
            self.dma_next_sw = (self.dma_next_sw + 1) % 16
        else:
            slot = self.dma_next
            self.dma_next = (self.dma_next + 1) % 32
        s = 5 + slot
        deps = self._deps(reads, writes)
        prev = self.dma_uses[slot] * 16
        if prev and deps.get(s, 0) < prev:
            deps[s] = prev
        waits = self._waits(e, deps)
        self.dma_uses[slot] += 1
        tok = (s, self.dma_uses[slot] * 16)
        e.ops.append((waits, [lambda g, out=out, in_=in_: g.dma_start(out=out, in_=in_)], (s, 16)))
        self._mark(tok, reads, writes)

    def barrier(self):
        deps = {e.idx: e.count for e in self.eng.values() if e.count}
        for slot, u in enumerate(self.dma_uses):
            if u:
                deps[5 + slot] = u * 16
        for e in self.eng.values():
            w = self._waits(e, dict(deps))
            if w:
                e.ops.append((w, [], None))

    def final_wait(self, en, resources):
        e = self.eng[en]
        deps = self._deps([], resources)
        e.ops.append((self._waits(e, deps), [], None))

    def emit(self, nc, sems):
        with nc.Block() as block:
            def run(e, g):
                for waits, fns, inc in e.ops:
                    for s, v in waits:
                        g.wait_ge(sems[s], v)
                    ins = None
                    for f in fns:
                        ins = f(g)
                    if inc is not None and ins is not None:
                        ins.then_inc(sems[inc[0]], inc[1])

            @block.tensor
            def _(g):
                run(self.eng["pe"], g)

            @block.scalar
            def _(g):
                run(self.eng["act"], g)

            @block.vector
            def _(g):
                run(self.eng["dve"], g)

            @block.gpsimd
            def _(g):
                run(self.eng["pool"], g)

            @block.sync
            def _(g):
                run(self.eng["sp"], g)


class Ring:
    def __init__(self, tiles):
        self.t = [(t, Res()) for t in tiles]
        self.i = 0

    def next(self):
        r = self.t[self.i % len(self.t)]
        self.i += 1
        return r


def mm(out, lhsT, rhs, start, stop):
    return lambda g: g.matmul(out, lhsT, rhs, start=start, stop=stop)


def build(upto="F", dbg=False):
    nc = bass.Bass("TRN2", target_bir_lowering=False)
    P = Prog()

    def din(name, shape, dt=F32):
        return nc.dram_tensor(name, list(shape), dt, kind="ExternalInput").ap()

    def dscr(name, shape, dt):
        return nc.dram_tensor(name, list(shape), dt, kind=("ExternalOutput" if (dbg and name in dbg) else "Internal")).ap()

    x_in = din("x", [S, D])
    c_fm = din("c_fm", [128, 8])
    w_ada = din("w_ada", [D, 9 * D])
    bada_fm = din("bada_fm", [128, 72])
    gains_fm = din("gains_fm", [128, 4, 8])
    w_gate = [din("w1_gate", [D, DFF]), din("w2_gate", [D, DFF])]
    w_up = [din("w1_up", [D, DFF]), din("w2_up", [D, DFF])]
    w_down = [din("w1_down", [DFF, D]), din("w2_down", [DFF, D])]
    w_in = din("w_in", [D, 7248])
    gq_fm = din("gq_fm", [128, 6])
    gkv_fm = din("gkv_fm", [128, 2])
    w_uq = din("w_uq", [768, 1536])
    w_ukv = din("w_ukv", [256, 2048])
    wconv_fm = din("wconv_fm", [128, 24, 4])
    alog_bc = din("alog_bc", [64, 8])
    dtb_bc = din("dtb_bc", [64, 8])
    ggdn_fm = din("ggdn_fm", [128, 1])
    w_o_mla = din("w_o_mla", [D, D])
    w_o_gdn = din("w_o_gdn", [D, D])
    w_out = din("w_out", [D, D])
    cosT = din("cosT", [64, 4096])
    sinS = din("sinS", [64, 4096])
    c_ident = din("c_ident", [128, 128])
    c_tri = din("c_tri", [64, 64])
    c_eye = din("c_eye", [64, 64])
    c_negm = din("c_negm", [64, 64])
    y_out = nc.dram_tensor("y", [S, D], F32, kind="ExternalOutput").ap()

    X1T = dscr("X1T", [D, S], F32); rX1T = [Res() for _ in range(8)]
    H2T = dscr("H2T", [D, S], BF16); rH2T = [Res() for _ in range(8)]
    QLAT = dscr("QLAT", [768, S], BF16); rQLAT = Res()
    KVLAT = dscr("KVLAT", [256, S], BF16); rKVLAT = Res()
    GQKV = dscr("GQKV", [3072, S], BF16); rGQKV = [Res() for _ in range(24)]
    ZT = dscr("ZT", [1024, S], BF16); rZT = Res()
    GATES = dscr("GATES", [2048, S], BF16); rGATES = Res()
    OMLA = dscr("OMLA", [1024, S], BF16); rOMLA = Res()
    OGDN = dscr("OGDN", [1024, S], BF16); rOGDN = Res()
    X2T = dscr("X2T", [D, S], F32); rX2T = [Res() for _ in range(8)]

    with ExitStack() as top:
        cnt = [0]

        def sbt(st, shape, dt):
            cnt[0] += 1
            return st.enter_context(nc.sbuf_tensor(f"t{cnt[0]}", list(shape), dt))

        def pst(st, shape, dt=F32):
            cnt[0] += 1
            return st.enter_context(nc.psum_tensor(f"p{cnt[0]}", list(shape), dt))

        sems = [top.enter_context(nc.semaphore(f"s{i}")) for i in range(P.nsem)]

        ident_f = sbt(top, [128, 128], F32); r_const = Res()
        ident_b = sbt(top, [128, 128], BF16)
        ones_b = sbt(top, [128, 128], BF16)
        P.dma("sp", ident_f[:], c_ident, writes=[r_const])
        P.op("dve", lambda g: g.tensor_copy(ident_b[:], ident_f[:]), reads=[r_const], writes=[r_const])
        P.op("dve", lambda g: g.memset(ones_b[:], 1.0), writes=[r_const])
        gains = sbt(top, [128, 4, 8], F32)
        P.dma("sp", gains[:], gains_fm, writes=[r_const])
        ada = sbt(top, [128, 72], F32); r_ada = Res()
        PA = sbt(top, [128, 3, 8], F32)
        PG = sbt(top, [128, 3, 8], F32)

        with ExitStack() as st:
            c_sb = sbt(st, [128, 8], F32); rc = Res()
            csil = sbt(st, [128, 8], BF16)
            bada = sbt(st, [128, 72], F32); rb = Res()
            P.dma("sp", c_sb[:], c_fm, writes=[rc])
            P.dma("sp", bada[:], bada_fm, writes=[rb])
            P.op("act", lambda g: g.activation(csil[:], c_sb[:], AF.Silu), reads=[rc], writes=[rc])
            adaps = pst(st, [128, 72]); rps = Res()
            wring = Ring([sbt(st, [128, 8, 1024], BF16) for _ in range(2)])
            wv = w_ada.rearrange("(kc p) n -> p kc n", p=128)
            for j in range(9):
                wb, rw = wring.next()
                P.dma("pool", wb[:], wv[:, :, j * 1024:(j + 1) * 1024], writes=[rw])
                fns = []
                for cc in range(8):
                    for kc in range(8):
                        fns.append(mm(adaps[:, j * 8 + cc:j * 8 + cc + 1], wb[:, kc, cc * 128:(cc + 1) * 128],
                                      csil[:, kc:kc + 1], kc == 0, kc == 7))
                P.op("pe", fns, reads=[rw, rc], writes=[rps])
            P.op("dve", lambda g: g.tensor_tensor(out=ada[:], in0=adaps[:], in1=bada[:], op=ALU.add),
                 reads=[rps, rb], writes=[r_ada])
            for j in range(3):
                P.op("dve", lambda g, j=j: g.scalar_tensor_tensor(
                    out=PA[:, j, :], in0=ada[:, (3 * j + 1) * 8:(3 * j + 2) * 8], scalar=1.0,
                    in1=gains[:, j, :], op0=ALU.add, op1=ALU.mult), reads=[r_ada, r_const], writes=[r_ada])
                P.op("dve", lambda g, j=j: g.tensor_scalar(
                    out=PG[:, j, :], in0=ada[:, (3 * j + 2) * 8:(3 * j + 3) * 8], scalar1=(1.0 if j == 1 else 0.5),
                    scalar2=None, op0=ALU.mult), reads=[r_ada], writes=[r_ada])

        P.barrier()
        if upto == "0":
            ado = nc.dram_tensor("ada_o", [128, 72], F32, kind="ExternalOutput").ap()
            P.dma("sp", ado, ada[:], reads=[r_ada], writes=[r_ada])
            P.final_wait("sp", [r_ada])
            P.emit(nc, sems)
            return nc

        def Bcol(j, kc):
            return ada[:, 3 * j * 8 + kc:3 * j * 8 + kc + 1]

        def fm_stats(src3, rsrc, nkc, T, dim, sq, rsq, pss, rpss, rstd, rrstd):
            P.op("act", lambda g: g.activation(sq[:, 0:nkc, 0:T], src3, AF.Square), reads=[rsrc], writes=[rsq])
            P.op("pe", [mm(pss[:, 0:T], ones_b[:], sq[:, kc, 0:T], kc == 0, kc == nkc - 1) for kc in range(nkc)],
                 reads=[rsq, r_const], writes=[rpss])
            P.op("act", lambda g: g.activation(rstd[:, 0:T], pss[:, 0:T], AF.Sqrt, scale=1.0 / dim, bias=EPS),
                 reads=[rpss], writes=[rrstd])
            P.op("dve", lambda g: g.reciprocal(rstd[:, 0:T], rstd[:, 0:T]), reads=[rrstd], writes=[rrstd])

        def modulate(j, xT, rxT, hT, rhT, T, sq, rsq, pss, rpss, rstd, rrstd, tmp_ring):
            fm_stats(xT[:, :, 0:T], rxT, 8, T, D, sq, rsq, pss, rpss, rstd, rrstd)
            for kc in range(8):
                tmp, rt = tmp_ring.next()
                P.op("dve", lambda g, kc=kc, tmp=tmp: g.scalar_tensor_tensor(
                    out=tmp[:, 0:T], in0=xT[:, kc, 0:T], scalar=PA[:, j, kc:kc + 1], in1=rstd[:, 0:T],
                    op0=ALU.mult, op1=ALU.mult), reads=[rxT, rrstd, r_ada], writes=[rt])
                P.op("act", lambda g, kc=kc, tmp=tmp: g.activation(
                    hT[:, kc, 0:T], tmp[:, 0:T], AF.Identity, bias=Bcol(j, kc), scale=1.0),
                    reads=[rt, r_ada], writes=[rhT])

        fsplit = [(0, 6), (6, 12), (12, 17), (17, 22)]

        def load_gu(which, st, defer=False):
            wg = sbt(st, [128, 8, DFF], BF16)
            wu = sbt(st, [128, 8, DFF], BF16)
            rwg = [Res() for _ in fsplit]; rwu = [Res() for _ in fsplit]
            gv_ = w_gate[which].rearrange("(kc p) f -> p kc f", p=128)
            uv_ = w_up[which].rearrange("(kc p) f -> p kc f", p=128)
            def issue():
                for qi, (a, b) in enumerate(fsplit):
                    P.dma("pool", wg[:, :, a * 128:b * 128], gv_[:, :, a * 128:b * 128], writes=[rwg[qi]])
                    P.dma("pool", wu[:, :, a * 128:b * 128], uv_[:, :, a * 128:b * 128], writes=[rwu[qi]])
            if not defer:
                issue()
                return wg, wu, rwg, rwu
            return (wg, wu, rwg, rwu), issue

        def ffn_phase(which, pre=None):
            with ExitStack() as st:
                if pre is None:
                    pre = load_gu(which, st)
                wg, wu, rwg, rwu = pre
                wd = sbt(st, [128, NFC, D], BF16)
                rwd = [Res() for _ in range(NFC)]
                dv_ = w_down[which].rearrange("(fc p) n -> p fc n", p=128)
                import os
                for fc in range(0, NFC if not os.environ.get('SKIPW') else 0, 2):
                    P.dma("pool", wd[:, fc:fc + 2, :], dv_[:, fc:fc + 2, :], writes=[rwd[fc], rwd[fc + 1]])

                def q_of(f):
                    for qi, (a, b) in enumerate(fsplit):
                        if a <= f < b:
                            return qi
                xT = sbt(st, [128, 8, TF], F32); rxT = Res()
                hT = sbt(st, [128, 8, TF], BF16); rhT = Res()
                actT = sbt(st, [128, NFC, TF], BF16); ract = [Res() for _ in range(NFC)]
                sq = sbt(st, [128, 8, TF], BF16); rsq = Res()
                rstd = sbt(st, [128, TF], F32); rrstd = Res()
                tmp_ring = Ring([sbt(st, [128, TF], F32) for _ in range(2)])
                xtm_ring = Ring([sbt(st, [128, D], F32) for _ in range(2)])
                ps_ring = Ring([pst(st, [128, 512]) for _ in range(6)])
                pss = pst(st, [128, 512]); rpss = Res()
                jn = 0 if which == 0 else 2
                xloaded = {}

                def ensure_x(i):
                    if i < S // 128 and i < NTG * (TF // 128):
                        xt_, rxt_ = xtm_ring.next()
                        P.dma("sp", xt_[:], x_in[i * 128:(i + 1) * 128, :], writes=[rxt_])
                        xloaded[i] = (xt_, rxt_)
                import os
                NTG = int(os.environ.get('NTG', S // TF)); LVL = int(os.environ.get('LVL', 9))
                for tg in range(NTG):
                    t0 = tg * TF
                    if which == 0:
                        for tt in range(TF // 128):
                            gi = tg * (TF // 128) + tt
                            if gi == 0:
                                ensure_x(0); ensure_x(1)
                            xtm, rxtm = xloaded.pop(gi)
                            for half in range(2):
                                pt, rpt = ps_ring.next()
                                P.op("pe", [lambda g, kc=kc, pt=pt, xtm=xtm, half=half: g.transpose(
                                    pt[:, (kc - 4 * half) * 128:(kc - 4 * half + 1) * 128], xtm[:, kc * 128:(kc + 1) * 128], ident_f[:])
                                    for kc in range(4 * half, 4 * half + 4)], reads=[rxtm, r_const], writes=[rpt])
                                dst = xT[:, 4 * half:4 * half + 4, tt * 128:(tt + 1) * 128]
                                src = pt[:].rearrange("p (k t) -> p k t", k=4)
                                if half:
                                    P.op("act", lambda g, dst=dst, src=src: g.copy(dst, src), reads=[rpt], writes=[rxT])
                                else:
                                    P.op("dve", lambda g, dst=dst, src=src: g.tensor_copy(dst, src), reads=[rpt], writes=[rxT])
                            ensure_x(gi + 2)
                    else:
                        P.dma("sp", xT[:], X2T.rearrange("(kc p) t -> p kc t", p=128)[:, :, t0:t0 + TF],
                              reads=[rX2T[tg]], writes=[rxT])
                    if LVL < 1:
                        P.dma('sp', X1T.rearrange('(kc p) t -> p kc t', p=128)[:, :, t0:t0 + TF], xT[:], reads=[rxT], writes=[rX1T[tg]])
                        continue
                    modulate(jn, xT, rxT, hT, rhT, TF, sq, rsq, pss, rpss, rstd, rrstd, tmp_ring)
                    if LVL < 2:
                        P.dma('sp', H2T.rearrange('(kc p) t -> p kc t', p=128)[:, :, t0:t0 + TF], hT[:], reads=[rhT], writes=[rH2T[tg]])
                        continue
                    for f in range(NFC):
                        pg, rpg = ps_ring.next()
                        pu, rpu = ps_ring.next()
                        P.op("pe", [mm(pg[:, 0:TF], wg[:, kc, f * 128:(f + 1) * 128], hT[:, kc, :], kc == 0, kc == 7)
                                    for kc in range(8)], reads=[rhT, rwg[q_of(f)]], writes=[rpg])
                        P.op("pe", [mm(pu[:, 0:TF], wu[:, kc, f * 128:(f + 1) * 128], hT[:, kc, :], kc == 0, kc == 7)
                                    for kc in range(8)], reads=[rhT, rwu[q_of(f)]], writes=[rpu])
                        tmp, rt = tmp_ring.next()
                        P.op("act", lambda g, pg=pg, tmp=tmp: g.activation(tmp[:, 0:TF], pg[:, 0:TF], AF.Silu),
                             reads=[rpg], writes=[rt])
                        wr = [ract[f]]
                        P.op("dve", lambda g, pu=pu, tmp=tmp, f=f: g.tensor_tensor(
                            out=actT[:, f, :], in0=tmp[:, 0:TF], in1=pu[:, 0:TF], op=ALU.mult),
                            reads=[rt, rpu], writes=wr)
                    for n in range(8):
                        pd, rpd = ps_ring.next()
                        P.op("pe", [mm(pd[:, 0:TF], wd[:, f, n * 128:(n + 1) * 128], actT[:, f, :], f == 0, f == NFC - 1)
                                    for f in range(NFC)], reads=ract + rwd, writes=[rpd])
                        P.op("dve", lambda g, pd=pd, n=n: g.scalar_tensor_tensor(
                            out=xT[:, n, :], in0=pd[:, 0:TF], scalar=PG[:, jn, n:n + 1], in1=xT[:, n, :],
                            op0=ALU.mult, op1=ALU.add), reads=[rpd, r_ada, rxT], writes=[rxT])
                    if which == 0:
                        P.dma("sp", X1T.rearrange("(kc p) t -> p kc t", p=128)[:, :, t0:t0 + TF], xT[:],
                              reads=[rxT], writes=[rX1T[tg]])
                        modulate(1, xT, rxT, hT, rhT, TF, sq, rsq, pss, rpss, rstd, rrstd, tmp_ring)
                        P.dma("sp", H2T.rearrange("(kc p) t -> p kc t", p=128)[:, :, t0:t0 + TF], hT[:],
                              reads=[rhT], writes=[rH2T[tg]])
                    else:
                        fm_stats(xT[:, :, :], rxT, 8, TF, D, sq, rsq, pss, rpss, rstd, rrstd)
                        for kc in range(8):
                            P.op("dve", lambda g, kc=kc: g.scalar_tensor_tensor(
                                out=xT[:, kc, :], in0=xT[:, kc, :], scalar=gains[:, 3, kc:kc + 1], in1=rstd[:, :],
                                op0=ALU.mult, op1=ALU.mult), reads=[rxT, rrstd, r_const], writes=[rxT])
                        for tt in range(TF // 128):
                            xtm, rxtm = xtm_ring.next()
                            for half in range(2):
                                pt, rpt = ps_ring.next()
                                P.op("pe", [lambda g, kc=kc, pt=pt, half=half, tt=tt: g.transpose(
                                    pt[:, (kc - 4 * half) * 128:(kc - 4 * half + 1) * 128],
                                    xT[:, kc, tt * 128:(tt + 1) * 128], ident_f[:])
                                    for kc in range(4 * half, 4 * half + 4)], reads=[rxT, r_const], writes=[rpt])
                                if half:
                                    P.op("act", lambda g, pt=pt, xtm=xtm: g.copy(xtm[:, 512:1024], pt[:]),
                                         reads=[rpt], writes=[rxtm])
                                else:
                                    P.op("dve", lambda g, pt=pt, xtm=xtm: g.tensor_copy(xtm[:, 0:512], pt[:]),
                                         reads=[rpt], writes=[rxtm])
                            P.dma("sp", y_out[t0 + tt * 128:t0 + (tt + 1) * 128, :], xtm[:], reads=[rxtm], writes=[r_y])

        def rsq_guard(rsq, ract):
            for f in range(8):
                for s_, v_ in ract[f].r.items():
                    if rsq.r.get(s_, 0) < v_:
                        rsq.r[s_] = v_
                if ract[f].w is not None:
                    s_, v_ = ract[f].w
                    if rsq.r.get(s_, 0) < v_:
                        rsq.r[s_] = v_
            return rsq

        r_y = Res()

        ffn_phase(0)
        P.barrier()
        if upto == "A":
            P.final_wait("sp", rX1T + rH2T)
            P.emit(nc, sems)
            return nc

        QDT = dscr("QDT", [8, 128, S], BF16); rQDT = Res()
        AITs = dscr("AITs", [8, 64, S], BF16); rAIT = Res()
        ATs = dscr("ATs", [512, 4096], F32); rATs = [Res() for _ in range(8)]
        TTs = dscr("TTs", [512, 4096], BF16); rTTs = [Res() for _ in range(4)]
        KBDs = dscr("KBDs", [8, 64, 64 * 128], BF16); rKBD = Res()
        KDECs = dscr("KDECs", [8, 64, 64 * 128], BF16); rKDEC = Res()
        VBs = dscr("VBs", [8, 64, 64 * 128], BF16); rVB = Res()
        c_eyeb = din("c_eyeb", [128, 4096])

        def gdn_phase():
            with ExitStack() as gst:
                tri_f = sbt(gst, [64, 64], F32); eye_f = sbt(gst, [64, 64], F32); negm = sbt(gst, [64, 64], F32)
                ones_f = sbt(gst, [64, 128], F32); rcg = Res()
                P.dma("sp", tri_f[:], c_tri, writes=[rcg]); P.dma("sp", eye_f[:], c_eye, writes=[rcg]); P.dma("sp", negm[:], c_negm, writes=[rcg])
                P.op("dve", lambda g: g.memset(ones_f[:], 1.0), writes=[rcg])
                dec_tm = sbt(gst, [64, 64, 8], F32); s_kbd = sbt(gst, [64, 64, 8], F32); s_kdec = sbt(gst, [64, 64, 8], F32)
                s_vb = sbt(gst, [64, 64, 8], F32); cd_bc = sbt(gst, [128, 64, 8], F32); rdc = Res()
                ggdn = sbt(gst, [128, 1], F32); wcv = sbt(gst, [128, 24, 4], F32)
                P.dma("sp", ggdn[:], ggdn_fm, writes=[rcg]); P.dma("sp", wcv[:], wconv_fm, writes=[rcg])
                gflat = g_tm[:].rearrange("p n h -> p (n h)")
                fl = lambda t: t[:].rearrange("p n h -> p (n h)")
                with ExitStack() as st:
                    p1 = pst(st, [64, 512]); p2 = pst(st, [64, 512]); p3 = pst(st, [128, 512]); rp = Res()
                    P.op("pe", [mm(p1[:], tri_f[:], gflat, True, True), mm(p2[:], ones_f[:, 0:64], gflat, True, True),
                                mm(p3[:], ones_f[:, 0:128], gflat, True, True)], reads=[r_gl, rcg], writes=[rp])
                    P.op("dve", lambda g: g.tensor_copy(fl(dec_tm), p1[:]), reads=[rp], writes=[rdc])
                    P.op("dve", lambda g: g.tensor_tensor(out=fl(s_kbd), in0=p1[:], in1=fl(lnb_tm), op=ALU.add), reads=[rp, r_gl], writes=[rdc])
                    P.op("act", lambda g: g.activation(fl(s_kbd), fl(s_kbd), AF.Exp), reads=[rdc], writes=[rdc])
                    P.op("dve", lambda g: g.tensor_tensor(out=fl(s_kdec), in0=p2[:], in1=fl(dec_tm), op=ALU.subtract), reads=[rp, rdc], writes=[rdc])
                    P.op("act", lambda g: g.activation(fl(s_kdec), fl(s_kdec), AF.Exp), reads=[rdc], writes=[rdc])
                    P.op("act", lambda g: g.activation(fl(s_vb), fl(lnb_tm), AF.Exp), reads=[r_gl], writes=[rdc])
                    P.op("act", lambda g: g.activation(fl(cd_bc), p3[:], AF.Exp), reads=[rp], writes=[rdc])
                P.barrier()
                with ExitStack() as st:
                    xin_ring = Ring([sbt(st, [128, S + 4], BF16) for _ in range(2)])
                    y_ring = Ring([sbt(st, [128, S], F32) for _ in range(2)])
                    qT = sbt(st, [128, S], BF16); kT = sbt(st, [128, S], BF16); vT = sbt(st, [128, S], BF16)
                    rqT = Res(); rkT = Res(); rvT = Res()
                    QD = sbt(st, [128, S], BF16); rQD = Res()
                    AIT = sbt(st, [64, 64, 64], BF16); rAITt = Res()
                    KBD = sbt(st, [64, 64, 128], BF16); KDEC = sbt(st, [64, 64, 128], BF16); VB = sbt(st, [64, 64, 128], BF16)
                    rKBDt = Res(); rKDECt = Res(); rVBt = Res()
                    sqr = Ring([sbt(st, [128, 512], BF16) for _ in range(2)])
                    rsr = Ring([sbt(st, [128, 512], F32) for _ in range(2)])
                    gt1r = Ring([sbt(st, [64, 8, 64], F32) for _ in range(2)])
                    gt2r = Ring([sbt(st, [64, 8, 64], F32) for _ in range(2)])
                    tmr = Ring([sbt(st, [64, 8, 64], F32) for _ in range(2)])
                    l1r = Ring([sbt(st, [64, 8, 64], F32) for _ in range(2)])
                    l2r = Ring([sbt(st, [64, 8, 64], F32) for _ in range(2)])
                    edr = Ring([sbt(st, [128, 8, 64], F32) for _ in range(2)])
                    atr = Ring([sbt(st, [64, 8, 64], F32) for _ in range(2)])
                    psr = Ring([pst(st, [128, 512]) for _ in range(4)])
                    ptr_ = Ring([pst(st, [64, 1024], F32) for _ in range(2)])
                    for (xt_, rxt_) in xin_ring.t:
                        P.op("pool", lambda g, xt_=xt_: g.memset(xt_[:, 0:4], 0.0), writes=[rxt_])
                    dg_ring = Ring([sbt(st, [128, 4, 128], BF16) for _ in range(2)])
                    for h in range(8):
                        for which_, dstT, rdst in ((0, qT, rqT), (1, kT, rkT), (2, vT, rvT)):
                            ch = which_ * 8 + h
                            (xA, rxA), (xB, rxB) = xin_ring.t
                            y, ry = y_ring.next()
                            P.dma("sp", xA[:, 4:S + 4], GQKV[ch * 128:(ch + 1) * 128, :], reads=[rGQKV[ch]], writes=[rxA])
                            P.op("act", lambda g, xA=xA, xB=xB: g.copy(xB[:, 3:S + 3], xA[:, 4:S + 4]), reads=[rxA], writes=[rxB])
                            dg, rdg = dg_ring.next()
                            P.op("pool", lambda g, dg=dg, ch=ch: g.tensor_tensor(out=dg[:], in0=ident_b[:].unsqueeze(1).to_broadcast([128, 4, 128]),
                                 in1=wcv[:, ch, :].unsqueeze(2).to_broadcast([128, 4, 128]), op=ALU.mult), reads=[r_const, rcg], writes=[rdg])
                            for tg in range(8):
                                T0 = tg * 512
                                ps, rps = psr.next()
                                P.op("pe", [mm(ps[:], dg[:, 0, :], xB[:, T0:T0 + 512], True, False),
                                            mm(ps[:], dg[:, 1, :], xA[:, T0 + 2:T0 + 514], False, False),
                                            mm(ps[:], dg[:, 2, :], xB[:, T0 + 2:T0 + 514], False, False),
                                            mm(ps[:], dg[:, 3, :], xA[:, T0 + 4:T0 + 516], False, True)],
                                     reads=[rdg, rxA, rxB], writes=[rps])
                                if which_ == 2:
                                    P.op("act", lambda g, ps=ps, T0=T0: g.activation(vT[:, T0:T0 + 512], ps[:], AF.Silu), reads=[rps], writes=[rvT])
                                else:
                                    P.op("act", lambda g, ps=ps, T0=T0, y=y: g.activation(y[:, T0:T0 + 512], ps[:], AF.Silu), reads=[rps], writes=[ry])
                            if which_ != 2:
                                for tg in range(8):
                                    tsl = slice(tg * 512, (tg + 1) * 512)
                                    sq_, rsq_ = sqr.next(); rs_, rrs_ = rsr.next(); ps, rps = psr.next()
                                    P.op("act", lambda g, sq_=sq_, tsl=tsl, y=y: g.activation(sq_[:], y[:, tsl], AF.Square), reads=[ry], writes=[rsq_])
                                    P.op("pe", mm(ps[:], ones_b[:], sq_[:], True, True), reads=[rsq_, r_const], writes=[rps])
                                    P.op("act", lambda g, rs_=rs_, ps=ps: g.activation(rs_[:], ps[:], AF.Sqrt, scale=1.0, bias=EPS), reads=[rps], writes=[rrs_])
                                    P.op("dve", lambda g, rs_=rs_: g.reciprocal(rs_[:], rs_[:]), reads=[rrs_], writes=[rrs_])
                                    P.op("dve", lambda g, rs_=rs_, tsl=tsl, dstT=dstT, y=y, sc=(128.0 ** -0.5 if which_ == 0 else 1.0): g.scalar_tensor_tensor(
                                        out=dstT[:, tsl], in0=y[:, tsl], scalar=sc, in1=rs_[:], op0=ALU.mult, op1=ALU.mult), reads=[ry, rrs_], writes=[rdst])
                        for nb in range(8):
                            nsl = slice(nb * 8, (nb + 1) * 8)
                            tsl = slice(nb * 512, (nb + 1) * 512)
                            gbc = g_tm[:, nsl, h:h + 1].to_broadcast([64, 8, 64])
                            lbc = lnb_tm[:, nsl, h:h + 1].to_broadcast([64, 8, 64])
                            dbc = dec_tm[:, nsl, h:h + 1].to_broadcast([64, 8, 64])
                            gt1, rg1 = gt1r.next(); gt2, rg2 = gt2r.next()
                            P.op("pool", lambda g, gt1=gt1, gbc=gbc: g.tensor_tensor(out=gt1[:], in0=tri_f[:].unsqueeze(1).to_broadcast([64, 8, 64]), in1=gbc, op=ALU.mult),
                                 reads=[r_gl, rcg], writes=[rg1])
                            P.op("pool", lambda g, gt2=gt2, lbc=lbc: g.tensor_tensor(out=gt2[:], in0=eye_f[:].unsqueeze(1).to_broadcast([64, 8, 64]), in1=lbc, op=ALU.mult),
                                 reads=[r_gl, rcg], writes=[rg2])
                            P.op("pool", lambda g, gt1=gt1, gt2=gt2: g.tensor_tensor(out=gt2[:], in0=gt2[:], in1=gt1[:], op=ALU.add), reads=[rg1, rg2], writes=[rg2])
                            pd1, rpd1 = psr.next(); pd2, rpd2 = psr.next()
                            P.op("pe", mm(pd1[:], ones_f[:, 0:128], gt1[:].rearrange("p a b -> p (a b)"), True, True), reads=[rg1, rcg], writes=[rpd1])
                            P.op("pe", mm(pd2[0:64, :], ones_f[:, 0:64], gt2[:].rearrange("p a b -> p (a b)"), True, True), reads=[rg2, rcg], writes=[rpd2])
                            ed, red = edr.next()
                            P.op("act", lambda g, ed=ed, pd1=pd1: g.activation(ed[:].rearrange("p a b -> p (a b)"), pd1[:], AF.Exp), reads=[rpd1], writes=[red])
                            P.op("pool", lambda g, ed=ed, tsl=tsl: g.tensor_tensor(out=QD[:, tsl], in0=qT[:, tsl], in1=ed[:].rearrange("p a b -> p (a b)"), op=ALU.mult),
                                 reads=[rqT, red], writes=[rQD])
                            l1, rl1 = l1r.next(); l2, rl2 = l2r.next()
                            for (pd, rpd, ll, rll) in ((pd1, rpd1, l1, rl1), (pd2, rpd2, l2, rl2)):
                                tm_, rtm_ = tmr.next()
                                P.op("dve", lambda g, tm_=tm_, pd=pd, dbc=dbc: g.tensor_tensor(out=tm_[:], in0=pd[0:64, :].rearrange("p (a b) -> p a b", a=8), in1=dbc, op=ALU.subtract),
                                     reads=[rpd, rdc], writes=[rtm_])
                                P.op("pool", lambda g, tm_=tm_: g.tensor_tensor(out=tm_[:], in0=tm_[:], in1=negm[:].unsqueeze(1).to_broadcast([64, 8, 64]), op=ALU.add),
                                     reads=[rtm_, rcg], writes=[rtm_])
                                P.op("act", lambda g, tm_=tm_, ll=ll: g.activation(ll[:], tm_[:], AF.Exp), reads=[rtm_], writes=[rll])
                            pkq, rpkq = psr.next(); pkk, rpkk = psr.next()
                            P.op("pe", [mm(pkq[0:64, n * 64:(n + 1) * 64], kT[:, nb * 512 + n * 64:nb * 512 + (n + 1) * 64], qT[:, nb * 512 + n * 64:nb * 512 + (n + 1) * 64], True, True)
                                        for n in range(8)], reads=[rkT, rqT], writes=[rpkq])
                            P.op("pe", [mm(pkk[0:64, n * 64:(n + 1) * 64], kT[:, nb * 512 + n * 64:nb * 512 + (n + 1) * 64], kT[:, nb * 512 + n * 64:nb * 512 + (n + 1) * 64], True, True)
                                        for n in range(8)], reads=[rkT], writes=[rpkk])
                            P.op("dve", lambda g, pkq=pkq, l1=l1, nsl=nsl: g.tensor_tensor(out=AIT[:, nsl, :], in0=pkq[0:64, :].rearrange("p (a b) -> p a b", a=8), in1=l1[:], op=ALU.mult),
                                 reads=[rpkq, rl1], writes=[rAITt])
                            at_, rat_ = atr.next()
                            P.op("dve", lambda g, pkk=pkk, l2=l2, at_=at_: g.tensor_tensor(out=at_[:], in0=pkk[0:64, :].rearrange("p (a b) -> p a b", a=8), in1=l2[:], op=ALU.mult),
                                 reads=[rpkk, rl2], writes=[rat_])
                            P.dma("sp", ATs[h * 64 + nb * 8:h * 64 + nb * 8 + 8, :].rearrange("n (j i) -> j n i", j=64), at_[:], reads=[rat_], writes=[rATs[h]])
                            pk_, rpk_ = ptr_.next(); pv_, rpv_ = ptr_.next()
                            P.op("pe", [mm(pk_[:, n * 128:(n + 1) * 128], kT[:, nb * 512 + n * 64:nb * 512 + (n + 1) * 64], ident_b[:], True, True)
                                        for n in range(8)], reads=[rkT, r_const], writes=[rpk_])
                            P.op("pe", [mm(pv_[:, n * 128:(n + 1) * 128], vT[:, nb * 512 + n * 64:nb * 512 + (n + 1) * 64], ident_b[:], True, True)
                                        for n in range(8)], reads=[rvT, r_const], writes=[rpv_])
                            for (pp, rpp, dst_, rdst_, sc_) in ((pk_, rpk_, KBD, rKBDt, s_kbd), (pk_, rpk_, KDEC, rKDECt, s_kdec), (pv_, rpv_, VB, rVBt, s_vb)):
                                P.op("dve", lambda g, pp=pp, dst_=dst_, sc_=sc_, nsl=nsl, h=h: g.tensor_tensor(
                                    out=dst_[:, nsl, :], in0=pp[:].rearrange("p (a b) -> p a b", a=8), in1=sc_[:, nsl, h:h + 1].to_broadcast([64, 8, 128]), op=ALU.mult),
                                    reads=[rpp, rdc], writes=[rdst_])
                        P.dma("sp", QDT[h], QD[:], reads=[rQD], writes=[rQDT])
                        P.dma("sp", AITs[h], AIT[:].rearrange("p a b -> p (a b)"), reads=[rAITt], writes=[rAIT])
                        P.dma("sp", KBDs[h], KBD[:].rearrange("p a b -> p (a b)"), reads=[rKBDt], writes=[rKBD])
                        P.dma("sp", KDECs[h], KDEC[:].rearrange("p a b -> p (a b)"), reads=[rKDECt], writes=[rKDEC])
                        P.dma("sp", VBs[h], VB[:].rearrange("p a b -> p (a b)"), reads=[rVBt], writes=[rVB])
                P.barrier()
                if upto == "D1":
                    return True
                with ExitStack() as st:
                    sets = [(sbt(st, [128, 64, 64], F32), sbt(st, [128, 64, 64], F32), sbt(st, [128, 64, 64], F32), sbt(st, [128, 4096], BF16),
                             Res(), Res(), Res()) for _ in range(2)]
                    for pp_ in range(2):
                        for gi in range(2):
                            pg = pp_ * 2 + gi
                            A_, X_, tmp_, Xb, rA, rX, rT = sets[gi]
                            P.dma("sp", A_[:].rearrange("p a b -> p (a b)"), ATs[pg * 128:(pg + 1) * 128, :], reads=[rATs[2 * pg], rATs[2 * pg + 1]], writes=[rA])
                            P.dma("sp", X_[:].rearrange("p a b -> p (a b)"), c_eyeb, writes=[rX])
                        for j in range(62, -1, -1):
                            m = 63 - j
                            for gi in range(2):
                                A_, X_, tmp_, Xb, rA, rX, rT = sets[gi]
                                P.op("dve" if gi == 0 else D2MUL, lambda g, j=j, m=m, A_=A_, X_=X_, tmp_=tmp_: g.tensor_tensor(
                                    out=tmp_[:, 0:m, 0:m], in0=X_[:, j + 1:64, j + 1:64].rearrange("p i c -> p c i"),
                                    in1=A_[:, j, j + 1:64].unsqueeze(1).to_broadcast([128, m, m]), op=ALU.mult), reads=[rA, rX], writes=[rT])
                            for gi in range(2):
                                A_, X_, tmp_, Xb, rA, rX, rT = sets[gi]
                                P.op("dve", lambda g, j=j, m=m, X_=X_, tmp_=tmp_: g.tensor_reduce(
                                    out=X_[:, j, j + 1:64], in_=tmp_[:, 0:m, 0:m], axis=AX.X, op=ALU.add, negate=True), reads=[rT], writes=[rX])
                        for gi in range(2):
                            pg = pp_ * 2 + gi
                            A_, X_, tmp_, Xb, rA, rX, rT = sets[gi]
                            P.op("act", lambda g, X_=X_, Xb=Xb: g.copy(Xb[:], X_[:].rearrange("p a b -> p (a b)")), reads=[rX], writes=[rX])
                            P.dma("sp", TTs[pg * 128:(pg + 1) * 128, :], Xb[:], reads=[rX], writes=[rTTs[pg]])
                P.barrier()
                if upto == "D2":
                    return True
                with ExitStack() as st:
                    TTb = sbt(st, [64, 8, 8, 64], BF16); KBDb = sbt(st, [64, 8, 8, 128], BF16)
                    VBb = sbt(st, [64, 8, 8, 128], BF16)
                    kd_ring = Ring([sbt(st, [64, 8, 8, 128], BF16) for _ in range(2)])
                    qd_ring = Ring([sbt(st, [128, 8, 512], BF16) for _ in range(2)])
                    ai_ring = Ring([sbt(st, [64, 8, 8, 64], BF16) for _ in range(2)])
                    ZTb = sbt(st, [128, 8, 512], BF16); WTb = sbt(st, [128, 8, 8, 64], BF16); Ub = sbt(st, [64, 8, 8, 128], BF16)
                    Oraw = sbt(st, [128, 8, 512], F32); OG = sbt(st, [128, 8, 512], BF16)
                    Sf = sbt(st, [128, 8, 128], F32); Sb = sbt(st, [128, 8, 128], BF16); vnew = sbt(st, [64, 8, 128], BF16)
                    sq_ = sbt(st, [128, 512], BF16); rs_ = sbt(st, [128, 512], F32); t_ = sbt(st, [128, 512], F32)
                    rTTb = Res(); rKBDb = Res(); rKDECb = Res(); rVBb = Res(); rQDb = Res(); rAITb = Res(); rZTb = Res()
                    rWTb = Res(); rUb = Res(); rOraw = Res(); rOG = Res(); rS = Res(); rSb = Res(); rvn = Res(); rsq_ = Res(); rrs_ = Res(); rt_ = Res()
                    wps = pst(st, [128, 512]); ups = pst(st, [64, 1024]); wsps = pst(st, [64, 1024]); ops_ = pst(st, [128, 512]); dsps = pst(st, [128, 1024])
                    rwps = Res(); rups = Res(); rwsps = Res(); rops = Res(); rdsps = Res()
                    rSh = [Res(), Res()]; rSbh = [Res(), Res()]; rwsh = [Res(), Res()]; rvnh = [Res(), Res()]; _ro = Res(); ropsh = [_ro, _ro]; rdsh = [Res(), Res()]
                    P.op("dve", lambda g: g.memset(Sf[:], 0.0), writes=rSh)
                    P.op("dve", lambda g: g.memset(Sb[:], 0.0), writes=rSbh)
                    for nb in range(8):
                        tsl = slice(nb * 512, (nb + 1) * 512)
                        for h in range(8):
                            P.dma("sp", TTb[:, h, :, :], TTs[h * 64 + nb * 8:h * 64 + nb * 8 + 8, :].rearrange("n (j i) -> j n i", j=64),
                                  reads=[rTTs[h // 2]], writes=[rTTb])
                        P.dma("sp", KBDb[:], KBDs.rearrange("h t (n d) -> t h n d", d=128)[:, :, nb * 8:(nb + 1) * 8, :], reads=[rKBD], writes=[rKBDb])
                        P.dma("sp", VBb[:], VBs.rearrange("h t (n d) -> t h n d", d=128)[:, :, nb * 8:(nb + 1) * 8, :], reads=[rVB], writes=[rVBb])
                        KDECb, rKDECb = kd_ring.next(); QDb, rQDb = qd_ring.next(); AITb, rAITb = ai_ring.next()
                        P.dma("sp", KDECb[:], KDECs.rearrange("h t (n d) -> t h n d", d=128)[:, :, nb * 8:(nb + 1) * 8, :], reads=[rKDEC], writes=[rKDECb])
                        P.dma("sp", QDb[:], QDT.rearrange("h d t -> d h t")[:, :, tsl], reads=[rQDT], writes=[rQDb])
                        P.dma("sp", AITb[:], AITs.rearrange("h j (n i) -> j h n i", i=64)[:, :, nb * 8:(nb + 1) * 8, :], reads=[rAIT], writes=[rAITb])
                        P.dma("sp", ZTb[:], ZT.rearrange("(h p) t -> p h t", p=128)[:, :, tsl], reads=[rZT], writes=[rZTb])
                        for h in range(8):
                            P.op("pe", [mm(wps[:, n * 64:(n + 1) * 64], KBDb[:, h, n, :], TTb[:, h, n, :], True, True) for n in range(8)],
                                 reads=[rKBDb, rTTb], writes=[rwps])
                            P.op("act", lambda g, h=h: g.copy(WTb[:, h, :, :], wps[:].rearrange("p (a b) -> p a b", a=8)), reads=[rwps], writes=[rWTb])
                            P.op("pe", [mm(ups[:, n * 128:(n + 1) * 128], TTb[:, h, n, :], VBb[:, h, n, :], True, True) for n in range(8)],
                                 reads=[rVBb, rTTb], writes=[rups])
                            P.op("dve", lambda g, h=h: g.tensor_copy(Ub[:, h, :, :], ups[:].rearrange("p (a b) -> p a b", a=8)), reads=[rups], writes=[rUb])
                        for n in range(8):
                            ng = nb * 8 + n
                            for hg in range(2):
                                hs = range(4 * hg, 4 * hg + 4)
                                P.op("pe", [mm(wsps[:, h * 128:(h + 1) * 128], WTb[:, h, n, :], Sb[:, h, :], True, True) for h in hs],
                                     reads=[rWTb, rSbh[hg]], writes=[rwsh[hg]])
                            for hg in range(2):
                                hsl = slice(4 * hg, 4 * hg + 4)
                                P.op("dve", lambda g, n=n, hsl=hsl, hg=hg: g.tensor_tensor(out=vnew[:, hsl, :], in0=Ub[:, hsl, n, :],
                                     in1=wsps[:, hg * 512:(hg + 1) * 512].rearrange("p (a b) -> p a b", a=4), op=ALU.subtract),
                                     reads=[rUb, rwsh[hg]], writes=[rvnh[hg]])
                            for hg in range(2):
                                hs = range(4 * hg, 4 * hg + 4)
                                fns = []
                                for h in hs:
                                    fns.append(mm(ops_[:, h * 64:(h + 1) * 64], Sb[:, h, :], QDb[:, h, n * 64:(n + 1) * 64], True, False))
                                    fns.append(mm(ops_[:, h * 64:(h + 1) * 64], vnew[:, h, :], AITb[:, h, n, :], False, True))
                                P.op("pe", fns, reads=[rSbh[hg], rQDb, rvnh[hg], rAITb], writes=[ropsh[hg]])
                                P.op("pe", [mm(dsps[:, h * 128:(h + 1) * 128], KDECb[:, h, n, :], vnew[:, h, :], True, True) for h in hs],
                                     reads=[rKDECb, rvnh[hg]], writes=[rdsh[hg]])
                            for hg in range(2):
                                hsl = slice(4 * hg, 4 * hg + 4)
                                P.op("act", lambda g, n=n, hsl=hsl, hg=hg: g.copy(Oraw[:, hsl, n * 64:(n + 1) * 64],
                                     ops_[:, hg * 256:(hg + 1) * 256].rearrange("p (a b) -> p a b", a=4)), reads=[ropsh[hg]], writes=[rOraw])
                                P.op("dve", lambda g, ng=ng, hsl=hsl: g.tensor_tensor(out=Sf[:, hsl, :], in0=Sf[:, hsl, :],
                                     in1=cd_bc[:, ng, hsl].unsqueeze(2).to_broadcast([128, 4, 128]), op=ALU.mult), reads=[rSh[hg], rdc], writes=[rSh[hg]])
                                P.op("dve", lambda g, hsl=hsl, hg=hg: g.tensor_tensor(out=Sf[:, hsl, :], in0=Sf[:, hsl, :],
                                     in1=dsps[:, hg * 512:(hg + 1) * 512].rearrange("p (a b) -> p a b", a=4), op=ALU.add), reads=[rSh[hg], rdsh[hg]], writes=[rSh[hg]])
                                P.op("act", lambda g, hsl=hsl: g.copy(Sb[:, hsl, :], Sf[:, hsl, :]), reads=[rSh[hg]], writes=[rSbh[hg]])
                        for h in range(8):
                            P.op("act", lambda g, h=h: g.activation(sq_[:], Oraw[:, h, :], AF.Square), reads=[rOraw], writes=[rsq_])
                            P.op("pe", mm(wps[:], ones_b[:], sq_[:], True, True), reads=[rsq_, r_const], writes=[rwps])
                            P.op("act", lambda g: g.activation(rs_[:], wps[:], AF.Sqrt, scale=1.0 / 128, bias=EPS), reads=[rwps], writes=[rrs_])
                            P.op("dve", lambda g: g.reciprocal(rs_[:], rs_[:]), reads=[rrs_], writes=[rrs_])
                            P.op("dve", lambda g, h=h: g.scalar_tensor_tensor(out=t_[:], in0=Oraw[:, h, :], scalar=ggdn[:, 0:1], in1=rs_[:], op0=ALU.mult, op1=ALU.mult),
                                 reads=[rOraw, rrs_, rcg], writes=[rt_])
                            P.op("dve", lambda g, h=h: g.tensor_tensor(out=OG[:, h, :], in0=t_[:], in1=ZTb[:, h, :], op=ALU.mult), reads=[rt_, rZTb], writes=[rOG])
                        P.dma("sp", OGDN.rearrange("(h p) t -> p h t", p=128)[:, :, tsl], OG[:], reads=[rOG], writes=[rOGDN])
                P.barrier()

        with ExitStack() as mid:
            KR = sbt(mid, [64, S], BF16); rKR = Res()
            g_tm = sbt(mid, [64, 64, 8], F32); lnb_tm = sbt(mid, [64, 64, 8], F32); r_gl = Res()
            bc = ExitStack()
            cos_sb = sbt(bc, [64, S], F32); sin_sb = sbt(bc, [64, S], F32); r_cs = Res()
            P.dma("sp", cos_sb[:], cosT, writes=[r_cs])
            P.dma("sp", sin_sb[:], sinS, writes=[r_cs])

            with ExitStack() as st:
                h2 = sbt(st, [128, 8, S], BF16); rh2 = [Res() for _ in range(8)]
                H2v = H2T.rearrange("(kc p) t -> p kc t", p=128)
                for tg in range(8):
                    P.dma("sp", h2[:, :, tg * 512:(tg + 1) * 512], H2v[:, :, tg * 512:(tg + 1) * 512],
                          reads=[rH2T[tg]], writes=[rh2[tg]])
                wring = Ring([sbt(st, [128, 8, 512], BF16) for _ in range(2)])
                oring = Ring([sbt(st, [128, S], BF16) for _ in range(2)])
                psr = Ring([pst(st, [128, 512]) for _ in range(4)])
                winv = w_in.rearrange("(kc p) n -> p kc n", p=128)
                jobs = [(0, 512, QLAT, 0, None, [rQLAT] * 4), (512, 256, QLAT, 512, None, [rQLAT] * 2),
                        (768, 256, KVLAT, 0, None, [rKVLAT] * 2)]
                for i in range(6):
                    jobs.append((1088 + i * 512, 512, GQKV, i * 512, None, rGQKV[i * 4:(i + 1) * 4]))
                for i in range(2):
                    jobs.append((4176 + i * 512, 512, ZT, i * 512, AF.Silu, [rZT] * 4))
                for i in range(4):
                    jobs.append((5200 + i * 512, 512, GATES, i * 512, AF.Sigmoid, [rGATES] * 4))
                ev = 0
                for (c0, W, dst, r0, func, rds) in jobs:
                    wb, rw = wring.next()
                    P.dma("pool", wb[:, :, 0:W], winv[:, :, c0:c0 + W], writes=[rw])
                    for cc in range(W // 128):
                        ot, rot = oring.next()
                        for tg in range(8):
                            ps, rps = psr.next()
                            P.op("pe", [mm(ps[:], wb[:, kc, cc * 128:(cc + 1) * 128], h2[:, kc, tg * 512:(tg + 1) * 512],
                                           kc == 0, kc == 7) for kc in range(8)], reads=[rw, rh2[tg]], writes=[rps])
                            dsl = ot[:, tg * 512:(tg + 1) * 512]
                            if func is not None:
                                P.op("act", lambda g, dsl=dsl, ps=ps, func=func: g.activation(dsl, ps[:], func),
                                     reads=[rps], writes=[rot])
                            elif ev % 2 == 0:
                                P.op("act", lambda g, dsl=dsl, ps=ps: g.copy(dsl, ps[:]), reads=[rps], writes=[rot])
                            else:
                                P.op("dve", lambda g, dsl=dsl, ps=ps: g.tensor_copy(dsl, ps[:]), reads=[rps], writes=[rot])
                            ev += 1
                        P.dma("sp", dst[r0 + cc * 128:r0 + (cc + 1) * 128, :], ot[:], reads=[rot], writes=[rds[cc]])
                wkp = sbt(st, [128, 8, 64], BF16); wkps = sbt(st, [128, 8, 64], BF16); rwk = Res()
                P.dma("pool", wkp[:], winv[:, :, 1024:1088], writes=[rwk])
                P.dma("pool", wkps[:, :, 0:32], winv[:, :, 1056:1088], writes=[rwk])
                P.dma("pool", wkps[:, :, 32:64], winv[:, :, 1024:1056], writes=[rwk])
                t1r = Ring([sbt(st, [64, 512], F32) for _ in range(2)])
                t2r = Ring([sbt(st, [64, 512], F32) for _ in range(2)])
                for tg in range(8):
                    pa, rpa = psr.next(); pb, rpb = psr.next()
                    tsl = slice(tg * 512, (tg + 1) * 512)
                    P.op("pe", [mm(pa[0:64, :], wkp[:, kc, :], h2[:, kc, tsl], kc == 0, kc == 7) for kc in range(8)],
                         reads=[rwk, rh2[tg]], writes=[rpa])
                    P.op("pe", [mm(pb[0:64, :], wkps[:, kc, :], h2[:, kc, tsl], kc == 0, kc == 7) for kc in range(8)],
                         reads=[rwk, rh2[tg]], writes=[rpb])
                    t1, rt1 = t1r.next(); t2, rt2 = t2r.next()
                    P.op("dve", lambda g, t1=t1, pa=pa, tsl=tsl: g.tensor_tensor(out=t1[:], in0=pa[0:64, :], in1=cos_sb[:, tsl], op=ALU.mult),
                         reads=[rpa, r_cs], writes=[rt1])
                    P.op("dve", lambda g, t2=t2, pb=pb, tsl=tsl: g.tensor_tensor(out=t2[:], in0=pb[0:64, :], in1=sin_sb[:, tsl], op=ALU.mult),
                         reads=[rpb, r_cs], writes=[rt2])
                    P.op("dve", lambda g, t1=t1, t2=t2, tsl=tsl: g.tensor_tensor(out=KR[:, tsl], in0=t1[:], in1=t2[:], op=ALU.add),
                         reads=[rt1, rt2], writes=[rKR])
                wab = sbt(st, [128, 8, 16], BF16); rwab = Res()
                P.dma("pool", wab[:], winv[:, :, 4160:4176], writes=[rwab])
                alog_sb = sbt(st, [64, 8], F32); dtb_sb = sbt(st, [64, 8], F32); r_ad = Res()
                P.dma("sp", alog_sb[:], alog_bc, writes=[r_ad])
                P.dma("sp", dtb_sb[:], dtb_bc, writes=[r_ad])
                abps = pst(st, [64, 1024]); rab = Res()
                for n in range(64):
                    P.op("pe", [mm(abps[:, n * 16:(n + 1) * 16], h2[:, kc, n * 64:(n + 1) * 64], wab[:, kc, :], kc == 0, kc == 7)
                                for kc in range(8)], reads=[rwab, rh2[n // 8]], writes=[rab])
                abv = abps[:].rearrange("p (n c) -> p n c", c=16)
                tA = sbt(st, [64, 64, 8], F32); tB = sbt(st, [64, 64, 8], F32); rtA = Res(); rtB = Res()
                P.op("dve", lambda g: g.tensor_tensor(out=tA[:], in0=abv[:, :, 0:8], in1=dtb_sb[:].unsqueeze(1).to_broadcast([64, 64, 8]), op=ALU.add),
                     reads=[rab, r_ad], writes=[rtA])
                P.op("act", lambda g: g.activation(tA[:], tA[:], AF.Exp), reads=[rtA], writes=[rtA])
                P.op("act", lambda g: g.activation(tA[:], tA[:], AF.Ln, bias=1.0), reads=[rtA], writes=[rtA])
                P.op("act", lambda g: g.activation(alog_sb[:], alog_sb[:], AF.Exp), reads=[r_ad], writes=[r_ad])
                P.op("dve", lambda g: g.tensor_scalar(out=alog_sb[:], in0=alog_sb[:], scalar1=-1.0, scalar2=None, op0=ALU.mult),
                     reads=[r_ad], writes=[r_ad])
                P.op("dve", lambda g: g.tensor_tensor(out=g_tm[:], in0=tA[:], in1=alog_sb[:].unsqueeze(1).to_broadcast([64, 64, 8]), op=ALU.mult),
                     reads=[rtA, r_ad], writes=[r_gl])
                P.op("act", lambda g: g.activation(tB[:], abv[:, :, 8:16], AF.Exp, scale=-1.0), reads=[rab], writes=[rtB])
                P.op("act", lambda g: g.activation(tB[:], tB[:], AF.Ln, bias=1.0), reads=[rtB], writes=[rtB])
                P.op("dve", lambda g: g.tensor_scalar(out=lnb_tm[:], in0=tB[:], scalar1=-1.0, scalar2=None, op0=ALU.mult),
                     reads=[rtB], writes=[r_gl])
            P.barrier()
            if upto == "B":
                if dbg:
                    kro = nc.dram_tensor("KR_o", [64, S], BF16, kind="ExternalOutput").ap()
                    gto = nc.dram_tensor("g_o", [64, 512], F32, kind="ExternalOutput").ap()
                    lbo = nc.dram_tensor("lnb_o", [64, 512], F32, kind="ExternalOutput").ap()
                    P.dma("sp", kro, KR[:], reads=[rKR], writes=[rKR])
                    P.dma("sp", gto, g_tm[:].rearrange("p n h -> p (n h)"), reads=[r_gl], writes=[r_gl])
                    P.dma("sp", lbo, lnb_tm[:].rearrange("p n h -> p (n h)"), reads=[r_gl], writes=[r_gl])
                P.final_wait("sp", [rQLAT, rKVLAT, rZT, rGATES, rKR, r_gl] + rGQKV)
                bc.close()
                P.emit(nc, sems)
                return nc

            with ExitStack() as st:
                qn = sbt(st, [128, 6, S], BF16); rqn = Res()
                kvn = sbt(st, [128, 2, S], BF16); rkvn = Res()
                for kc in range(6):
                    P.dma("sp", qn[:, kc, :], QLAT[kc * 128:(kc + 1) * 128, :], reads=[rQLAT], writes=[rqn])
                for kc in range(2):
                    P.dma("sp", kvn[:, kc, :], KVLAT[kc * 128:(kc + 1) * 128, :], reads=[rKVLAT], writes=[rkvn])
                gq_sb = sbt(st, [128, 6], F32); gkv_sb = sbt(st, [128, 2], F32); rgg = Res()
                P.dma("sp", gq_sb[:], gq_fm, writes=[rgg])
                P.dma("sp", gkv_sb[:], gkv_fm, writes=[rgg])
                wuq = sbt(st, [128, 6, 1536], BF16); wuqs = sbt(st, [128, 6, 8, 64], BF16); wukv = sbt(st, [128, 2, 2048], BF16)
                rwq = Res()
                wuqv = w_uq.rearrange("(kc p) n -> p kc n", p=128)
                wuq4 = w_uq.rearrange("(kc p) (h c) -> p kc h c", p=128, c=192)
                for kc in range(6):
                    P.dma("pool", wuq[:, kc, :], wuqv[:, kc, :], writes=[rwq])
                    P.dma("pool", wuqs[:, kc, :, 0:32], wuq4[:, kc, :, 160:192], writes=[rwq])
                    P.dma("pool", wuqs[:, kc, :, 32:64], wuq4[:, kc, :, 128:160], writes=[rwq])
                P.dma("pool", wukv[:], w_ukv.rearrange("(kc p) n -> p kc n", p=128), writes=[rwq])
                sq = sbt(st, [128, 8, 512], BF16); rsq = Res()
                rstd = sbt(st, [128, 512], F32); rrstd = Res()
                psr = Ring([pst(st, [128, 512]) for _ in range(4)])
                pacc = Ring([pst(st, [128, 512]) for _ in range(4)])
                for tg in range(8):
                    tsl = slice(tg * 512, (tg + 1) * 512)
                    for (src, rsrc, nkc, dim, gsb) in ((qn, rqn, 6, 768, gq_sb), (kvn, rkvn, 2, 256, gkv_sb)):
                        pss, rpss = psr.next()
                        fm_stats(src[:, :, tsl], rsrc, nkc, 512, dim, sq, rsq, pss, rpss, rstd, rrstd)
                        for kc in range(nkc):
                            P.op("dve", lambda g, src=src, kc=kc, gsb=gsb, tsl=tsl: g.scalar_tensor_tensor(
                                out=src[:, kc, tsl], in0=src[:, kc, tsl], scalar=gsb[:, kc:kc + 1], in1=rstd[:],
                                op0=ALU.mult, op1=ALU.mult), reads=[rsrc, rrstd, rgg], writes=[rsrc])
                QN = sbt(st, [128, S], BF16); rQN = Res()
                QR = sbt(st, [64, S], BF16); rQR = Res()
                KN = sbt(st, [128, S], BF16); rKN = Res()
                Vh = sbt(st, [128, 32, 128], BF16); rV = Res()
                OH = sbt(st, [128, S], BF16); rOH = Res()
                t1r = Ring([sbt(st, [64, 512], F32) for _ in range(2)])
                t2r = Ring([sbt(st, [64, 512], F32) for _ in range(2)])
                ptr = Ring([sbt(st, [128, 512], BF16) for _ in range(4)])
                rden_sb = sbt(st, [128, 512], F32); rrd = Res()
                for h in range(8):
                    for tg in range(8):
                        tsl = slice(tg * 512, (tg + 1) * 512)
                        ps, rps = psr.next()
                        P.op("pe", [mm(ps[:], wuq[:, kc, h * 192:h * 192 + 128], qn[:, kc, tsl], kc == 0, kc == 5) for kc in range(6)],
                             reads=[rwq, rqn], writes=[rps])
                        P.op("act", lambda g, ps=ps, tsl=tsl: g.activation(QN[:, tsl], ps[:], AF.Copy, scale=QSCALE), reads=[rps], writes=[rQN])
                        pa, rpa = psr.next(); pb, rpb = psr.next()
                        P.op("pe", [mm(pa[0:64, :], wuq[:, kc, h * 192 + 128:h * 192 + 192], qn[:, kc, tsl], kc == 0, kc == 5) for kc in range(6)],
                             reads=[rwq, rqn], writes=[rpa])
                        P.op("pe", [mm(pb[0:64, :], wuqs[:, kc, h, :], qn[:, kc, tsl], kc == 0, kc == 5) for kc in range(6)],
                             reads=[rwq, rqn], writes=[rpb])
                        t1, rt1 = t1r.next(); t2, rt2 = t2r.next()
                        P.op("dve", lambda g, t1=t1, pa=pa, tsl=tsl: g.scalar_tensor_tensor(out=t1[:], in0=pa[0:64, :], scalar=QSCALE, in1=cos_sb[:, tsl], op0=ALU.mult, op1=ALU.mult),
                             reads=[rpa, r_cs], writes=[rt1])
                        P.op("dve", lambda g, t2=t2, pb=pb, tsl=tsl: g.scalar_tensor_tensor(out=t2[:], in0=pb[0:64, :], scalar=QSCALE, in1=sin_sb[:, tsl], op0=ALU.mult, op1=ALU.mult),
                             reads=[rpb, r_cs], writes=[rt2])
                        P.op("dve", lambda g, t1=t1, t2=t2, tsl=tsl: g.tensor_tensor(out=QR[:, tsl], in0=t1[:], in1=t2[:], op=ALU.add),
                             reads=[rt1, rt2], writes=[rQR])
                        pk, rpk = psr.next()
                        P.op("pe", [mm(pk[:], wukv[:, kc, h * 256:h * 256 + 128], kvn[:, kc, tsl], kc == 0, kc == 1) for kc in range(2)],
                             reads=[rwq, rkvn], writes=[rpk])
                        P.op("act", lambda g, pk=pk, tsl=tsl: g.copy(KN[:, tsl], pk[:]), reads=[rpk], writes=[rKN])
                        pv, rpv = psr.next()
                        fns = []
                        for tt in range(4):
                            for kc in range(2):
                                fns.append(mm(pv[:, tt * 128:(tt + 1) * 128], kvn[:, kc, (tg * 4 + tt) * 128:(tg * 4 + tt + 1) * 128],
                                              wukv[:, kc, h * 256 + 128:h * 256 + 256], kc == 0, kc == 1))
                        P.op("pe", fns, reads=[rwq, rkvn], writes=[rpv])
                        P.op("dve", lambda g, pv=pv, tg=tg: g.tensor_copy(Vh[:, tg * 4:(tg + 1) * 4, :], pv[:].rearrange("p (a b) -> p a b", a=4)),
                             reads=[rpv], writes=[rV])
                    for gq_ in range(8):
                        q0 = gq_ * 512
                        nj = 4 * gq_ + 4
                        pden, rpden = pacc.next(); po, rpo = pacc.next()
                        pendq = []

                        def emit_acc(j, c0, pt, rpt):
                            P.op("pe", [mm(pden[:, c0:512], ones_b[:], pt[:, c0:512], j == 0, j == nj - 1),
                                        mm(po[:, c0:512], Vh[:, j, :], pt[:, c0:512], j == 0, j == nj - 1)],
                                 reads=[rpt, rV, r_const], writes=[rpden, rpo])
                        for j in range(nj):
                            c0 = max(0, (j - 4 * gq_) * 128)
                            ps, rps = psr.next()
                            P.op("pe", [mm(ps[:, c0:512], KN[:, j * 128:(j + 1) * 128], QN[:, q0 + c0:q0 + 512], True, False),
                                        mm(ps[:, c0:512], KR[:, j * 128:(j + 1) * 128], QR[:, q0 + c0:q0 + 512], False, True)],
                                 reads=[rKN, rQN, rKR, rQR], writes=[rps])
                            pt, rpt = ptr.next()
                            P.op("act", lambda g, pt=pt, ps=ps, c0=c0: g.activation(pt[:, c0:512], ps[:, c0:512], AF.Exp), reads=[rps], writes=[rpt])
                            if j >= 4 * gq_:
                                P.op("dve", lambda g, pt=pt, c0=c0: g.memset(pt[64:128, c0:c0 + 64], 0.0), reads=[rpt], writes=[rpt])
                            pendq.append((j, c0, pt, rpt))
                            if len(pendq) > 2:
                                emit_acc(*pendq.pop(0))
                        while pendq:
                            emit_acc(*pendq.pop(0))
                        P.op("dve", lambda g, pden=pden: g.reciprocal(rden_sb[:], pden[:]), reads=[rpden], writes=[rrd])
                        P.op("dve", lambda g, po=po, q0=q0: g.tensor_tensor(out=OH[:, q0:q0 + 512], in0=po[:], in1=rden_sb[:], op=ALU.mult),
                             reads=[rpo, rrd], writes=[rOH])
                    P.dma("sp", OMLA[h * 128:(h + 1) * 128, :], OH[:], reads=[rOH], writes=[rOMLA])
                if upto == "C" and dbg:
                    for nm, tl, rr, pp in (("QN_o", QN, rQN, 128), ("QR_o", QR, rQR, 64), ("KN_o", KN, rKN, 128), ("qn_o", qn[:, 0, :], rqn, 128)):
                        o_ = nc.dram_tensor(nm, [pp, S], BF16, kind="ExternalOutput").ap()
                        P.dma("sp", o_, tl[:] if nm != "qn_o" else tl, reads=[rr], writes=[rr])
                    o_ = nc.dram_tensor("V_o", [128, 32 * 128], BF16, kind="ExternalOutput").ap()
                    P.dma("sp", o_, Vh[:].rearrange("p a b -> p (a b)"), reads=[rV], writes=[rV])
            P.barrier()
            if upto == "C":
                P.final_wait("sp", [rOMLA])
                bc.close()
                P.emit(nc, sems)
                return nc

            bc.close()
            early = gdn_phase()
            P.barrier()
            if upto == "D" or early:
                P.final_wait("sp", [rOGDN])
                P.emit(nc, sems)
                return nc

            mid.close()
            fw = ExitStack()
            pre2, issue2 = load_gu(1, fw, defer=True)
            with ExitStack() as st:
                wom = sbt(st, [128, 8, D], BF16); wog = sbt(st, [128, 8, D], BF16); wo = sbt(st, [128, 8, D], BF16); rwe = Res()
                for wt_, src_ in ((wom, w_o_mla), (wog, w_o_gdn), (wo, w_out)):
                    sv = src_.rearrange("(kc p) n -> p kc n", p=128)
                    for kc in range(0, 8, 2):
                        P.dma("pool", wt_[:, kc:kc + 2, :], sv[:, kc:kc + 2, :], writes=[rwe])
                issue2()
                om = sbt(st, [128, 8, 512], BF16); og = sbt(st, [128, 8, 512], BF16); gt_ = sbt(st, [128, 16, 512], BF16)
                x1t = sbt(st, [128, 8, 512], F32); zt = sbt(st, [128, 8, 512], BF16)
                rom = Res(); rog = Res(); rgt = Res(); rx1 = Res(); rzt = Res()
                ta = Ring([sbt(st, [128, 512], F32) for _ in range(2)])
                tb = Ring([sbt(st, [128, 512], F32) for _ in range(2)])
                psr = Ring([pst(st, [128, 512]) for _ in range(6)])
                def loads_a(tg_):
                    ts_ = slice(tg_ * 512, (tg_ + 1) * 512)
                    P.dma("sp", om[:], OMLA.rearrange("(kc p) t -> p kc t", p=128)[:, :, ts_], reads=[rOMLA], writes=[rom])
                    P.dma("sp", og[:], OGDN.rearrange("(kc p) t -> p kc t", p=128)[:, :, ts_], reads=[rOGDN], writes=[rog])
                    P.dma("sp", gt_[:], GATES.rearrange("(kc p) t -> p kc t", p=128)[:, :, ts_], reads=[rGATES], writes=[rgt])

                def load_x(tg_):
                    ts_ = slice(tg_ * 512, (tg_ + 1) * 512)
                    P.dma("sp", x1t[:], X1T.rearrange("(kc p) t -> p kc t", p=128)[:, :, ts_], reads=[rX1T[tg_]], writes=[rx1])
                loads_a(0)
                load_x(0)
                for tg in range(8):
                    tsl = slice(tg * 512, (tg + 1) * 512)
                    for n in range(8):
                        pm, rpm = psr.next(); pg_, rpg_ = psr.next()
                        P.op("pe", [mm(pm[:], wom[:, kc, n * 128:(n + 1) * 128], om[:, kc, :], kc == 0, kc == 7) for kc in range(8)],
                             reads=[rwe, rom], writes=[rpm])
                        P.op("pe", [mm(pg_[:], wog[:, kc, n * 128:(n + 1) * 128], og[:, kc, :], kc == 0, kc == 7) for kc in range(8)],
                             reads=[rwe, rog], writes=[rpg_])
                        a_, ra_ = ta.next(); b_, rb_ = tb.next()
                        P.op("dve", lambda g, a_=a_, pm=pm, n=n: g.tensor_tensor(out=a_[:], in0=pm[:], in1=gt_[:, n, :], op=ALU.mult),
                             reads=[rpm, rgt], writes=[ra_])
                        P.op("dve", lambda g, b_=b_, pg_=pg_, n=n: g.tensor_tensor(out=b_[:], in0=pg_[:], in1=gt_[:, 8 + n, :], op=ALU.mult),
                             reads=[rpg_, rgt], writes=[rb_])
                        P.op("dve", lambda g, a_=a_, b_=b_, n=n: g.tensor_tensor(out=zt[:, n, :], in0=a_[:], in1=b_[:], op=ALU.add),
                             reads=[ra_, rb_], writes=[rzt])
                    if tg + 1 < 8:
                        loads_a(tg + 1)
                    for n in range(8):
                        px, rpx = psr.next()
                        P.op("pe", [mm(px[:], wo[:, kc, n * 128:(n + 1) * 128], zt[:, kc, :], kc == 0, kc == 7) for kc in range(8)],
                             reads=[rwe, rzt], writes=[rpx])
                        P.op("dve", lambda g, px=px, n=n: g.scalar_tensor_tensor(out=x1t[:, n, :], in0=px[:], scalar=PG[:, 1, n:n + 1], in1=x1t[:, n, :],
                                                                        op0=ALU.mult, op1=ALU.add), reads=[rpx, r_ada, rx1], writes=[rx1])
                    P.dma("sp", X2T.rearrange("(kc p) t -> p kc t", p=128)[:, :, tsl], x1t[:], reads=[rx1], writes=[rX2T[tg]])
                    if tg + 1 < 8:
                        load_x(tg + 1)
            P.barrier()
        P.barrier()
        if upto == "E":
            P.final_wait("sp", rX2T)
            fw.close()
            P.emit(nc, sems)
            return nc
        ffn_phase(1, pre2)
        P.final_wait("sp", [r_y])
        fw.close()
        P.emit(nc, sems)
    return nc


_CACHE = {}


def _consts():
    inv_freq = 10000.0 ** (-np.arange(0, 64, 2, dtype=np.float64) / 64.0)
    pos = np.arange(S, dtype=np.float64)
    ang = pos[:, None] * inv_freq[None, :]
    cos = np.cos(ang).astype(np.float32).T
    sin = np.sin(ang).astype(np.float32).T
    cosT = np.concatenate([cos, cos], 0)
    sinS = np.concatenate([-sin, sin], 0)
    ident = np.eye(128, dtype=np.float32)
    t = np.arange(64)
    tri = (t[:, None] <= t[None, :]).astype(np.float32)
    eye = np.eye(64, dtype=np.float32)
    negm = np.where(t[None, :] >= t[:, None], 0.0, NEG).astype(np.float32)
    return dict(cosT=np.ascontiguousarray(cosT), sinS=np.ascontiguousarray(sinS), c_ident=ident, c_tri=tri,
                c_eye=eye, c_negm=negm, c_eyeb=np.ascontiguousarray(np.broadcast_to(eye.reshape(1, 4096), (128, 4096))))


def _fm(v, nk):
    return np.ascontiguousarray(np.asarray(v, np.float32).reshape(nk, 128).T)


def make_in_maps(inp):
    cst = _consts()
    shared = dict(cst)
    shared["w_ada"] = np.ascontiguousarray(inp["w_ada"][0])
    shared["bada_fm"] = _fm(inp["b_ada"][0], 72)
    shared["gains_fm"] = np.ascontiguousarray(np.stack(
        [_fm(inp["g_ffn1"][0], 8), _fm(inp["g_mix"][0], 8), _fm(inp["g_ffn2"][0], 8), _fm(inp["g_final"], 8)], 1))
    for k in ["w1_gate", "w1_up", "w1_down", "w2_gate", "w2_up", "w2_down", "w_in", "w_uq", "w_ukv",
              "w_o_mla", "w_o_gdn", "w_out"]:
        shared[k] = np.ascontiguousarray(inp[k][0])
    shared["gq_fm"] = _fm(inp["g_q_lat"][0], 6)
    shared["gkv_fm"] = _fm(inp["g_kv_lat"][0], 2)
    wc = np.asarray(inp["w_conv"][0], np.float32)
    shared["wconv_fm"] = np.ascontiguousarray(wc.reshape(4, 24, 128).transpose(2, 1, 0))
    shared["alog_bc"] = np.ascontiguousarray(np.broadcast_to(np.asarray(inp["a_log"][0], np.float32)[None, :], (64, 8)))
    shared["dtb_bc"] = np.ascontiguousarray(np.broadcast_to(np.asarray(inp["dt_bias"][0], np.float32)[None, :], (64, 8)))
    shared["ggdn_fm"] = _fm(inp["g_gdn_out"][0], 1)
    maps = []
    for b in range(NCORES):
        m = dict(shared)
        m["x"] = np.ascontiguousarray(inp["x"][b])
        m["c_fm"] = _fm(inp["c"][b], 8)
        maps.append(m)
    return maps


def kernel(**inputs):
    if "nc" not in _CACHE:
        _CACHE["nc"] = build()
    nc = _CACHE["nc"]
    in_maps = make_in_maps(inputs)
    res = run_bass_kernel_spmd(nc, in_maps, core_ids=list(range(NCORES)))
    return np.stack([np.asarray(res.results[b]["y"], np.float32) for b in range(4)], 0)
```
